# Optimizing a Trainium2 kernel written in Bass

```python
import jax, jax.numpy as jnp
from jax import lax
import numpy as np

D_MODEL = 1024
BATCH = 4
SEQ = 4096
DEPTH = 1

MIX_WIDTH = D_MODEL
HG_WIDTH = MIX_WIDTH // 2
HG_HEAD_DIM = 128
HG_HEADS = HG_WIDTH // HG_HEAD_DIM
HG_CHUNK = 64
RW_WIDTH = MIX_WIDTH - HG_WIDTH
RW_HEAD_DIM = 64
RW_HEADS = RW_WIDTH // RW_HEAD_DIM
RW_DECAY_LORA = 64
RW_AAA_LORA = 64
RW_GATE_LORA = 128
RW_COLS = 3 * RW_WIDTH + RW_DECAY_LORA + RW_AAA_LORA + RW_GATE_LORA
HG_COLS = 4 * HG_WIDTH
IN_COLS = HG_COLS + RW_COLS
D_FF = 2816
CONV_WIDTH = 3
NORM_EPS = 1e-6
RW_GN_EPS = 64e-5
L2_EPS = 1e-12

kernel_name = "hybrid_hgrn2_rwkv7_convffn"


def _rmsnorm(x, w):
    xf = x.astype(jnp.float32)
    y = xf * lax.rsqrt(jnp.mean(xf * xf, axis=-1, keepdims=True) + NORM_EPS)
    return (y * w.astype(jnp.float32)).astype(x.dtype)


def _token_shift(z):
    return jnp.pad(z, ((0, 0), (1, 0), (0, 0)))[:, :-1]


def _hgrn2_chunk_scan(q, k, v, logf):
    B, T, H, K = q.shape
    V = v.shape[-1]
    C = HG_CHUNK
    N = T // C

    def to_chunks(z):
        return z.reshape(B, N, C, H, z.shape[-1]).transpose(1, 0, 3, 2, 4)

    qc, kc, vc, gc = to_chunks(q), to_chunks(k), to_chunks(v), to_chunks(logf)
    causal = jnp.tril(jnp.ones((C, C), dtype=bool))[:, :, None]

    def step(S, inp):
        qb, kb, vb, gb = inp
        b = jnp.cumsum(gb, axis=2)
        diff = b[:, :, :, None, :] - b[:, :, None, :, :]
        dec = jnp.exp(jnp.where(causal, diff, -jnp.inf))
        A = jnp.einsum('bhtk,bhsk,bhtsk->bhts', qb, kb, dec)
        o = jnp.einsum('bhts,bhsv->bhtv', A, vb) + jnp.einsum('bhtk,bhkv->bhtv', qb * jnp.exp(b), S)
        b_last = b[:, :, -1:, :]
        S = jnp.exp(b_last[:, :, 0, :])[..., None] * S + jnp.einsum(
            'bhsk,bhsv->bhkv', kb * jnp.exp(b_last - b), vb)
        return S, o

    S0 = jnp.zeros((B, H, K, V), jnp.float32)
    _, o = lax.scan(step, S0, (qc, kc, vc, gc))
    return o.transpose(1, 0, 3, 2, 4).reshape(B, T, H, V)


def _hgrn2_mixer(q_raw, f_raw, i_raw, g_raw, lb, norm_w):
    B, T, _ = q_raw.shape
    f32 = jnp.float32

    def heads(z):
        return z.reshape(B, T, HG_HEADS, HG_HEAD_DIM)

    f = lb + (1.0 - lb) * jax.nn.sigmoid(f_raw.astype(f32))
    q = heads(jax.nn.silu(q_raw.astype(f32))) * (HG_HEAD_DIM ** -0.5)
    k = heads(1.0 - f)
    logf = heads(jnp.log(f))
    v = heads(i_raw.astype(f32))
    o = _hgrn2_chunk_scan(q, k, v, logf)
    o = o * lax.rsqrt(jnp.mean(o * o, axis=-1, keepdims=True) + NORM_EPS)
    o = o.reshape(B, T, HG_WIDTH) * norm_w.astype(f32) * jax.nn.silu(g_raw.astype(f32))
    return o.astype(q_raw.dtype)


def _rwkv7_scan(r, w, k, v, a, b):
    B, T, H, N = r.shape

    def step(S, inp):
        rt, wt, kt, vt, at, bt = inp
        sa = jnp.einsum('bhvk,bhk->bhv', S, at)
        S = S * wt[:, :, None, :] + sa[..., None] * bt[:, :, None, :] + vt[..., None] * kt[:, :, None, :]
        y = jnp.einsum('bhvk,bhk->bhv', S, rt)
        return S, y

    xs = tuple(z.transpose(1, 0, 2, 3) for z in (r, w, k, v, a, b))
    S0 = jnp.zeros((B, H, N, N), jnp.float32)
    _, y = lax.scan(step, S0, xs)
    return y.transpose(1, 0, 2, 3)


def _rwkv7_mixer(r, k, v, wd, ad, gd, w0, w2, a0, a2, g2, k_k, k_a, r_k, ln_w, ln_b):
    B, T, _ = r.shape
    out_dtype = r.dtype
    f32 = jnp.float32

    def heads(z):
        return z.reshape(B, T, RW_HEADS, RW_HEAD_DIM)

    r, k, v = r.astype(f32), k.astype(f32), v.astype(f32)
    w = -jax.nn.softplus(-(w0.astype(f32) + jnp.tanh(wd.astype(f32)) @ w2.astype(f32))) - 0.5
    decay = jnp.exp(-jnp.exp(w))
    a = jax.nn.sigmoid(a0.astype(f32) + ad.astype(f32) @ a2.astype(f32))
    g = jax.nn.sigmoid(gd.astype(f32)) @ g2.astype(f32)
    kk = heads(k * k_k.astype(f32))
    kk = kk / jnp.maximum(jnp.sqrt(jnp.sum(kk * kk, axis=-1, keepdims=True)), L2_EPS)
    k = k * (1.0 + (a - 1.0) * k_a.astype(f32))
    y = _rwkv7_scan(heads(r), heads(decay), heads(k), heads(v), -kk, kk * heads(a))
    mu = jnp.mean(y, axis=-1, keepdims=True)
    var = jnp.mean(jnp.square(y - mu), axis=-1, keepdims=True)
    y = ((y - mu) * lax.rsqrt(var + RW_GN_EPS)).reshape(B, T, RW_WIDTH)
    y = y * ln_w.astype(f32) + ln_b.astype(f32)
    bonus = jnp.sum(heads(r * k * r_k.astype(f32)), axis=-1, keepdims=True) * heads(v)
    y = y + bonus.reshape(B, T, RW_WIDTH)
    return (y * g).astype(out_dtype)


def _conv_ffn(h, w_up, conv_w, conv_b, w_down):
    u = h @ w_up
    C = u.shape[-1]
    u = lax.conv_general_dilated(
        u, conv_w[:, None, :], window_strides=(1,), padding=[(CONV_WIDTH - 1, 0)],
        dimension_numbers=('NWC', 'WIO', 'NWC'), feature_group_count=C) + conv_b
    gate, val = jnp.split(u, 2, axis=-1)
    return (jax.nn.silu(gate) * val) @ w_down


def setup_inputs(seed: int = 0) -> dict:
    key = jax.random.key(seed)
    ks = jax.random.split(key, 24)
    f32 = jnp.float32
    nrm = lambda k, s: jax.random.normal(k, s, f32)
    L = DEPTH
    return {
        "x": nrm(ks[0], (BATCH, SEQ, D_MODEL)),
        "norm1_w": 1.0 + 0.02 * nrm(ks[1], (L, D_MODEL)),
        "w_in": nrm(ks[2], (L, D_MODEL, IN_COLS)) * D_MODEL ** -0.5,
        "hg_lb_logits": 0.1 * nrm(ks[3], (L + 1, HG_WIDTH)),
        "hg_norm_w": 1.0 + 0.02 * nrm(ks[4], (L, HG_WIDTH)),
        "rw_shift_mu": jax.random.uniform(ks[5], (L, RW_COLS), f32),
        "rw_w0": jax.random.uniform(ks[6], (L, RW_WIDTH), f32, -6.5, -1.5),
        "rw_w2": 0.1 * nrm(ks[7], (L, RW_DECAY_LORA, RW_WIDTH)) * RW_DECAY_LORA ** -0.5,
        "rw_a0": 0.1 * nrm(ks[8], (L, RW_WIDTH)),
        "rw_a2": 0.1 * nrm(ks[9], (L, RW_AAA_LORA, RW_WIDTH)) * RW_AAA_LORA ** -0.5,
        "rw_g2": nrm(ks[10], (L, RW_GATE_LORA, RW_WIDTH)) * RW_GATE_LORA ** -0.5,
        "rw_k_k": 0.85 + 0.02 * nrm(ks[11], (L, RW_WIDTH)),
        "rw_k_a": 1.0 + 0.02 * nrm(ks[12], (L, RW_WIDTH)),
        "rw_r_k": 0.1 * nrm(ks[13], (L, RW_WIDTH)),
        "rw_ln_w": 1.0 + 0.02 * nrm(ks[14], (L, RW_WIDTH)),
        "rw_ln_b": 0.02 * nrm(ks[15], (L, RW_WIDTH)),
        "w_out": nrm(ks[16], (L, MIX_WIDTH, D_MODEL)) * MIX_WIDTH ** -0.5,
        "norm2_w": 1.0 + 0.02 * nrm(ks[17], (L, D_MODEL)),
        "w_up": nrm(ks[18], (L, D_MODEL, 2 * D_FF)) * D_MODEL ** -0.5,
        "conv_w": nrm(ks[19], (L, CONV_WIDTH, 2 * D_FF)) * CONV_WIDTH ** -0.5,
        "conv_b": 0.02 * nrm(ks[20], (L, 2 * D_FF)),
        "w_down": nrm(ks[21], (L, D_FF, D_MODEL)) * D_FF ** -0.5,
        "final_norm_w": 1.0 + 0.02 * nrm(ks[22], (D_MODEL,)),
    }


def reference(x, norm1_w, w_in, hg_lb_logits, hg_norm_w, rw_shift_mu, rw_w0, rw_w2, rw_a0, rw_a2,
              rw_g2, rw_k_k, rw_k_a, rw_r_k, rw_ln_w, rw_ln_b, w_out, norm2_w, w_up, conv_w,
              conv_b, w_down, final_norm_w):
    lower_bounds = jnp.cumsum(jax.nn.softmax(hg_lb_logits.astype(jnp.float32), axis=0), axis=0)
    rw_split = np.cumsum([RW_WIDTH, RW_WIDTH, RW_WIDTH, RW_DECAY_LORA, RW_AAA_LORA]).tolist()
    for l in range(DEPTH):
        h = _rmsnorm(x, norm1_w[l])
        proj = h @ w_in[l]
        hg_p, rw_p = proj[..., :HG_COLS], proj[..., HG_COLS:]
        q_raw, f_raw, i_raw, g_raw = jnp.split(hg_p, 4, axis=-1)
        o_hg = _hgrn2_mixer(q_raw, f_raw, i_raw, g_raw, lower_bounds[l], hg_norm_w[l])
        rw_p = rw_p + (_token_shift(rw_p) - rw_p) * rw_shift_mu[l]
        r, k, v, wd, ad, gd = jnp.split(rw_p, rw_split, axis=-1)
        o_rw = _rwkv7_mixer(r, k, v, wd, ad, gd, rw_w0[l], rw_w2[l], rw_a0[l], rw_a2[l], rw_g2[l],
                            rw_k_k[l], rw_k_a[l], rw_r_k[l], rw_ln_w[l], rw_ln_b[l])
        x = x + jnp.concatenate([o_hg, o_rw], axis=-1) @ w_out[l]
        x = x + _conv_ffn(_rmsnorm(x, norm2_w[l]), w_up[l], conv_w[l], conv_b[l], w_down[l])
    return _rmsnorm(x, final_norm_w)
```

```python
import numpy as np
from contextlib import ExitStack
import concourse.bass as bass
import concourse.mybir as mybir
from concourse.bass_utils import run_bass_kernel_spmd
from concourse.ap import AP

F32 = mybir.dt.float32
BF16 = mybir.dt.bfloat16
AF = mybir.ActivationFunctionType
ALU = mybir.AluOpType
AX = mybir.AxisListType

SELF_SYNC = True
PROFILE = False
NTF = 27
POOL_E = "dve"
PSH_BANKS = 0
VWIN = 100
TB = 256
NS = TB // 128
D = 1024
DFF = 2816
NFC = DFF // 128


class StopBuild(Exception):
    pass


class Buf:
    __slots__ = ("name", "w", "r", "nodes", "valloc")

    def __init__(self, name):
        self.name = name
        self.w = None
        self.r = []
        self.valloc = None
        self.nodes = None


_SM = {"pass": 0, "vallocs": [], "assign": {}}
SMART_RINGS = False


class Tl:
    __slots__ = ("ap", "buf", "sem")

    def __init__(self, ap, name, sem=None):
        self.ap = ap
        self.buf = Buf(name)
        self.sem = sem


class Ring:
    def __init__(self, tiles, name=None):
        self.t = tiles
        self.i = 0
        self.name = name

    def next(self):
        i = self.i
        self.i += 1
        if self.name is not None and len(self.t) > 1:
            if _SM["pass"] == 1:
                prev = _SM["assign"].get(self.name)
                t = self.t[prev[i]] if prev is not None else self.t[i % len(self.t)]
                t.buf.nodes = []
                _SM["vallocs"].append((self.name, i, len(self.t), t.buf.nodes))
                return t
            if _SM["pass"] == 2 and self.name in _SM["assign"]:
                return self.t[_SM["assign"][self.name][i]]
        return self.t[i % len(self.t)]


def _bufs(xs):
    out = []
    for x in xs:
        if x is None:
            continue
        out.append(x.buf if isinstance(x, Tl) else x)
    return out


class Node:
    __slots__ = ("id", "eng", "fn", "preds", "succs", "dur", "kind", "semkey", "lat", "prio", "start", "fin",
                 "ev", "nready", "tag", "seg", "vallocs")

    def __init__(self, id_, eng, fn, dur, kind, semkey, lat):
        self.id = id_
        self.eng = eng
        self.fn = fn
        self.preds = set()
        self.succs = []
        self.dur = dur
        self.kind = kind
        self.semkey = semkey
        self.lat = lat
        self.prio = 0.0
        self.ev = None


class Prog:
    ENG = ("pe", "act", "dve", "pool", "sp")
    XLAT = 120.0

    def __init__(self, nc, sems, group_sems=()):
        self.nc = nc
        self.q = {e: [] for e in self.ENG}
        self.sem = sems
        self.group = set(group_sems)
        self.cnt = {k: 0 for k in sems}
        self.known = {e: {} for e in self.ENG}
        self.snap = {}
        self.nodes = []
        self.seg = 0
        self.valloc_nodes = {}
        self.place = {}
        self.alloc_class = {}
        self.class_slots = {}
        self.nwait = 0
        self.nins = {e: 0 for e in self.ENG}
        self.sim_end = 0.0
        self.sim_total = 0.0
        self.profile = None

    def _add(self, e, fn, reads, writes, dur, kind, semkey, lat):
        n = Node(len(self.nodes), e, fn, dur, kind, semkey, lat)
        n.seg = self.seg
        n.vallocs = None
        for b in list(reads) + list(writes):
            if b.nodes is not None:
                b.nodes.append(n)
            if b.valloc is not None:
                if n.vallocs is None:
                    n.vallocs = []
                if b.valloc not in n.vallocs:
                    n.vallocs.append(b.valloc)
                    self.valloc_nodes.setdefault(b.valloc, []).append(n)
        if self.profile is not None:
            import sys as _sys
            f = _sys._getframe(3)
            n.tag = f.f_lineno if f.f_code.co_name not in ("act", "tt", "ts", "stt", "cp", "mm", "tr", "scan") else _sys._getframe(4).f_lineno
        sg = self.seg
        for b in reads:
            if b.w is not None and b.w[0] == sg:
                n.preds.add(b.w[1])
        for b in writes:
            if b.w is not None and b.w[0] == sg:
                n.preds.add(b.w[1])
            for r in b.r:
                if r[0] == sg:
                    n.preds.add(r[1])
        n.preds.discard(n.id)
        me = (sg, n.id)
        for b in writes:
            b.w = me
            b.r = []
        for b in reads:
            if not b.r or b.r[-1] != me:
                b.r.append(me)
        self.nodes.append(n)
        return n

    def op(self, e, fn, reads=(), writes=(), inc=True, dur=100.0):
        self._add(e, fn, _bufs(reads), _bufs(writes), dur, "op", None, dur if e != "pe" else 60.0)

    def dma(self, e, semkey, out, in_, reads=(), writes=(), nbytes=65536, **kw):
        nbytes = int(np.prod(out.shape)) * 4
        lat = 2000.0 + nbytes / 150.0
        self._add(e, lambda eng: eng.dma_start(out=out, in_=in_, **kw), _bufs(reads), _bufs(writes), 60.0, "dma",
                  semkey, lat)

    def _schedule(self):
        import heapq
        nodes = self.nodes
        for n in nodes:
            for p in n.preds:
                nodes[p].succs.append(n.id)
        for n in reversed(nodes):
            best = 0.0
            for s_ in n.succs:
                sn = nodes[s_]
                lat = 0.0 if (n.eng == "pe" and sn.eng == "pe") else (n.lat + self.XLAT)
                v = sn.prio + lat
                if v > best:
                    best = v
            n.prio = n.dur + best
        free = {e: 0.0 for e in self.ENG}
        waiting = {e: [] for e in self.ENG}
        avail = {e: [] for e in self.ENG}
        blocked = []
        ready_t = [0.0] * len(nodes)
        npred = [len(n.preds) for n in nodes]
        for n in nodes:
            if npred[n.id] == 0:
                heapq.heappush(waiting[n.eng], (0.0, n.id))
        vnodes = self.valloc_nodes
        remaining_v = {k: len(v) for k, v in vnodes.items()}
        cls_of = self.alloc_class
        place = self.place
        owner = {c: [None] * n_ for c, n_ in self.class_slots.items()}
        free_t = {c: [0.0] * n_ for c, n_ in self.class_slots.items()}
        vorder = {c: [] for c in self.class_slots}
        for k in sorted(vnodes.keys()):
            vorder[cls_of[k]].append(k)
        optr = {c: 0 for c in self.class_slots}
        rptr = {c: 0 for c in self.class_slots}
        vfin = {k: 0.0 for k in vnodes}

        placed_flag = [False]

        def try_alloc(n):
            need_ = [k for k in n.vallocs if k not in place]
            if not need_:
                return True
            plan = []
            used = {}
            for k in need_:
                c = cls_of[k]
                vo = vorder[c]
                while optr[c] < len(vo) and vo[optr[c]] in place:
                    optr[c] += 1
                is_oldest = optr[c] < len(vo) and vo[optr[c]] == k
                ro = rptr[c]
                while ro < len(vo) and remaining_v[vo[ro]] == 0:
                    ro += 1
                rptr[c] = ro
                if ro < len(vo) and k > vo[ro] + VWIN:
                    return False
                ow = owner[c]
                freeb = [b for b in range(len(ow)) if (ow[b] is None or remaining_v[ow[b]] == 0)
                         and b not in used.get(c, ())]
                if not freeb or (len(freeb) < 2 and not is_oldest):
                    return False
                b = min(freeb, key=lambda b_: free_t[c][b_])
                used.setdefault(c, set()).add(b)
                plan.append((k, c, b))
            for k, c, b in plan:
                prev = owner[c][b]
                if prev is not None:
                    for p in vnodes[prev]:
                        if p.id not in n.preds:
                            n.preds.add(p.id)
                            p.succs.append(n.id)
                        lat = 0.0 if (p.eng == "pe" and n.eng == "pe") else (p.lat + self.XLAT)
                        rt = p.fin + lat
                        if rt > ready_t[n.id]:
                            ready_t[n.id] = rt
                owner[c][b] = k
                place[k] = b
            placed_flag[0] = True
            return True

        order = []
        remaining = len(nodes)
        while remaining:
            cand = []
            for e in self.ENG:
                w, a = waiting[e], avail[e]
                while w and w[0][0] <= free[e]:
                    t_, i_ = heapq.heappop(w)
                    heapq.heappush(a, (-nodes[i_].prio, i_))
                if a:
                    cand.append((free[e], e))
                elif w:
                    cand.append((w[0][0], e))
            cand.sort()
            picked = None
            for t, e in cand:
                tmp = []
                while True:
                    if avail[e]:
                        item = heapq.heappop(avail[e])
                        i_ = item[1]
                    elif waiting[e]:
                        item = heapq.heappop(waiting[e])
                        i_ = item[1]
                    else:
                        break
                    n = nodes[i_]
                    if n.vallocs is not None and not try_alloc(n):
                        blocked.append(i_)
                        continue
                    picked = n
                    break
                if picked is not None:
                    break
            if picked is None:
                raise RuntimeError("scheduler deadlock: all candidates blocked on resource slots")
            n = picked
            e = n.eng
            i_ = n.id
            start = max(free[e], ready_t[i_])
            n.start = start
            n.fin = start + n.dur
            free[e] = n.fin
            order.append(n)
            remaining -= 1
            if n.vallocs is not None:
                released = placed_flag[0]
                placed_flag[0] = False
                for k in n.vallocs:
                    remaining_v[k] -= 1
                    if n.fin > vfin[k]:
                        vfin[k] = n.fin
                    if remaining_v[k] == 0:
                        free_t[cls_of[k]][place[k]] = vfin[k]
                        released = True
                if released and blocked:
                    for j_ in blocked:
                        heapq.heappush(waiting[nodes[j_].eng], (ready_t[j_], j_))
                    blocked = []
            for s_ in n.succs:
                sn = nodes[s_]
                lat = 0.0 if (n.eng == "pe" and sn.eng == "pe") else (n.lat + self.XLAT)
                rt = start + lat if n.kind == "dma" else n.fin + (0.0 if lat == 0.0 else lat)
                if rt > ready_t[s_]:
                    ready_t[s_] = rt
                npred[s_] -= 1
                if npred[s_] == 0:
                    heapq.heappush(waiting[sn.eng], (ready_t[s_], s_))
        self.sim_end = max(free.values()) if nodes else 0.0
        if self.profile is not None:
            self.profile.append((self.sim_end, [(n.eng, n.tag, n.dur, n.start) for n in nodes]))
        return order

    def _merge(self, e, k, v):
        kn = self.known[e]
        if kn.get(k, 0) >= v:
            return False
        kn[k] = v
        s = self.snap.get((k, v))
        if s:
            for k2, v2 in s.items():
                if kn.get(k2, 0) < v2:
                    kn[k2] = v2
        return True

    def flush(self):
        nodes = self.nodes
        if not nodes:
            return
        order = self._schedule()
        self.sim_total += self.sim_end
        gtot = {}
        for n in nodes:
            if n.kind == "dma" and n.semkey in self.group:
                gtot[n.semkey] = gtot.get(n.semkey, self.cnt[n.semkey]) + 16
        need = [False] * len(nodes)
        for n in nodes:
            if n.kind == "dma":
                need[n.id] = True
                continue
            for s_ in n.succs:
                sn = nodes[s_]
                if not (n.eng == "pe" and sn.eng == "pe"):
                    if sn.eng != n.eng or SELF_SYNC:
                        need[n.id] = True
                        break
        sems = self.sem
        for n in order:
            e = n.eng
            evs = {}
            for p in n.preds:
                pn = nodes[p]
                if pn.eng == "pe" and e == "pe" and pn.kind == "op":
                    continue
                if pn.kind == "op" and pn.eng == e and not SELF_SYNC:
                    continue
                if pn.kind == "dma" and n.kind == "dma" and pn.semkey == n.semkey and n.semkey in self.group:
                    continue
                k, v = pn.ev
                if evs.get(k, 0) < v:
                    evs[k] = v
            waits = []
            for k, v in evs.items():
                if self._merge(e, k, v):
                    waits.append((k, v))
            if n.kind == "dma":
                k = n.semkey
                self.cnt[k] += 16
                n.ev = (k, gtot[k]) if k in self.group else (k, self.cnt[k])
                amount, semkey = 16, k
                if k not in self.group:
                    self.snap[n.ev] = dict(self.known[e])
            elif need[n.id]:
                self.cnt[e] += 1
                n.ev = (e, self.cnt[e])
                self.snap[n.ev] = dict(self.known[e])
                amount, semkey = 1, e
            else:
                n.ev = (e, self.cnt[e] + 1)
                amount, semkey = 0, e
            self.nwait += len(waits)
            self.nins[e] += 1

            def run(eng, waits=waits, fn=n.fn, amount=amount, semkey=semkey):
                for k, v in waits:
                    eng.wait_ge(sems[k], v)
                ins = fn(eng)
                if amount:
                    ins.then_inc(sems[semkey], amount)
            self.q[e].append(run)
        self.nodes = []
        self.valloc_nodes = {}
        self.seg += 1

    def barrier(self):
        self.flush()
        for e in self.ENG:
            waits = []
            for k, v in self.cnt.items():
                if v > 0 and k != e and self.known[e].get(k, 0) < v:
                    self.known[e][k] = v
                    waits.append((k, v))
            sems = self.sem

            def run(eng, waits=waits):
                for k, v in waits:
                    eng.wait_ge(sems[k], v)
            self.q[e].append(run)

    def replay(self, block):
        self.flush()
        if _SM["pass"] == 1:
            return
        q = self.q

        @block.tensor
        def _(eng):
            for f in q["pe"]:
                f(eng)

        @block.scalar
        def _(eng):
            for f in q["act"]:
                f(eng)

        @block.vector
        def _(eng):
            for f in q["dve"]:
                f(eng)

        @block.gpsimd
        def _(eng):
            for f in q["pool"]:
                f(eng)

        @block.sync
        def _(eng):
            for f in q["sp"]:
                f(eng)


PARAM_SHAPES = [
    ("norm1_w", [1024]), ("w_in", [1024, 3840]), ("hg_lb_logits", [2, 512]), ("hg_norm_w", [512]),
    ("rw_shift_mu", [1792]), ("rw_w0", [512]), ("rw_w2", [64, 512]), ("rw_a0", [512]),
    ("rw_a2", [64, 512]), ("rw_g2", [128, 512]), ("rw_k_k", [512]), ("rw_k_a", [512]),
    ("rw_r_k", [512]), ("rw_ln_w", [512]), ("rw_ln_b", [512]), ("w_out", [1024, 1024]),
    ("norm2_w", [1024]), ("w_up", [1024, 5632]), ("conv_w", [3, 5632]), ("conv_b", [5632]),
    ("w_down", [2816, 1024]), ("final_norm_w", [1024]),
]


def _assign_slots():
    by = {}
    for name, i, nslots, nodes in _SM["vallocs"]:
        by.setdefault(name, []).append((i, nslots, nodes))
    assign = {}
    for name, lst in by.items():
        if name not in SMART_SET:
            continue
        nslots = lst[0][1]
        res = [0] * len(lst)
        items = []
        for i, _, nodes in lst:
            if not nodes:
                items.append((0, 0.0, i))
            else:
                items.append((nodes[0].seg, min(n.start for n in nodes), i))
        items.sort()
        for rank, (seg, st, i) in enumerate(items):
            res[i] = rank % nslots
        assign[name] = res
    return assign


SMART_SET = ("ps",)
SMART_ITERS = 1


def build(nl=7, nm=8, dbg=None, stop=None):
    global VWIN
    if not SMART_RINGS:
        _SM["pass"] = 0
        last = None
        for w in (VWIN, 64, 48, 80, 40, 128, 32):
            VWIN = w
            try:
                return _build(nl, nm, dbg, stop)
            except RuntimeError as e_:
                if "scheduler deadlock" not in str(e_):
                    raise
                last = e_
        raise last
    _SM["assign"] = {}
    for _ in range(SMART_ITERS):
        _SM["pass"] = 1
        _SM["vallocs"] = []
        _build(nl, nm, dbg, stop)
        _SM["assign"] = _assign_slots()
    _SM["pass"] = 2
    try:
        return _build(nl, nm, dbg, stop)
    finally:
        _SM["pass"] = 0


def _build(nl=7, nm=8, dbg=None, stop=None):
    nblk = nl + 1 + nm
    nc = bass.Bass("TRN2", target_bir_lowering=False)
    dr = {}
    for name, shp in PARAM_SHAPES:
        dr[name] = nc.dram_tensor(name, shp, F32, kind="ExternalInput").ap()
    xs = nc.dram_tensor("xs", [nblk * TB, D], F32, kind="ExternalInput").ap()
    flag = nc.dram_tensor("flag", [128, 1], F32, kind="ExternalInput").ap()
    out = nc.dram_tensor("out", [nm * TB, D], F32, kind="ExternalOutput").ap()
    x1d = nc.dram_tensor("x1d", [128 + nm * TB, D], F32, kind="Internal").ap()
    dbg_t = {}
    if dbg:
        for name, shp in dbg.items():
            dt_ = F32
            if shp and shp[0] == "bf16":
                dt_ = BF16
                shp = shp[1:]
            dbg_t[name] = nc.dram_tensor("dbg_" + name, shp, dt_, kind="ExternalOutput").ap()

    es = ExitStack()
    with es:
        ARENA_F32 = 53000
        arena = es.enter_context(nc.sbuf_tensor("arena", [128, ARENA_F32], F32))
        psum_all = es.enter_context(nc.psum_tensor("psum_all", [128, 8, 512], F32))
        semkeys = ["pe", "act", "dve", "pool", "sp", "d_par", "d_x0", "d_x1", "d_x2", "d_x3",
                   "d_w0", "d_w1", "d_w2", "d_w3", "d_w4", "d_w5", "d_w6", "d_w7", "d_wo",
                   "d_st0", "d_st1", "d_st2", "d_st3", "d_dbg", "d_fn", "d_wdn0", "d_wdn1"] + [f"d_wup{i}" for i in range(NFC // 2)]
        sems = {k: es.enter_context(nc.semaphore(k)) for k in semkeys}
        block = es.enter_context(nc.Block())
        P = Prog(nc, sems, group_sems=("d_par", "d_w7", "d_dbg"))
        if PROFILE:
            P.profile = []

        st = {"off": 0}

        def carve(name, free_shape, dt, parts=128, sem=None):
            n = int(np.prod(free_shape))
            nf32 = (n + 1) // 2 if dt == BF16 else n
            nf32 = (nf32 + 3) // 4 * 4
            off = st["off"]
            st["off"] = off + nf32
            assert st["off"] <= ARENA_F32, (name, st["off"])
            v = arena[:, off:off + nf32]
            if dt == BF16:
                v = v.bitcast(BF16)[:, 0:n]
            else:
                v = v[:, 0:n]
            if len(free_shape) == 2:
                v = v.rearrange("p (a b) -> p a b", b=free_shape[1])
            elif len(free_shape) == 3:
                v = v.rearrange("p (a b c) -> p a b c", b=free_shape[1], c=free_shape[2])
            if parts != 128:
                v = v[0:parts]
            return Tl(v, name, sem)

        carve_off = {}

        def vring(name, n, free_shape, dt):
            tiles = []
            bases = []
            for i in range(n):
                bases.append(st["off"])
                tiles.append(carve(f"{name}{i}", free_shape, dt))
            return VRing(name, tiles, bases)

        def ring(name, n, free_shape, dt, sems_=None):
            return Ring([carve(f"{name}{i}", free_shape, dt, sem=(sems_[i] if sems_ else None)) for i in range(n)],
                        name=name)

        VS32 = 1 << 24
        ps_b0 = psum_all[:, 0, :]
        v_cnt = [0]
        v_info = {}
        P.class_slots["ps"] = 8

        def valloc(cname, slot0_ap, bases, name):
            k = v_cnt[0]
            v_cnt[0] += 1
            t = Tl(AP(slot0_ap.tensor, slot0_ap.offset + (k + 1) * VS32 * (4 // (2 if slot0_ap.dtype == BF16 else 4)),
                      slot0_ap.ap), f"{name}{k}")
            t.buf.valloc = k
            P.alloc_class[k] = cname
            v_info[k] = bases
            return t

        PS_BASES = [b * 512 for b in range(8 - PSH_BANKS)]
        PSH_BASES = [b * 512 + h * 256 for b in range(8 - PSH_BANKS, 8) for h in range(2)]
        P.class_slots["ps"] = 8 - PSH_BANKS
        if PSH_BANKS:
            P.class_slots["psh"] = 2 * PSH_BANKS
            psh_0 = psum_all[:, 8 - PSH_BANKS, 0:256]

        def psum(kind="full"):
            if kind == "half" and PSH_BANKS:
                return valloc("psh", psh_0, PSH_BASES, "psh")
            return valloc("ps", ps_b0, PS_BASES, "psv")

        class VRing:
            def __init__(self, name, tiles, bases):
                self.name = name
                self.t0 = tiles[0]
                self.bases = bases
                P.class_slots[name] = len(tiles)

            def next(self):
                return valloc(self.name, self.t0.ap, self.bases, self.name + "v")

        def RB(ap):
            if ap is None or not hasattr(ap, "name") or ap.name not in ("psum_all", "arena"):
                return ap
            mul = 2 if ap.dtype == BF16 else 1
            vs = VS32 * mul
            k = ap.offset // vs - 1
            if k < 0:
                return ap
            real = ap.offset % vs
            bases = v_info[k]
            return AP(ap.tensor, real + (bases[P.place[k]] - bases[0]) * mul, ap.ap)

        def fsz(ap):
            return int(np.prod(ap.shape[1:]))

        def d_act(o):
            return 220.0 + 0.72 * fsz(o)

        def d_dve(o, k=1.0):
            return 70.0 + 1.05 * k * fsz(o)

        def act(o, i, func, reads, writes, **kw):
            P.op("act", lambda e: e.activation(out=RB(o), in_=RB(i), func=func, **kw), reads, writes,
                 dur=d_act(o) + (60.0 if "accum_out" in kw else 0.0))

        def tt(eng, o, a, b, op, reads, writes):
            P.op(eng, lambda e: e.tensor_tensor(out=RB(o), in0=RB(a), in1=RB(b), op=op), reads, writes,
                 dur=(d_dve(o) if eng != "pool" else 150.0 + 2.3 * fsz(o)))

        def ts(eng, o, a, s1, s2, op0, op1, reads, writes):
            if s2 is None:
                P.op(eng, lambda e: e.tensor_scalar(out=RB(o), in0=RB(a), scalar1=s1, scalar2=None, op0=op0), reads, writes,
                     dur=d_dve(o))
            else:
                P.op(eng, lambda e: e.tensor_scalar(out=RB(o), in0=RB(a), scalar1=s1, scalar2=s2, op0=op0, op1=op1), reads, writes,
                     dur=d_dve(o))

        def stt(o, a, s, b, op0, op1, reads, writes):
            P.op("dve", lambda e: e.scalar_tensor_tensor(out=RB(o), in0=RB(a), scalar=s, in1=RB(b), op0=op0, op1=op1), reads, writes,
                 dur=d_dve(o))

        def cp(eng, o, i, reads, writes):
            if eng == "act":
                P.op("act", lambda e: e.activation(out=RB(o), in_=RB(i), func=AF.Identity), reads, writes, dur=d_act(o))
            else:
                P.op(eng, lambda e: e.tensor_copy(out=RB(o), in_=RB(i)), reads, writes, dur=d_dve(o))

        def mm(o, l, r, start, stop, reads, writes, inc=True):
            P.op("pe", lambda e: e.matmul(out=RB(o), lhsT=RB(l), rhs=RB(r), start=start, stop=stop), reads, writes,
                 dur=max(64, fsz(r)) * 0.45 + 12.0)

        def tr(o, i, ident, reads, writes, inc=True):
            P.op("pe", lambda e: e.transpose(out=RB(o), in_=RB(i), identity=ident), reads, writes, dur=70.0)

        def memset(eng, ap, val, writes):
            P.op(eng, lambda e: e.memset(ap, val), (), writes, dur=200.0 + fsz(ap))

        def asel(o, i, pattern, cmp, fill, base, cm, reads, writes):
            P.op("pool", lambda e: e.affine_select(out=o, in_=i, pattern=pattern, compare_op=cmp, fill=fill,
                                                   base=base, channel_multiplier=cm), reads, writes, dur=1000.0)

        def scan(o, d0, d1, init, op0, op1, reads, writes):
            P.op("dve", lambda e: e.tensor_tensor_scan(out=RB(o), data0=RB(d0), data1=RB(d1), initial=init, op0=op0, op1=op1),
                 reads, writes, dur=d_dve(o, 2.0))

        def ck(name):
            if stop == name:
                raise StopBuild()

        def dbg_dump(name, tl_ap, reads):
            if name in dbg_t:
                P.dma("sp", "d_dbg", dbg_t[name], tl_ap, reads=reads)

        dumped = set()

        def dd(name, tl, ap=None, when=True):
            if name in dbg_t and name not in dumped and when:
                dumped.add(name)
                P.dma("sp", "d_dbg", dbg_t[name], tl.ap if ap is None else ap, reads=[tl])

        identf = carve("identf", [128], F32)
        identb = carve("identb", [128], BF16)
        ones = carve("ones", [128], F32)
        zeros = carve("zeros", [128], F32)
        negh = carve("negh", [TB], F32)
        mUI = carve("mUI", [128], BF16)
        mRW = carve("mRW", [4, 128], BF16)
        mL = carve("mL", [128], BF16)
        blk = carve("blk", [128], F32)
        PT = carve("PT", [70], F32)
        CW = carve("CW", [4, NFC * 2], F32)
        DP = carve("DP", [56], F32)
        flg = carve("flg", [1], F32)
        S_hg = carve("S_hg", [4, 128], F32)
        Sb_hg = carve("Sb_hg", [4, 128], BF16)
        H_rw = carve("H_rw", [4, 64], F32)
        Hb_rw = carve("Hb_rw", [4, 64], BF16)
        glob_end = st["off"]

        memset("pool", identf.ap, 0.0, [identf])
        asel(identf.ap, identf.ap, [[-1, 128]], ALU.not_equal, 1.0, 0, 1, [identf], [identf])
        cp("dve", identb.ap, identf.ap, [identf], [identb])
        memset("pool", ones.ap, 1.0, [ones])
        memset("pool", zeros.ap, 0.0, [zeros])
        memset("pool", negh.ap, -0.5, [negh])
        memset("pool", S_hg.ap, 0.0, [S_hg])
        memset("pool", Sb_hg.ap, 0.0, [Sb_hg])
        memset("pool", H_rw.ap, 0.0, [H_rw])
        memset("pool", Hb_rw.ap, 0.0, [Hb_rw])
        tmpm = carve("tmpm", [128], F32)
        memset("pool", tmpm.ap, 1.0, [tmpm])
        asel(tmpm.ap, tmpm.ap, [[1, 128]], ALU.is_ge, 0.0, 0, -1, [tmpm], [tmpm])
        cp("dve", mRW.ap[:, 1, :], tmpm.ap, [tmpm], [mRW])
        cp("dve", mRW.ap[:, 3, :], tmpm.ap, [tmpm], [mRW])
        asel(tmpm.ap[:, 64:128], tmpm.ap[:, 64:128], [[0, 64]], ALU.is_ge, 0.0, -64, 1, [tmpm], [tmpm])
        cp("dve", mUI.ap, tmpm.ap, [tmpm], [mUI])
        memset("pool", tmpm.ap, 1.0, [tmpm])
        asel(tmpm.ap, tmpm.ap, [[1, 128]], ALU.is_gt, 0.0, 0, -1, [tmpm], [tmpm])
        cp("dve", mRW.ap[:, 0, :], tmpm.ap, [tmpm], [mRW])
        cp("dve", mRW.ap[:, 2, :], tmpm.ap, [tmpm], [mRW])
        memset("pool", tmpm.ap, 1.0, [tmpm])
        asel(tmpm.ap, tmpm.ap, [[-1, 128]], ALU.is_gt, 0.0, 0, 1, [tmpm], [tmpm])
        cp("dve", mL.ap, tmpm.ap, [tmpm], [mL])
        memset("pool", blk.ap, 1.0, [blk])
        asel(blk.ap[:, 0:64], blk.ap[:, 0:64], [[0, 64]], ALU.is_ge, 0.0, 63, -1, [blk], [blk])
        asel(blk.ap[:, 64:128], blk.ap[:, 64:128], [[0, 64]], ALU.is_ge, 0.0, -64, 1, [blk], [blk])

        PMa = carve("PMa", [128], F32)
        PMc = carve("PMc", [4, 128], F32)
        rows = [("norm1_w", None, 8), ("norm2_w", None, 8), ("hg_lb_logits", 0, 4), ("hg_lb_logits", 1, 4),
                ("hg_norm_w", None, 4), ("rw_shift_mu", None, 14), ("rw_w0", None, 4), ("rw_a0", None, 4),
                ("rw_k_k", None, 4), ("rw_k_a", None, 4), ("rw_r_k", None, 4), ("rw_ln_w", None, 4),
                ("rw_ln_b", None, 4)]
        r0 = 0
        for name, idx, n in rows:
            src = dr[name] if idx is None else dr[name][idx]
            P.dma("sp", "d_par", PMa.ap[r0:r0 + n, :], src.rearrange("(r p) -> r p", p=128), writes=[PMa])
            r0 += n
        assert r0 == 70
        for j in range(3):
            P.dma("sp", "d_par", PMc.ap[0:2 * NFC, j, :], dr["conv_w"][j].rearrange("(r p) -> r p", p=128), writes=[PMc])
        P.dma("sp", "d_par", PMc.ap[0:2 * NFC, 3, :], dr["conv_b"].rearrange("(r p) -> r p", p=128), writes=[PMc])
        P.dma("sp", "d_par", flg.ap, flag, writes=[flg])
        pp_ = psum()
        tr(pp_.ap[:, 0:70], PMa.ap[0:70, :], identf.ap[0:70, 0:70], [PMa, identf], [pp_])
        cp("dve", PT.ap, pp_.ap[:, 0:70], [pp_], [PT])
        pp_ = psum()
        for j in range(4):
            tr(pp_.ap[:, j * 44:(j + 1) * 44], PMc.ap[0:44, j, :], identf.ap[0:44, 0:44], [PMc, identf], [pp_], inc=(j == 3))
        cp("dve", CW.ap.rearrange("p a b -> p (a b)"), pp_.ap[:, 0:176], [pp_], [CW])
        nw1T = PT.ap[:, 0:8]
        nw2T = PT.ap[:, 8:16]
        hgnw = PT.ap[:, 24:28]
        muT = PT.ap[:, 28:42]
        w0T = PT.ap[:, 42:46]
        a0T = PT.ap[:, 46:50]
        kkT = PT.ap[:, 50:54]
        kaT = PT.ap[:, 54:58]
        rkT = PT.ap[:, 58:62]
        lnwT = PT.ap[:, 62:66]
        lnbT = PT.ap[:, 66:70]
        lbT = DP.ap[:, 0:4]
        lb1T = DP.ap[:, 4:8]
        cq2 = DP.ap[:, 8:12]
        hgnw2 = DP.ap[:, 12:16]
        hw0 = DP.ap[:, 16:20]
        ha0 = DP.ap[:, 20:24]
        hka = DP.ap[:, 24:28]
        nhka = DP.ap[:, 28:32]
        dtmp = DP.ap[:, 32:36]
        thl = DP.ap[:, 36:40]
        tt("dve", dtmp, PT.ap[:, 16:20], PT.ap[:, 20:24], ALU.subtract, [PT], [DP])
        act(thl, dtmp, AF.Exp, [DP], [DP], scale=-1.0)
        ts("dve", thl, thl, 1.0, None, ALU.add, None, [DP], [DP])
        P.op("dve", lambda e: e.reciprocal(out=lbT, in_=thl), [DP], [DP])
        ts("dve", lb1T, lbT, -1.0, 1.0, ALU.mult, ALU.add, [DP], [DP])
        ts("dve", cq2, lb1T, -(128 ** -0.5), None, ALU.mult, None, [DP], [DP])
        ts("dve", hw0, w0T, -1.0, None, ALU.mult, None, [PT], [DP])
        omT = DP.ap[:, 40:54]
        ts("dve", omT, muT, -1.0, 1.0, ALU.mult, ALU.add, [PT], [DP])
        ts("dve", ha0, a0T, -1.0, None, ALU.mult, None, [PT], [DP])
        if "PT" in dbg_t:
            dbg_dump("PT", PT.ap, [PT])

        if stop == "setup":
            P.barrier()
            P.replay(block)
            return nc, P
        P.barrier()
        st["off"] = glob_end
        w_in = carve("w_in", [8, 3840], BF16)
        w_out = carve("w_out", [8, 1024], BF16)
        w2b = carve("w2b", [512], BF16)
        a2b = carve("a2b", [512], BF16)
        g2b = carve("g2b", [512], BF16)
        wgroups = [(512, 1536), (0, 512), (1536, 2048), (3584, 3840), (2560, 3072), (3072, 3584), (2048, 2560)]
        wbufs = []
        wsrc = dr["w_in"].rearrange("(kc p) c -> p kc c", p=128)
        for gi, (c0, c1) in enumerate(wgroups):
            b = Buf(f"win{gi}")
            wbufs.append((c0, c1, b))
            P.dma("pool", f"d_w{gi}", w_in.ap[:, :, c0:c1], wsrc[:, :, c0:c1], writes=[b])
        P.dma("pool", "d_w7", w2b.ap[0:64, :], dr["rw_w2"], writes=[w2b])
        P.dma("pool", "d_w7", a2b.ap[64:128, :], dr["rw_a2"], writes=[a2b])
        P.dma("pool", "d_w7", g2b.ap, dr["rw_g2"], writes=[g2b])
        P.dma("pool", "d_wo", w_out.ap, dr["w_out"].rearrange("(kc p) c -> p kc c", p=128), writes=[w_out])

        def wbuf_of(c):
            col = c * 128
            for c0, c1, b in wbufs:
                if c0 <= col < c1:
                    return b
            raise AssertionError

        xring = ring("xt", 2, [D], F32, ["d_x0", "d_x1"])
        xnr = ring("xn", 2, [D], BF16)
        sm = ring("sm", 8, [16], F32)
        hT = carve("hT", [8, TB + 1], BF16)
        memset("pool", hT.ap, 0.0, [hT])
        tf = ring("tf", NTF, [TB + 4], F32)
        tb = ring("tb", 4, [TB], BF16)
        hg_kT = [carve(f"hg_kT{h}", [TB], BF16) for h in range(4)]
        hg_ktok = [carve(f"hg_ktok{h}", [NS, 128], BF16) for h in range(4)]
        hg_vtok = [carve(f"hg_vtok{h}", [NS, 128], BF16) for h in range(4)]
        hg_qT = [carve(f"hg_qT{h}", [TB], BF16) for h in range(4)]
        hg_sgT = [carve(f"hg_sgT{h}", [TB], BF16) for h in range(4)]
        hg_ebl = carve("hg_ebl", [4, TB // 64], F32)
        rw_kT = [carve(f"rw_kT{c}", [TB], BF16) for c in range(4)]
        rw_bT = [carve(f"rw_bT{c}", [TB], BF16) for c in range(4)]
        rw_ar = [carve(f"rw_ar{c}", [NS * 2, 2, 128], BF16) for c in range(4)]
        rw_ktok = [carve(f"rw_ktok{c}", [NS, 128], BF16) for c in range(4)]
        rw_btok = [carve(f"rw_btok{c}", [NS, 128], BF16) for c in range(4)]
        rw_vtok = [carve(f"rw_vtok{c}", [NS, 128], BF16) for c in range(4)]
        rw_gT = [carve(f"rw_gT{c}", [TB], BF16) for c in range(4)]
        rw_bon = [carve(f"rw_bon{c}", [TB], BF16) for c in range(4)]
        rw_egx = [carve(f"rw_egx{c}", [NS, 129], F32) for c in range(4)]
        twd = carve("twd", [TB], BF16)
        adT = carve("adT", [TB], BF16)
        sgd = carve("sgd", [TB], BF16)
        ATh = ring("ATh", 2, [4, 128], BF16)
        Pq = [carve(f"Pq{i}", [4, 128], BF16) for i in range(2)] * NS
        PTq = [carve(f"PTq{i}", [4, 128], BF16) for i in range(2)] * NS
        TTq = [carve(f"TTq{i}", [4, 128], BF16) for i in range(2)] * NS
        Rb = carve("Rb", [512], BF16)
        Ub = carve("Ub", [512], BF16)
        ysb = ring("ysb", 2, [512], F32)
        ysq = carve("ysq", [512], F32)
        ynb = ring("ynb", 2, [512], BF16)
        oT = ring("oT", 2, [8, 128], BF16)
        x1t = ring("x1t", 1, [D], F32, ["d_st0"])
        for c in range(4):
            memset("pool", rw_egx[c].ap, 1.0, [rw_egx[c]])
            memset("pool", rw_ar[c].ap, 0.0, [rw_ar[c]])
        ATb = [carve(f"ATb{c}", [4, 128], BF16) for c in range(4)]
        ATk = [carve(f"ATk{c}", [4, 128], BF16) for c in range(4)]

        def arv(c, s, j, w):
            return rw_ar[c].ap[:, s * 2 + j, w, :]

        def rsqrt_small(dst_ap, dst_reads, src_ap, n, scale, eps, reads, writes):
            t_ = sm.next()
            act(t_.ap[:, 0:n], src_ap, AF.Ln, reads, [t_], scale=scale, bias=eps)
            act(dst_ap, t_.ap[:, 0:n], AF.Exp, [t_] + list(dst_reads), writes, scale=-0.5)

        def sig3(dst_ap, dst_tl, src_ap, reads, bias=None):
            e_ = tf.next()
            shp = [src_ap.shape[0], int(np.prod(src_ap.shape[1:]))]
            ev = e_.ap[0:shp[0], 0:shp[1]]
            if bias is None:
                act(ev, src_ap, AF.Exp, reads, [e_], scale=-1.0)
            else:
                act(ev, src_ap, AF.Exp, list(reads) + [DP], [e_], scale=-1.0, bias=bias)
            act(ev, ev, AF.Ln, [e_], [e_], bias=1.0)
            act(dst_ap, ev, AF.Exp, [e_], [dst_tl], scale=-1.0)

        def load_norm_T(src_rows_ap, nwT, hT_tile, s, xt=None, src_reads=(), off=0):
            if xt is None:
                xt = xring.next()
                P.dma("sp", xt.sem, xt.ap, src_rows_ap, reads=src_reads, writes=[xt])
            ss = sm.next()
            xn = xnr.next()
            act(xn.ap, xt.ap, AF.Square, [xt], [xn, ss], accum_out=ss.ap[:, 0:1])
            rs = sm.next()
            rsqrt_small(rs.ap[:, 0:1], [], ss.ap[:, 0:1], 1, 1.0 / D, 1e-6, [ss], [rs])
            ts("dve", xn.ap, xt.ap, rs.ap[:, 0:1], None, ALU.mult, None, [xt, rs], [xn])
            dd('ss', ss, ap=ss.ap[:, 0:1]); dd('rs', rs, ap=rs.ap[:, 0:1]); dd('xn', xn); dd('xt', xt)
            pb = psum()
            pbv = pb.ap.bitcast(BF16).rearrange("p (a b) -> p a b", b=128)
            for kc in range(8):
                tr(pbv[:, kc, :], xn.ap[:, kc * 128:(kc + 1) * 128], identb.ap, [xn, identb], [pb], inc=(kc == 7))
            tt("dve", hT_tile.ap[:, :, off + s * 128:off + (s + 1) * 128], pbv, nwT.unsqueeze(2).to_broadcast([128, 8, 128]),
               ALU.mult, [pb, PT], [hT_tile])
            return xt

        def proj(c, halo=False):
            pb = psum()
            wb = wbuf_of(c)
            for kc in range(8):
                if halo:
                    mm(pb.ap[:, 0:TB + 1], w_in.ap[:, kc, c * 128:(c + 1) * 128], hT.ap[:, kc, :], kc == 0, kc == 7,
                       [wb, hT], [pb], inc=(kc == 7))
                else:
                    mm(pb.ap[:, 0:TB], w_in.ap[:, kc, c * 128:(c + 1) * 128], hT.ap[:, kc, 1:TB + 1], kc == 0, kc == 7,
                       [wb, hT], [pb], inc=(kc == 7))
            return pb

        def to_tok(srcT, dst, eng="act"):
            pb = psum("half")
            pbv = pb.ap.bitcast(BF16)[:, 0:NS * 128].rearrange("p (a b) -> p a b", b=128)
            for s in range(NS):
                tr(pbv[:, s, :], srcT.ap[:, s * 128:(s + 1) * 128], identb.ap, [srcT, identb], [pb], inc=(s == NS - 1))
            cp(eng, dst.ap, pbv, [pb], [dst])


        def mixer_block(bi, full, emit_sub0):
            cp("dve", hT.ap[:, :, 0:1], hT.ap[:, :, TB:TB + 1], [hT], [hT])
            for s in range(NS):
                g = bi * NS + s
                load_norm_T(xs[g * 128:(g + 1) * 128, :], nw1T, hT, s, off=1)
            dd('hT', hT, when=(bi == nl))
            ck('A')
            for h in range(4):
                pf = proj(4 + h)
                thf = tf.next()
                sig3(thf.ap[:, 0:TB], thf, pf.ap[:, 0:TB], [pf])
                f_ = tf.next()
                ts("dve", f_.ap[:, 0:TB], thf.ap[:, 0:TB], lb1T[:, h:h + 1], lbT[:, h:h + 1], ALU.mult, ALU.add,
                   [thf, DP], [f_])
                eb = tf.next()
                for c4 in range(TB // 64):
                    sl = slice(c4 * 64, (c4 + 1) * 64)
                    scan(eb.ap[:, sl], f_.ap[:, sl], ones.ap[:, 0:64], 1.0, ALU.mult, ALU.mult, [f_, ones], [eb])
                enb = tf.next()
                P.op("dve", lambda e, o=enb.ap[:, 0:TB], i=eb.ap[:, 0:TB]: e.reciprocal(out=RB(o), in_=RB(i)), [eb], [enb], dur=340.0)
                stt(hg_kT[h].ap, thf.ap[:, 0:TB], 1.0, enb.ap[:, 0:TB], ALU.subtract, ALU.mult, [thf, enb], [hg_kT[h]])
                cp("act", hg_ebl.ap[:, h, :], eb.ap[:, 63:TB:64], [eb], [hg_ebl])
                kh = tb.next()
                tt(POOL_E, kh.ap.rearrange("p (a b) -> p a b", b=64), hg_kT[h].ap.rearrange("p (a b) -> p a b", b=64),
                   hg_ebl.ap[:, h, :].unsqueeze(2).to_broadcast([128, TB // 64, 64]), ALU.mult,
                   [hg_kT[h], hg_ebl], [kh])
                to_tok(kh, hg_ktok[h])
                pi = proj(8 + h)
                vT = tb.next()
                cp("act", vT.ap, pi.ap[:, 0:TB], [pi], [vT])
                to_tok(vT, hg_vtok[h])
                if full:
                    pq = proj(h)
                    thq = tf.next()
                    sig3(thq.ap[:, 0:TB], thq, pq.ap[:, 0:TB], [pq])
                    s1 = tf.next()
                    tt("dve", s1.ap[:, 0:TB], thq.ap[:, 0:TB], pq.ap[:, 0:TB], ALU.mult, [thq, pq], [s1])
                    stt(hg_qT[h].ap, s1.ap[:, 0:TB], cq2[:, h:h + 1], eb.ap[:, 0:TB], ALU.mult, ALU.mult,
                        [s1, eb, DP], [hg_qT[h]])
                    pg = proj(12 + h)
                    thg = tf.next()
                    sig3(thg.ap[:, 0:TB], thg, pg.ap[:, 0:TB], [pg])
                    tt("dve", hg_sgT[h].ap, thg.ap[:, 0:TB], pg.ap[:, 0:TB], ALU.mult, [thg, pg], [hg_sgT[h]])

            dd('hg_kT0', hg_kT[0], when=(bi == nl)); dd('hg_qT0', hg_qT[0], when=(bi == nl)); dd('hg_vtok0', hg_vtok[0], when=(bi == nl)); dd('hg_ktok0', hg_ktok[0], when=(bi == nl)); dd('hg_sgT0', hg_sgT[0], when=(bi == nl)); dd('hg_ebl', hg_ebl, when=(bi == nl))
            ck('hgprep')
            def shift_mix(pb, ci, out_ap, out_tl, rows=slice(0, 128)):
                a1_ = tf.next()
                act(a1_.ap[:, 0:TB], pb.ap[:, 1:TB + 1], AF.Identity, [pb, DP], [a1_], scale=omT[:, ci:ci + 1])
                stt(out_ap, pb.ap[rows, 0:TB], muT[rows, ci:ci + 1], a1_.ap[rows, 0:TB], ALU.mult, ALU.add,
                    [pb, a1_, PT], [out_tl])

            pl = proj(28, halo=True)
            lo = tf.next()
            shift_mix(pl, 12, lo.ap[:, 0:TB], lo)
            e_ = tf.next()
            act(e_.ap[0:64, 0:TB], lo.ap[0:64, 0:TB], AF.Exp, [lo], [e_], scale=-2.0)
            act(e_.ap[0:64, 0:TB], e_.ap[0:64, 0:TB], AF.Ln, [e_], [e_], bias=1.0)
            act(e_.ap[0:64, 0:TB], e_.ap[0:64, 0:TB], AF.Exp, [e_], [e_], scale=-1.0)
            ts("dve", twd.ap[0:64, :], e_.ap[0:64, 0:TB], 2.0, -1.0, ALU.mult, ALU.add, [e_], [twd])
            cp("act", adT.ap[64:128, :], lo.ap[64:128, 0:TB], [lo], [adT])
            if full:
                pg_ = proj(29, halo=True)
                gdm = tf.next()
                shift_mix(pg_, 13, gdm.ap[:, 0:TB], gdm)
                sig3(sgd.ap, sgd, gdm.ap[:, 0:TB], [gdm])
            for c in range(4):
                cs = slice(c * 128, (c + 1) * 128)
                pw = psum("half")
                mm(pw.ap[:, 0:TB], w2b.ap[0:64, cs], twd.ap[0:64, :], True, True, [w2b, twd], [pw])
                ld = tf.next()
                sig3(ld.ap[:, 0:TB], ld, pw.ap[:, 0:TB], [pw], bias=hw0[:, c:c + 1])
                cld = float(np.exp(-0.5))
                lg = tf.next()
                for s in range(NS):
                    sl = slice(s * 128, (s + 1) * 128)
                    scan(lg.ap[:, sl], ld.ap[:, sl], zeros.ap, 0.0, ALU.add, ALU.add, [ld, zeros], [lg])
                act(rw_egx[c].ap[:, :, 1:129], lg.ap[:, 0:TB].rearrange("p (a b) -> p a b", b=128), AF.Exp,
                    [lg], [rw_egx[c]], scale=-cld)
                eng_t = tf.next()
                eng_ap = eng_t.ap[:, 0:TB]
                act(eng_ap, lg.ap[:, 0:TB], AF.Exp, [lg], [eng_t], scale=cld)
                pa = psum("half")
                mm(pa.ap[:, 0:TB], a2b.ap[64:128, cs], adT.ap[64:128, :], True, True, [a2b, adT], [pa])
                tha_t = tf.next()
                tha_ap = tha_t.ap[:, 0:TB]
                sig3(tha_ap, tha_t, pa.ap[:, 0:TB], [pa], bias=ha0[:, c:c + 1])
                if full:
                    pgm = psum("half")
                    mm(pgm.ap[:, 0:TB], g2b.ap[:, cs], sgd.ap, True, True, [g2b, sgd], [pgm])
                    cp("act", rw_gT[c].ap, pgm.ap[:, 0:TB], [pgm], [rw_gT[c]])
                pk = proj(20 + c, halo=True)
                kr = tf.next()
                shift_mix(pk, 4 + c, kr.ap[:, 0:TB], kr)
                kk = tf.next()
                ts("dve", kk.ap[:, 0:TB], kr.ap[:, 0:TB], kkT[:, c:c + 1], None, ALU.mult, None, [kr, PT], [kk])
                sq = tf.next()
                act(sq.ap[:, 0:TB], kk.ap[:, 0:TB], AF.Square, [kk], [sq])
                pss = psum("half")
                mm(pss.ap[:, 0:TB], blk.ap, sq.ap[:, 0:TB], True, True, [blk, sq], [pss])
                mx = tf.next()
                rn = tf.next()
                act(mx.ap[:, 0:TB], pss.ap[:, 0:TB], AF.Ln, [pss], [mx], bias=float(2.0 ** -60))
                act(rn.ap[:, 0:TB], mx.ap[:, 0:TB], AF.Exp, [mx], [rn], scale=-0.5)
                kkn = tf.next()
                tt(POOL_E, kkn.ap[:, 0:TB], kk.ap[:, 0:TB], rn.ap[:, 0:TB], ALU.mult, [kk, rn], [kkn])
                t1 = tf.next()
                ts("dve", t1.ap[:, 0:TB], tha_ap, 1.0, kaT[:, c:c + 1], ALU.subtract, ALU.mult,
                   [tha_t, PT], [t1])
                kp = tf.next()
                stt(kp.ap[:, 0:TB], t1.ap[:, 0:TB], 1.0, kr.ap[:, 0:TB], ALU.add, ALU.mult, [t1, kr], [kp])
                tt(POOL_E, rw_kT[c].ap, kp.ap[:, 0:TB], eng_ap, ALU.mult, [kp, eng_t], [rw_kT[c]])
                b1 = tf.next()
                tt(POOL_E, b1.ap[:, 0:TB], tha_ap, kkn.ap[:, 0:TB], ALU.mult, [tha_t, kkn], [b1])
                tt(POOL_E, rw_bT[c].ap, b1.ap[:, 0:TB], eng_ap, ALU.mult, [b1, eng_t], [rw_bT[c]])
                for j in range(2):
                    pj = slice(64 * j, 64 * j + 64)
                    stt(rw_ar[c].ap[pj, j:2 * NS:2, 0, :], kkn.ap[pj, 0:TB].rearrange("p (a b) -> p a b", b=128), -1.0,
                        rw_egx[c].ap[pj, :, 0:128], ALU.mult, ALU.mult, [kkn, rw_egx[c]], [rw_ar[c]])
                to_tok(rw_kT[c], rw_ktok[c])
                to_tok(rw_bT[c], rw_btok[c])
                pv = proj(24 + c, halo=True)
                vr = tf.next()
                shift_mix(pv, 8 + c, vr.ap[:, 0:TB], vr)
                vTb = tb.next()
                cp("act", vTb.ap, vr.ap[:, 0:TB], [vr], [vTb])
                to_tok(vTb, rw_vtok[c])
                if full:
                    pr = proj(16 + c, halo=True)
                    rr = tf.next()
                    shift_mix(pr, c, rr.ap[:, 0:TB], rr)
                    for j in range(2):
                        pj = slice(64 * j, 64 * j + 64)
                        tt(POOL_E, rw_ar[c].ap[pj, j:2 * NS:2, 1, :], rr.ap[pj, 0:TB].rearrange("p (a b) -> p a b", b=128),
                           rw_egx[c].ap[pj, :, 1:129], ALU.mult, [rr, rw_egx[c]], [rw_ar[c]])
                    rkr = tf.next()
                    stt(rkr.ap[:, 0:TB], rr.ap[:, 0:TB], rkT[:, c:c + 1], kp.ap[:, 0:TB], ALU.mult, ALU.mult,
                        [rr, kp, PT], [rkr])
                    pbs = psum("half")
                    mm(pbs.ap[:, 0:TB], blk.ap, rkr.ap[:, 0:TB], True, True, [blk, rkr], [pbs])
                    tt("dve", rw_bon[c].ap, pbs.ap[:, 0:TB], vr.ap[:, 0:TB], ALU.mult, [pbs, vr], [rw_bon[c]])

            dd('rw_kT0', rw_kT[0], when=(bi == nl)); dd('rw_bT0', rw_bT[0], when=(bi == nl)); dd('rw_ar0', rw_ar[0], when=(bi == nl)); dd('rw_egx0', rw_egx[0], when=(bi == nl)); dd('rw_vtok0', rw_vtok[0], when=(bi == nl)); dd('rw_gT0', rw_gT[0], when=(bi == nl)); dd('rw_bon0', rw_bon[0], when=(bi == nl));
            ck('rwprep')
            for s in range(NS):
                ssl = slice(s * 128, (s + 1) * 128)
                emit = full and (s >= emit_sub0)
                if emit:
                    pa = psum()
                    for h in range(4):
                        mm(pa.ap[:, h * 128:(h + 1) * 128], hg_kT[h].ap[:, ssl], hg_qT[h].ap[:, ssl], True, True,
                           [hg_kT[h], hg_qT[h]], [pa], inc=(h == 3))
                    ath = ATh.next()
                    tt("dve", ath.ap, pa.ap.rearrange("p (a b) -> p a b", b=128),
                       mUI.ap.unsqueeze(1).to_broadcast([128, 4, 128]), ALU.mult, [pa, mUI], [ath])
                    po = psum()
                for cc in range(2):
                    ps_ = slice(64 * cc, 64 * cc + 64)
                    ci = s * 2 + cc
                    if emit:
                        for h in range(4):
                            hs = slice(h * 128, (h + 1) * 128)
                            mm(po.ap[ps_, hs], hg_qT[h].ap[:, s * 128 + 64 * cc:s * 128 + 64 * cc + 64],
                               Sb_hg.ap[:, h, :], True, False, [hg_qT[h], Sb_hg], [po], inc=False)
                            mm(po.ap[ps_, hs], ath.ap[:, h, 64 * cc:64 * cc + 64], hg_vtok[h].ap[:, s, :],
                               False, True, [ath, hg_vtok[h]], [po], inc=(h == 3))
                    pS = psum()
                    for h in range(4):
                        hs = slice(h * 128, (h + 1) * 128)
                        mm(pS.ap[:, hs], hg_ktok[h].ap[ps_, s, :], hg_vtok[h].ap[ps_, s, :], True, True,
                           [hg_ktok[h], hg_vtok[h]], [pS], inc=(h == 3))
                    for h in range(4):
                        hs = slice(h * 128, (h + 1) * 128)
                        stt(S_hg.ap[:, h, :], S_hg.ap[:, h, :], hg_ebl.ap[:, h, ci:ci + 1], pS.ap[:, hs],
                            ALU.mult, ALU.add, [S_hg, hg_ebl, pS], [S_hg])
                    cp("act", Sb_hg.ap, S_hg.ap, [S_hg], [Sb_hg])
                if emit:
                    s2 = sm.next()
                    jk = ysb.next()
                    for h in range(4):
                        act(jk.ap[:, h * 128:(h + 1) * 128], po.ap[:, h * 128:(h + 1) * 128], AF.Square, [po], [jk, s2],
                            accum_out=s2.ap[:, h:h + 1])
                    rs = sm.next()
                    rsqrt_small(rs.ap[:, 0:4], [], s2.ap[:, 0:4], 4, 1.0 / 128, 1e-6, [s2], [rs])
                    on = ynb.next()
                    tt("dve", on.ap.rearrange("p (a b) -> p a b", b=128), po.ap.rearrange("p (a b) -> p a b", b=128),
                       rs.ap[:, 0:4].unsqueeze(2).to_broadcast([128, 4, 128]), ALU.mult, [po, rs], [on])
                    pt_ = psum("half")
                    ptv = pt_.ap.bitcast(BF16)[:, 0:512].rearrange("p (a b) -> p a b", b=128)
                    for h in range(4):
                        tr(ptv[:, h, :], on.ap[:, h * 128:(h + 1) * 128], identb.ap, [on, identb], [pt_], inc=(h == 3))
                    ot = oT.next()
                    for h in range(4):
                        stt(ot.ap[:, h, :], ptv[:, h, :], hgnw[:, h:h + 1], hg_sgT[h].ap[:, ssl], ALU.mult, ALU.mult,
                            [pt_, DP, hg_sgT[h]], [ot])
                if emit:
                    dd('on', on); dd('ot_hg', ot, ap=ot.ap[:, 0:4, :])
                dd('S_hg', S_hg, when=(bi == nl and s == NS - 1))
                ck('hgchain')
                for c in range(4):
                    for (lt, dst) in ((rw_bT[c], ATb[c]), (rw_kT[c], ATk[c])):
                        pA = psum()
                        if emit:
                            mm(pA.ap, lt.ap[:, ssl], rw_ar[c].ap[:, 2 * s:2 * s + 2, :, :], True, True, [lt, rw_ar[c]], [pA])
                            tt("dve", dst.ap, pA.ap.rearrange("p (a b) -> p a b", b=128), mRW.ap, ALU.mult, [pA, mRW], [dst])
                        else:
                            mm(pA.ap[:, 0:256], lt.ap[:, ssl], rw_ar[c].ap[:, 2 * s:2 * s + 2, 0, :], True, True,
                               [lt, rw_ar[c]], [pA])
                            tt("dve", dst.ap[:, 0:3:2, :], pA.ap[:, 0:256].rearrange("p (a b) -> p a b", b=128),
                               mRW.ap[:, 0:3:2, :], ALU.mult, [pA, mRW], [dst])
                for g4 in range(2):
                    pN = psum()
                    for hh in range(4):
                        h = g4 * 4 + hh
                        c, j = h // 2, h % 2
                        pj = slice(64 * j, 64 * j + 64)
                        mm(pN.ap[:, hh * 128:(hh + 1) * 128], arv(c, s, j, 0), rw_bT[c].ap[:, ssl], True, True,
                           [rw_ar[c], rw_bT[c]], [pN], inc=(hh == 3))
                    Pc = Pq[s * 2 + g4]
                    tt("dve", Pc.ap, pN.ap.rearrange("p (a b) -> p a b", b=128),
                       mL.ap.unsqueeze(1).to_broadcast([128, 4, 128]), ALU.mult, [pN, mL], [Pc])
                    TTc = TTq[s * 2 + g4]
                    for cc_ in range(2):
                        c_ = g4 * 2 + cc_
                        tt(POOL_E, TTc.ap[:, 2 * cc_:2 * cc_ + 2, :], ATb[c_].ap[:, 0:3:2, :],
                           identb.ap.unsqueeze(1).to_broadcast([128, 2, 128]), ALU.add, [ATb[c_], identb], [TTc])
                    PTc = None
                    nlev = 6
                    for lev in range(1, nlev + 1):
                        last = (lev == nlev)
                        pP = psum()
                        def ptv_(hh):
                            if PTc is None:
                                h = g4 * 4 + hh
                                return ATb[h // 2].ap[:, 2 * (h % 2), :], ATb[h // 2]
                            return PTc.ap[:, hh, :], PTc
                        for hh in range(4):
                            pa_, pt_l = ptv_(hh)
                            mm(pP.ap[:, hh * 128:(hh + 1) * 128], pa_, Pc.ap[:, hh, :], True, True,
                               [pt_l, Pc], [pP], inc=(hh == 3))
                        if not last:
                            pPT = psum()
                            for hh in range(4):
                                pa_, pt_l = ptv_(hh)
                                mm(pPT.ap[:, hh * 128:(hh + 1) * 128], Pc.ap[:, hh, :], pa_, True, True,
                                   [pt_l, Pc], [pPT], inc=(hh == 3))
                        Pn = Pc
                        cp("act", Pn.ap, pP.ap.rearrange("p (a b) -> p a b", b=128), [pP], [Pn])
                        if not last:
                            PTn = PTq[s * 2 + g4]
                            cp("act", PTn.ap, pPT.ap.rearrange("p (a b) -> p a b", b=128), [pPT], [PTn])
                        pT2 = psum()
                        for hh in range(4):
                            mm(pT2.ap[:, hh * 128:(hh + 1) * 128], Pn.ap[:, hh, :], TTc.ap[:, hh, :], True, True,
                               [Pn, TTc], [pT2], inc=(hh == 3))
                        TTn = TTc
                        tt("dve", TTn.ap, pT2.ap.rearrange("p (a b) -> p a b", b=128), TTc.ap, ALU.add, [pT2, TTc], [TTn])
                        Pc = Pn
                        if not last:
                            PTc = PTn
                        TTc = TTn
                dd('ATb0', ATb[0], when=(bi == nl and s == NS - 1)); dd('ATk0', ATk[0], when=(bi == nl and s == NS - 1))
                ck('rwinv')
                pR = psum()
                for h in range(8):
                    c, j = h // 2, h % 2
                    pj = slice(64 * j, 64 * j + 64)
                    hs = slice(h * 64, (h + 1) * 64)
                    mm(pR.ap[:, hs], arv(c, s, j, 0), Hb_rw.ap[:, c, :], True, False, [rw_ar[c], Hb_rw], [pR],
                       inc=False)
                    mm(pR.ap[:, hs], ATk[c].ap[:, 2 * j, :], rw_vtok[c].ap[:, s, 64 * j:64 * j + 64], False, True,
                       [ATk[c], rw_vtok[c]], [pR], inc=(h == 7))
                cp("act", Rb.ap, pR.ap, [pR], [Rb])
                pU = psum()
                for h in range(8):
                    hs = slice(h * 64, (h + 1) * 64)
                    mm(pU.ap[:, hs], TTq[s * 2 + h // 4].ap[:, h % 4, :], Rb.ap[:, hs], True, True, [TTq[s * 2 + h // 4], Rb], [pU],
                       inc=(h == 7))
                cp("act", Ub.ap, pU.ap, [pU], [Ub])
                if emit:
                    pY = psum()
                    for h in range(8):
                        c, j = h // 2, h % 2
                        pj = slice(64 * j, 64 * j + 64)
                        hs = slice(h * 64, (h + 1) * 64)
                        mm(pY.ap[:, hs], arv(c, s, j, 1), Hb_rw.ap[:, c, :], True, False, [rw_ar[c], Hb_rw],
                           [pY], inc=False)
                        mm(pY.ap[:, hs], ATb[c].ap[:, 2 * j + 1, :], Ub.ap[:, hs], False, False, [ATb[c], Ub], [pY], inc=False)
                        mm(pY.ap[:, hs], ATk[c].ap[:, 2 * j + 1, :], rw_vtok[c].ap[:, s, 64 * j:64 * j + 64], False, True,
                           [ATk[c], rw_vtok[c]], [pY], inc=(h == 7))
                pH = psum()
                for c in range(4):
                    cs_ = slice(c * 128, (c + 1) * 128)
                    mm(pH.ap[:, cs_], rw_btok[c].ap[:, s, :], Ub.ap[:, cs_], True, False, [rw_btok[c], Ub], [pH], inc=False)
                    mm(pH.ap[:, cs_], rw_ktok[c].ap[:, s, :], rw_vtok[c].ap[:, s, :], False, True,
                       [rw_ktok[c], rw_vtok[c]], [pH], inc=(c == 3))
                ht = tf.next()
                htv = ht.ap[:, 0:256].rearrange("p (a b) -> p a b", b=64)
                for j in range(2):
                    pj = slice(64 * j, 64 * j + 64)
                    tt("dve", htv[pj], pH.ap[pj, :].rearrange("p (c x) -> p c x", x=128)[:, :, 64 * j:64 * j + 64],
                       H_rw.ap[pj], ALU.add, [pH, H_rw], [ht])
                for c in range(4):
                    ts("dve", H_rw.ap[:, c, :], htv[:, c, :], rw_egx[c].ap[:, s, 128:129], None, ALU.mult, None,
                       [ht, rw_egx[c]], [H_rw])
                cp("act", Hb_rw.ap, H_rw.ap, [H_rw], [Hb_rw])
                if emit:
                    dd('Ub', Ub); dd('Rb', Rb); dd('H_rw', H_rw)
                    ck('rwchain')
                    yb = ysb.next()
                    cp("act", yb.ap, pY.ap, [pY], [yb])
                    yv = yb.ap.rearrange("p (a b) -> p a b", b=64)
                    s1_ = sm.next()
                    P.op("dve", lambda e, o=s1_.ap[:, 0:8], i=yv: e.tensor_reduce(out=o, in_=i, axis=AX.X, op=ALU.add),
                         [yb], [s1_], dur=600.0)
                    act(ysq.ap, yb.ap, AF.Square, [yb], [ysq])
                    s2_ = sm.next()
                    P.op("dve", lambda e, o=s2_.ap[:, 0:8], i=ysq.ap.rearrange("p (a b) -> p a b", b=64):
                         e.tensor_reduce(out=o, in_=i, axis=AX.X, op=ALU.add), [ysq], [s2_], dur=600.0)
                    mean = sm.next()
                    ts("dve", mean.ap[:, 0:8], s1_.ap[:, 0:8], 1.0 / 64, None, ALU.mult, None, [s1_], [mean])
                    msq = sm.next()
                    tt("dve", msq.ap[:, 0:8], mean.ap[:, 0:8], mean.ap[:, 0:8], ALU.mult, [mean], [msq])
                    var = sm.next()
                    stt(var.ap[:, 0:8], s2_.ap[:, 0:8], 1.0 / 64, msq.ap[:, 0:8], ALU.mult, ALU.subtract, [s2_, msq], [var])
                    rs_ = sm.next()
                    rsqrt_small(rs_.ap[:, 0:8], [], var.ap[:, 0:8], 8, 1.0, 64e-5, [var], [rs_])
                    ycen = ysb.next()
                    tt("dve", ycen.ap.rearrange("p (a b) -> p a b", b=64), yv,
                       mean.ap[:, 0:8].unsqueeze(2).to_broadcast([128, 8, 64]), ALU.subtract, [yb, mean], [ycen])
                    yn = ynb.next()
                    tt("dve", yn.ap.rearrange("p (a b) -> p a b", b=64), ycen.ap.rearrange("p (a b) -> p a b", b=64),
                       rs_.ap[:, 0:8].unsqueeze(2).to_broadcast([128, 8, 64]), ALU.mult, [ycen, rs_], [yn])
                    pt2 = psum("half")
                    ptv2 = pt2.ap.bitcast(BF16)[:, 0:512].rearrange("p (a b) -> p a b", b=128)
                    for c in range(4):
                        tr(ptv2[:, c, :], yn.ap[:, c * 128:(c + 1) * 128], identb.ap, [yn, identb], [pt2], inc=(c == 3))
                    for c in range(4):
                        t_ = tf.next()
                        stt(t_.ap[:, 0:128], ptv2[:, c, :], lnwT[:, c:c + 1], rw_bon[c].ap[:, ssl], ALU.mult, ALU.add,
                            [pt2, PT, rw_bon[c]], [t_])
                        stt(ot.ap[:, 4 + c, :], t_.ap[:, 0:128], lnbT[:, c:c + 1], rw_gT[c].ap[:, ssl], ALU.add, ALU.mult,
                            [t_, PT, rw_gT[c]], [ot])
                    dd('yn', yn); dd('ot', ot)
                    ck('rwout')
                    g = bi * NS + s
                    po1 = psum()
                    po2 = psum()
                    for kc in range(8):
                        mm(po1.ap, ot.ap[:, kc, :], w_out.ap[:, kc, 0:512], kc == 0, kc == 7, [ot, w_out], [po1], inc=False)
                    for kc in range(8):
                        mm(po2.ap, ot.ap[:, kc, :], w_out.ap[:, kc, 512:1024], kc == 0, kc == 7, [ot, w_out], [po2],
                           inc=(kc == 7))
                    xo = x1t.next()
                    P.dma("sp", "d_st1", xo.ap, xs[g * 128:(g + 1) * 128, :], writes=[xo])
                    tt("dve", xo.ap[:, 0:512], po1.ap, xo.ap[:, 0:512], ALU.add, [po1, xo], [xo])
                    tt("dve", xo.ap[:, 512:1024], po2.ap, xo.ap[:, 512:1024], ALU.add, [po2, xo], [xo])
                    row = (g - (nl + 1) * NS + 1) * 128
                    P.dma("sp", xo.sem, x1d[row:row + 128, :], xo.ap, reads=[xo], writes=[x1d_bufs[row // 128]])

        x1d_bufs = [Buf(f"x1d{i}") for i in range(1 + nm * NS)]
        print("arena phase1 used f32 words", st["off"], "of", ARENA_F32)
        try:
            for bi in range(nblk):
                if bi < nl:
                    mixer_block(bi, False, NS)
                elif bi == nl:
                    mixer_block(bi, True, NS - 1)
                else:
                    mixer_block(bi, True, 0)
        except StopBuild:
            P.barrier()
            P.replay(block)
            return nc, P
        if "x1" in dbg_t:
            P.barrier()
            P.dma("sp", "d_dbg", dbg_t["x1"], x1d[:, :], reads=x1d_bufs)

        NPRE = 7
        st_save = st["off"]
        st["off"] = glob_end
        w_up = carve("w_up", [NFC // 2, 8, 512], BF16)
        assert NPRE * 8 * 512 // 2 <= 8 * 3840 // 2
        st["off"] = st_save
        wup_src = dr["w_up"].rearrange("(kc p) c -> p kc c", p=128)
        wupb = [Buf(f"wup{gi}") for gi in range(NFC // 2)]

        def load_wup(gi, extra_writes=()):
            c0 = gi * 256
            P.dma("pool", f"d_wup{gi}", w_up.ap[:, gi, :, 0:256], wup_src[:, :, c0:c0 + 256],
                  writes=[wupb[gi]] + list(extra_writes))
            P.dma("pool", f"d_wup{gi}", w_up.ap[:, gi, :, 256:512], wup_src[:, :, DFF + c0:DFF + c0 + 256],
                  writes=[wupb[gi]] + list(extra_writes))

        if stop is None:
            for gi in range(NPRE):
                load_wup(gi, [b for (_, _, b) in wbufs])

        P.barrier()
        if stop == "p1":
            P.replay(block)
            return nc, P
        st["off"] = glob_end
        w_up = carve("w_up", [NFC // 2, 8, 512], BF16)
        w_dn = carve("w_dn", [NFC, D], BF16)
        xring2 = ring("x2t", NS + 1, [D], F32, ["d_x0", "d_x1", "d_x2", "d_x3"][:NS + 1])
        xnr = ring("xn2_", 2, [D], BF16)
        sm = ring("sm2_", 8, [16], F32)
        h2Ts = [carve(f"h2T{i}", [8, TB + 2], BF16) for i in range(2)]
        gT = carve("gT", [NFC, TB], BF16)
        acg = ring("acg", 4, [TB], F32)
        acv = ring("acv", 4, [TB], F32)
        sgl = ring("sgl", 4, [TB], F32)
        xo2 = ring("xo2", 1, [D], F32)
        fnwb = carve("fnwb", [D], F32)
        P.dma("sp", "d_fn", fnwb.ap, dr["final_norm_w"].partition_broadcast(128), writes=[fnwb])
        yo = ring("yo", 2, [D], F32, ["d_st2", "d_st3"])
        xring = xring2
        for gi in range(NPRE, NFC // 2):
            load_wup(gi)
        wdn_src = dr["w_down"].rearrange("(c p) n -> p c n", p=128)
        wdnb = []
        for gi in range(2):
            b = Buf(f"wdn{gi}")
            wdnb.append(b)
            P.dma("pool", f"d_wdn{gi}", w_dn.ap[:, gi * 11:(gi + 1) * 11, :], wdn_src[:, gi * 11:(gi + 1) * 11, :], writes=[b])
        print("arena phase2 used f32 words", st["off"], "of", ARENA_F32)
        cw0 = CW.ap[:, 0, :]
        cw1 = CW.ap[:, 1, :]
        cw2 = CW.ap[:, 2, :]
        cbT = CW.ap[:, 3, :]

        xt0 = xring.next()
        P.dma("sp", xt0.sem, xt0.ap, x1d[0:128, :], reads=[x1d_bufs[0]], writes=[xt0])
        load_norm_T(None, nw2T, h2Ts[1], NS - 1, xt=xt0, off=2)
        ts("dve", h2Ts[1].ap[:, :, TB:TB + 2], h2Ts[1].ap[:, :, TB:TB + 2], flg.ap[:, 0:1], None, ALU.mult, None,
           [h2Ts[1], flg], [h2Ts[1]])

        for m in range(nm):
            x1ts = []
            h2T = h2Ts[m % 2]
            cp("dve", h2T.ap[:, :, 0:2], h2Ts[(m + 1) % 2].ap[:, :, TB:TB + 2], [h2Ts[(m + 1) % 2]], [h2T])
            for s in range(NS):
                row = 128 + (m * NS + s) * 128
                xt = xring.next()
                P.dma("sp", xt.sem, xt.ap, x1d[row:row + 128, :], reads=[x1d_bufs[row // 128]], writes=[xt])
                load_norm_T(None, nw2T, h2T, s, xt=xt, off=2)
                x1ts.append(xt)
            for c in range(NFC):
                res = []
                for (cc, ac_ring) in ((c, acg), (NFC + c, acv)):
                    pb = psum()
                    for kc in range(8):
                        lc = (0 if cc < NFC else 256) + (c % 2) * 128
                        mm(pb.ap[:, 0:TB + 2], w_up.ap[:, c // 2, kc, lc:lc + 128], h2T.ap[:, kc, :], kc == 0, kc == 7,
                           [wupb[c // 2], h2T], [pb], inc=(kc == 7))
                    ac = ac_ring.next()
                    act(ac.ap, pb.ap[:, 2:TB + 2], AF.Identity, [pb, CW], [ac], scale=cw2[:, cc:cc + 1], bias=cbT[:, cc:cc + 1])
                    stt(ac.ap, pb.ap[:, 1:TB + 1], cw1[:, cc:cc + 1], ac.ap, ALU.mult, ALU.add, [pb, ac, CW], [ac])
                    stt(ac.ap, pb.ap[:, 0:TB], cw0[:, cc:cc + 1], ac.ap, ALU.mult, ALU.add, [pb, ac, CW], [ac])
                    res.append(ac)
                sg_ = sgl.next()
                act(sg_.ap, res[0].ap, AF.Silu, [res[0]], [sg_])
                tt("dve", gT.ap[:, c, :], sg_.ap, res[1].ap, ALU.mult, [sg_, res[1]], [gT])
            for s in range(NS):
                ssl = slice(s * 128, (s + 1) * 128)
                po1 = psum()
                po2 = psum()
                for c in range(NFC):
                    mm(po1.ap, gT.ap[:, c, ssl], w_dn.ap[:, c, 0:512], c == 0, c == NFC - 1, [gT, wdnb[c // 11]], [po1],
                       inc=False)
                for c in range(NFC):
                    mm(po2.ap, gT.ap[:, c, ssl], w_dn.ap[:, c, 512:1024], c == 0, c == NFC - 1, [gT, wdnb[c // 11]], [po2],
                       inc=(c == NFC - 1))
                xo = xo2.next()
                xt = x1ts[s]
                tt("dve", xo.ap[:, 0:512], po1.ap, xt.ap[:, 0:512], ALU.add, [po1, xt], [xo])
                tt("dve", xo.ap[:, 512:1024], po2.ap, xt.ap[:, 512:1024], ALU.add, [po2, xt], [xo])
                ss = sm.next()
                jk = xnr.next()
                act(jk.ap, xo.ap, AF.Square, [xo], [jk, ss], accum_out=ss.ap[:, 0:1])
                rs = sm.next()
                rsqrt_small(rs.ap[:, 0:1], [], ss.ap[:, 0:1], 1, 1.0 / D, 1e-6, [ss], [rs])
                y_ = yo.next()
                stt(y_.ap, xo.ap, rs.ap[:, 0:1], fnwb.ap, ALU.mult, ALU.mult, [xo, rs, fnwb], [y_])
                row = (m * NS + s) * 128
                P.dma("sp", y_.sem, out[row:row + 128, :], y_.ap, reads=[y_])
        P.barrier()
        P.replay(block)
    return nc, P


def _prep_inputs(inputs):
    sq = {}
    for name, shp in PARAM_SHAPES:
        a = np.asarray(inputs[name], dtype=np.float32)
        sq[name] = np.ascontiguousarray(a.reshape(shp))
    return sq


def kernel(**inputs):
    x = np.asarray(inputs["x"], dtype=np.float32)
    B, T, Dm = x.shape
    half = T // 2
    params = _prep_inputs(inputs)
    nl = half // TB - 1
    nm = half // TB
    nc, _ = build(nl, nm)
    in_maps = []
    for c in range(8):
        b, j = c // 2, c % 2
        xs = np.zeros((2 * half, Dm), np.float32)
        if j == 1:
            xs[:half] = x[b, :half]
        xs[half:] = x[b, j * half:(j + 1) * half]
        m = dict(params)
        m["xs"] = xs
        m["flag"] = np.full((128, 1), float(j), np.float32)
        in_maps.append(m)
    res = run_bass_kernel_spmd(nc, in_maps, core_ids=list(range(8)))
    outp = np.empty((B, T, Dm), np.float32)
    for c in range(8):
        b, j = c // 2, c % 2
        outp[b, j * half:(j + 1) * half] = res.results[c]["out"]
    return outp
```

```python
import numpy as np
from contextlib import ExitStack
import concourse.bass as bass
import concourse.mybir as mybir
from concourse.bass_utils import run_bass_kernel_spmd
from concourse.ap import AP

F32 = mybir.dt.float32
BF16 = mybir.dt.bfloat16
AF = mybir.ActivationFunctionType
ALU = mybir.AluOpType
AX = mybir.AxisListType

SELF_SYNC = True
PROFILE = False
NTF = 27
POOL_E = "dve"
PSH_BANKS = 0
VWIN = 100
TB = 256
NS = TB // 128
D = 1024
DFF = 2816
NFC = DFF // 128


class StopBuild(Exception):
    pass


class Buf:
    __slots__ = ("name", "w", "r", "nodes", "valloc")

    def __init__(self, name):
        self.name = name
        self.w = None
        self.r = []
        self.valloc = None
        self.nodes = None


_SM = {"pass": 0, "vallocs": [], "assign": {}}
SMART_RINGS = False


class Tl:
    __slots__ = ("ap", "buf", "sem")

    def __init__(self, ap, name, sem=None):
        self.ap = ap
        self.buf = Buf(name)
        self.sem = sem


class Ring:
    def __init__(self, tiles, name=None):
        self.t = tiles
        self.i = 0
        self.name = name

    def next(self):
        i = self.i
        self.i += 1
        if self.name is not None and len(self.t) > 1:
            if _SM["pass"] == 1:
                prev = _SM["assign"].get(self.name)
                t = self.t[prev[i]] if prev is not None else self.t[i % len(self.t)]
                t.buf.nodes = []
                _SM["vallocs"].append((self.name, i, len(self.t), t.buf.nodes))
                return t
            if _SM["pass"] == 2 and self.name in _SM["assign"]:
                return self.t[_SM["assign"][self.name][i]]
        return self.t[i % len(self.t)]


def _bufs(xs):
    out = []
    for x in xs:
        if x is None:
            continue
        out.append(x.buf if isinstance(x, Tl) else x)
    return out


class Node:
    __slots__ = ("id", "eng", "fn", "preds", "succs", "dur", "kind", "semkey", "lat", "prio", "start", "fin",
                 "ev", "nready", "tag", "seg", "vallocs")

    def __init__(self, id_, eng, fn, dur, kind, semkey, lat):
        self.id = id_
        self.eng = eng
        self.fn = fn
        self.preds = set()
        self.succs = []
        self.dur = dur
        self.kind = kind
        self.semkey = semkey
        self.lat = lat
        self.prio = 0.0
        self.ev = None


class Prog:
    ENG = ("pe", "act", "dve", "pool", "sp")
    XLAT = 120.0

    def __init__(self, nc, sems, group_sems=()):
        self.nc = nc
        self.q = {e: [] for e in self.ENG}
        self.sem = sems
        self.group = set(group_sems)
        self.cnt = {k: 0 for k in sems}
        self.known = {e: {} for e in self.ENG}
        self.snap = {}
        self.nodes = []
        self.seg = 0
        self.valloc_nodes = {}
        self.place = {}
        self.alloc_class = {}
        self.class_slots = {}
        self.nwait = 0
        self.nins = {e: 0 for e in self.ENG}
        self.sim_end = 0.0
        self.sim_total = 0.0
        self.profile = None

    def _add(self, e, fn, reads, writes, dur, kind, semkey, lat):
        n = Node(len(self.nodes), e, fn, dur, kind, semkey, lat)
        n.seg = self.seg
        n.vallocs = None
        for b in list(reads) + list(writes):
            if b.nodes is not None:
                b.nodes.append(n)
            if b.valloc is not None:
                if n.vallocs is None:
                    n.vallocs = []
                if b.valloc not in n.vallocs:
                    n.vallocs.append(b.valloc)
                    self.valloc_nodes.setdefault(b.valloc, []).append(n)
        if self.profile is not None:
            import sys as _sys
            f = _sys._getframe(3)
            n.tag = f.f_lineno if f.f_code.co_name not in ("act", "tt", "ts", "stt", "cp", "mm", "tr", "scan") else _sys._getframe(4).f_lineno
        sg = self.seg
        for b in reads:
            if b.w is not None and b.w[0] == sg:
                n.preds.add(b.w[1])
        for b in writes:
            if b.w is not None and b.w[0] == sg:
                n.preds.add(b.w[1])
            for r in b.r:
                if r[0] == sg:
                    n.preds.add(r[1])
        n.preds.discard(n.id)
        me = (sg, n.id)
        for b in writes:
            b.w = me
            b.r = []
        for b in reads:
            if not b.r or b.r[-1] != me:
                b.r.append(me)
        self.nodes.append(n)
        return n

    def op(self, e, fn, reads=(), writes=(), inc=True, dur=100.0):
        self._add(e, fn, _bufs(reads), _bufs(writes), dur, "op", None, dur if e != "pe" else 60.0)

    def dma(self, e, semkey, out, in_, reads=(), writes=(), nbytes=65536, **kw):
        nbytes = int(np.prod(out.shape)) * 4
        lat = 2000.0 + nbytes / 150.0
        self._add(e, lambda eng: eng.dma_start(out=out, in_=in_, **kw), _bufs(reads), _bufs(writes), 60.0, "dma",
                  semkey, lat)

    def _schedule(self):
        import heapq
        nodes = self.nodes
        for n in nodes:
            for p in n.preds:
                nodes[p].succs.append(n.id)
        for n in reversed(nodes):
            best = 0.0
            for s_ in n.succs:
                sn = nodes[s_]
                lat = 0.0 if (n.eng == "pe" and sn.eng == "pe") else (n.lat + self.XLAT)
                v = sn.prio + lat
                if v > best:
                    best = v
            n.prio = n.dur + best
        free = {e: 0.0 for e in self.ENG}
        waiting = {e: [] for e in self.ENG}
        avail = {e: [] for e in self.ENG}
        blocked = []
        ready_t = [0.0] * len(nodes)
        npred = [len(n.preds) for n in nodes]
        for n in nodes:
            if npred[n.id] == 0:
                heapq.heappush(waiting[n.eng], (0.0, n.id))
        vnodes = self.valloc_nodes
        remaining_v = {k: len(v) for k, v in vnodes.items()}
        cls_of = self.alloc_class
        place = self.place
        owner = {c: [None] * n_ for c, n_ in self.class_slots.items()}
        free_t = {c: [0.0] * n_ for c, n_ in self.class_slots.items()}
        vorder = {c: [] for c in self.class_slots}
        for k in sorted(vnodes.keys()):
            vorder[cls_of[k]].append(k)
        optr = {c: 0 for c in self.class_slots}
        rptr = {c: 0 for c in self.class_slots}
        vfin = {k: 0.0 for k in vnodes}

        placed_flag = [False]

        def try_alloc(n):
            need_ = [k for k in n.vallocs if k not in place]
            if not need_:
                return True
            plan = []
            used = {}
            for k in need_:
                c = cls_of[k]
                vo = vorder[c]
                while optr[c] < len(vo) and vo[optr[c]] in place:
                    optr[c] += 1
                is_oldest = optr[c] < len(vo) and vo[optr[c]] == k
                ro = rptr[c]
                while ro < len(vo) and remaining_v[vo[ro]] == 0:
                    ro += 1
                rptr[c] = ro
                if ro < len(vo) and k > vo[ro] + VWIN:
                    return False
                ow = owner[c]
                freeb = [b for b in range(len(ow)) if (ow[b] is None or remaining_v[ow[b]] == 0)
                         and b not in used.get(c, ())]
                if not freeb or (len(freeb) < 2 and not is_oldest):
                    return False
                b = min(freeb, key=lambda b_: free_t[c][b_])
                used.setdefault(c, set()).add(b)
                plan.append((k, c, b))
            for k, c, b in plan:
                prev = owner[c][b]
                if prev is not None:
                    for p in vnodes[prev]:
                        if p.id not in n.preds:
                            n.preds.add(p.id)
                            p.succs.append(n.id)
                        lat = 0.0 if (p.eng == "pe" and n.eng == "pe") else (p.lat + self.XLAT)
                        rt = p.fin + lat
                        if rt > ready_t[n.id]:
                            ready_t[n.id] = rt
                owner[c][b] = k
                place[k] = b
            placed_flag[0] = True
            return True

        order = []
        remaining = len(nodes)
        while remaining:
            cand = []
            for e in self.ENG:
                w, a = waiting[e], avail[e]
                while w and w[0][0] <= free[e]:
                    t_, i_ = heapq.heappop(w)
                    heapq.heappush(a, (-nodes[i_].prio, i_))
                if a:
                    cand.append((free[e], e))
                elif w:
                    cand.append((w[0][0], e))
            cand.sort()
            picked = None
            for t, e in cand:
                tmp = []
                while True:
                    if avail[e]:
                        item = heapq.heappop(avail[e])
                        i_ = item[1]
                    elif waiting[e]:
                        item = heapq.heappop(waiting[e])
                        i_ = item[1]
                    else:
                        break
                    n = nodes[i_]
                    if n.vallocs is not None and not try_alloc(n):
                        blocked.append(i_)
                        continue
                    picked = n
                    break
                if picked is not None:
                    break
            if picked is None:
                raise RuntimeError("scheduler deadlock: all candidates blocked on resource slots")
            n = picked
            e = n.eng
            i_ = n.id
            start = max(free[e], ready_t[i_])
            n.start = start
            n.fin = start + n.dur
            free[e] = n.fin
            order.append(n)
            remaining -= 1
            if n.vallocs is not None:
                released = placed_flag[0]
                placed_flag[0] = False
                for k in n.vallocs:
                    remaining_v[k] -= 1
                    if n.fin > vfin[k]:
                        vfin[k] = n.fin
                    if remaining_v[k] == 0:
                        free_t[cls_of[k]][place[k]] = vfin[k]
                        released = True
                if released and blocked:
                    for j_ in blocked:
                        heapq.heappush(waiting[nodes[j_].eng], (ready_t[j_], j_))
                    blocked = []
            for s_ in n.succs:
                sn = nodes[s_]
                lat = 0.0 if (n.eng == "pe" and sn.eng == "pe") else (n.lat + self.XLAT)
                rt = start + lat if n.kind == "dma" else n.fin + (0.0 if lat == 0.0 else lat)
                if rt > ready_t[s_]:
                    ready_t[s_] = rt
                npred[s_] -= 1
                if npred[s_] == 0:
                    heapq.heappush(waiting[sn.eng], (ready_t[s_], s_))
        self.sim_end = max(free.values()) if nodes else 0.0
        if self.profile is not None:
            self.profile.append((self.sim_end, [(n.eng, n.tag, n.dur, n.start) for n in nodes]))
        return order

    def _merge(self, e, k, v):
        kn = self.known[e]
        if kn.get(k, 0) >= v:
            return False
        kn[k] = v
        s = self.snap.get((k, v))
        if s:
            for k2, v2 in s.items():
                if kn.get(k2, 0) < v2:
                    kn[k2] = v2
        return True

    def flush(self):
        nodes = self.nodes
        if not nodes:
            return
        order = self._schedule()
        self.sim_total += self.sim_end
        gtot = {}
        for n in nodes:
            if n.kind == "dma" and n.semkey in self.group:
                gtot[n.semkey] = gtot.get(n.semkey, self.cnt[n.semkey]) + 16
        need = [False] * len(nodes)
        for n in nodes:
            if n.kind == "dma":
                need[n.id] = True
                continue
            for s_ in n.succs:
                sn = nodes[s_]
                if not (n.eng == "pe" and sn.eng == "pe"):
                    if sn.eng != n.eng or SELF_SYNC:
                        need[n.id] = True
                        break
        sems = self.sem
        for n in order:
            e = n.eng
            evs = {}
            for p in n.preds:
                pn = nodes[p]
                if pn.eng == "pe" and e == "pe" and pn.kind == "op":
                    continue
                if pn.kind == "op" and pn.eng == e and not SELF_SYNC:
                    continue
                if pn.kind == "dma" and n.kind == "dma" and pn.semkey == n.semkey and n.semkey in self.group:
                    continue
                k, v = pn.ev
                if evs.get(k, 0) < v:
                    evs[k] = v
            waits = []
            for k, v in evs.items():
                if self._merge(e, k, v):
                    waits.append((k, v))
            if n.kind == "dma":
                k = n.semkey
                self.cnt[k] += 16
                n.ev = (k, gtot[k]) if k in self.group else (k, self.cnt[k])
                amount, semkey = 16, k
                if k not in self.group:
                    self.snap[n.ev] = dict(self.known[e])
            elif need[n.id]:
                self.cnt[e] += 1
                n.ev = (e, self.cnt[e])
                self.snap[n.ev] = dict(self.known[e])
                amount, semkey = 1, e
            else:
                n.ev = (e, self.cnt[e] + 1)
                amount, semkey = 0, e
            self.nwait += len(waits)
            self.nins[e] += 1

            def run(eng, waits=waits, fn=n.fn, amount=amount, semkey=semkey):
                for k, v in waits:
                    eng.wait_ge(sems[k], v)
                ins = fn(eng)
                if amount:
                    ins.then_inc(sems[semkey], amount)
            self.q[e].append(run)
        self.nodes = []
        self.valloc_nodes = {}
        self.seg += 1

    def barrier(self):
        self.flush()
        for e in self.ENG:
            waits = []
            for k, v in self.cnt.items():
                if v > 0 and k != e and self.known[e].get(k, 0) < v:
                    self.known[e][k] = v
                    waits.append((k, v))
            sems = self.sem

            def run(eng, waits=waits):
                for k, v in waits:
                    eng.wait_ge(sems[k], v)
            self.q[e].append(run)

    def replay(self, block):
        self.flush()
        if _SM["pass"] == 1:
            return
        q = self.q

        @block.tensor
        def _(eng):
            for f in q["pe"]:
                f(eng)

        @block.scalar
        def _(eng):
            for f in q["act"]:
                f(eng)

        @block.vector
        def _(eng):
            for f in q["dve"]:
                f(eng)

        @block.gpsimd
        def _(eng):
            for f in q["pool"]:
                f(eng)

        @block.sync
        def _(eng):
            for f in q["sp"]:
                f(eng)


PARAM_SHAPES = [
    ("norm1_w", [1024]), ("w_in", [1024, 3840]), ("hg_lb_logits", [2, 512]), ("hg_norm_w", [512]),
    ("rw_shift_mu", [1792]), ("rw_w0", [512]), ("rw_w2", [64, 512]), ("rw_a0", [512]),
    ("rw_a2", [64, 512]), ("rw_g2", [128, 512]), ("rw_k_k", [512]), ("rw_k_a", [512]),
    ("rw_r_k", [512]), ("rw_ln_w", [512]), ("rw_ln_b", [512]), ("w_out", [1024, 1024]),
    ("norm2_w", [1024]), ("w_up", [1024, 5632]), ("conv_w", [3, 5632]), ("conv_b", [5632]),
    ("w_down", [2816, 1024]), ("final_norm_w", [1024]),
]


def _assign_slots():
    by = {}
    for name, i, nslots, nodes in _SM["vallocs"]:
        by.setdefault(name, []).append((i, nslots, nodes))
    assign = {}
    for name, lst in by.items():
        if name not in SMART_SET:
            continue
        nslots = lst[0][1]
        res = [0] * len(lst)
        items = []
        for i, _, nodes in lst:
            if not nodes:
                items.append((0, 0.0, i))
            else:
                items.append((nodes[0].seg, min(n.start for n in nodes), i))
        items.sort()
        for rank, (seg, st, i) in enumerate(items):
            res[i] = rank % nslots
        assign[name] = res
    return assign


SMART_SET = ("ps",)
SMART_ITERS = 1


def build(nl=7, nm=8, dbg=None, stop=None):
    global VWIN
    if not SMART_RINGS:
        _SM["pass"] = 0
        last = None
        for w in (VWIN, 64, 48, 80, 40, 128, 32):
            VWIN = w
            try:
                return _build(nl, nm, dbg, stop)
            except RuntimeError as e_:
                if "scheduler deadlock" not in str(e_):
                    raise
                last = e_
        raise last
    _SM["assign"] = {}
    for _ in range(SMART_ITERS):
        _SM["pass"] = 1
        _SM["vallocs"] = []
        _build(nl, nm, dbg, stop)
        _SM["assign"] = _assign_slots()
    _SM["pass"] = 2
    try:
        return _build(nl, nm, dbg, stop)
    finally:
        _SM["pass"] = 0


def _build(nl=7, nm=8, dbg=None, stop=None):
    nblk = nl + 1 + nm
    nc = bass.Bass("TRN2", target_bir_lowering=False)
    dr = {}
    for name, shp in PARAM_SHAPES:
        dr[name] = nc.dram_tensor(name, shp, F32, kind="ExternalInput").ap()
    xs = nc.dram_tensor("xs", [nblk * TB, D], F32, kind="ExternalInput").ap()
    flag = nc.dram_tensor("flag", [128, 1], F32, kind="ExternalInput").ap()
    out = nc.dram_tensor("out", [nm * TB, D], F32, kind="ExternalOutput").ap()
    x1d = nc.dram_tensor("x1d", [128 + nm * TB, D], F32, kind="Internal").ap()
    dbg_t = {}
    if dbg:
        for name, shp in dbg.items():
            dt_ = F32
            if shp and shp[0] == "bf16":
                dt_ = BF16
                shp = shp[1:]
            dbg_t[name] = nc.dram_tensor("dbg_" + name, shp, dt_, kind="ExternalOutput").ap()

    es = ExitStack()
    with es:
        ARENA_F32 = 53000
        arena = es.enter_context(nc.sbuf_tensor("arena", [128, ARENA_F32], F32))
        psum_all = es.enter_context(nc.psum_tensor("psum_all", [128, 8, 512], F32))
        semkeys = ["pe", "act", "dve", "pool", "sp", "d_par", "d_x0", "d_x1", "d_x2", "d_x3",
                   "d_w0", "d_w1", "d_w2", "d_w3", "d_w4", "d_w5", "d_w6", "d_w7", "d_wo",
                   "d_st0", "d_st1", "d_st2", "d_st3", "d_dbg", "d_fn", "d_wdn0", "d_wdn1"] + [f"d_wup{i}" for i in range(NFC // 2)]
        sems = {k: es.enter_context(nc.semaphore(k)) for k in semkeys}
        block = es.enter_context(nc.Block())
        P = Prog(nc, sems, group_sems=("d_par", "d_w7", "d_dbg"))
        if PROFILE:
            P.profile = []

        st = {"off": 0}

        def carve(name, free_shape, dt, parts=128, sem=None):
            n = int(np.prod(free_shape))
            nf32 = (n + 1) // 2 if dt == BF16 else n
            nf32 = (nf32 + 3) // 4 * 4
            off = st["off"]
            st["off"] = off + nf32
            assert st["off"] <= ARENA_F32, (name, st["off"])
            v = arena[:, off:off + nf32]
            if dt == BF16:
                v = v.bitcast(BF16)[:, 0:n]
            else:
                v = v[:, 0:n]
            if len(free_shape) == 2:
                v = v.rearrange("p (a b) -> p a b", b=free_shape[1])
            elif len(free_shape) == 3:
                v = v.rearrange("p (a b c) -> p a b c", b=free_shape[1], c=free_shape[2])
            if parts != 128:
                v = v[0:parts]
            return Tl(v, name, sem)

        carve_off = {}

        def vring(name, n, free_shape, dt):
            tiles = []
            bases = []
            for i in range(n):
                bases.append(st["off"])
                tiles.append(carve(f"{name}{i}", free_shape, dt))
            return VRing(name, tiles, bases)

        def ring(name, n, free_shape, dt, sems_=None):
            return Ring([carve(f"{name}{i}", free_shape, dt, sem=(sems_[i] if sems_ else None)) for i in range(n)],
                        name=name)

        VS32 = 1 << 24
        ps_b0 = psum_all[:, 0, :]
        v_cnt = [0]
        v_info = {}
        P.class_slots["ps"] = 8

        def valloc(cname, slot0_ap, bases, name):
            k = v_cnt[0]
            v_cnt[0] += 1
            t = Tl(AP(slot0_ap.tensor, slot0_ap.offset + (k + 1) * VS32 * (4 // (2 if slot0_ap.dtype == BF16 else 4)),
                      slot0_ap.ap), f"{name}{k}")
            t.buf.valloc = k
            P.alloc_class[k] = cname
            v_info[k] = bases
            return t

        PS_BASES = [b * 512 for b in range(8 - PSH_BANKS)]
        PSH_BASES = [b * 512 + h * 256 for b in range(8 - PSH_BANKS, 8) for h in range(2)]
        P.class_slots["ps"] = 8 - PSH_BANKS
        if PSH_BANKS:
            P.class_slots["psh"] = 2 * PSH_BANKS
            psh_0 = psum_all[:, 8 - PSH_BANKS, 0:256]

        def psum(kind="full"):
            if kind == "half" and PSH_BANKS:
                return valloc("psh", psh_0, PSH_BASES, "psh")
            return valloc("ps", ps_b0, PS_BASES, "psv")

        class VRing:
            def __init__(self, name, tiles, bases):
                self.name = name
                self.t0 = tiles[0]
                self.bases = bases
                P.class_slots[name] = len(tiles)

            def next(self):
                return valloc(self.name, self.t0.ap, self.bases, self.name + "v")

        def RB(ap):
            if ap is None or not hasattr(ap, "name") or ap.name not in ("psum_all", "arena"):
                return ap
            mul = 2 if ap.dtype == BF16 else 1
            vs = VS32 * mul
            k = ap.offset // vs - 1
            if k < 0:
                return ap
            real = ap.offset % vs
            bases = v_info[k]
            return AP(ap.tensor, real + (bases[P.place[k]] - bases[0]) * mul, ap.ap)

        def fsz(ap):
            return int(np.prod(ap.shape[1:]))

        def d_act(o):
            return 220.0 + 0.72 * fsz(o)

        def d_dve(o, k=1.0):
            return 70.0 + 1.05 * k * fsz(o)

        def act(o, i, func, reads, writes, **kw):
            P.op("act", lambda e: e.activation(out=RB(o), in_=RB(i), func=func, **kw), reads, writes,
                 dur=d_act(o) + (60.0 if "accum_out" in kw else 0.0))

        def tt(eng, o, a, b, op, reads, writes):
            P.op(eng, lambda e: e.tensor_tensor(out=RB(o), in0=RB(a), in1=RB(b), op=op), reads, writes,
                 dur=(d_dve(o) if eng != "pool" else 150.0 + 2.3 * fsz(o)))

        def ts(eng, o, a, s1, s2, op0, op1, reads, writes):
            if s2 is None:
                P.op(eng, lambda e: e.tensor_scalar(out=RB(o), in0=RB(a), scalar1=s1, scalar2=None, op0=op0), reads, writes,
                     dur=d_dve(o))
            else:
                P.op(eng, lambda e: e.tensor_scalar(out=RB(o), in0=RB(a), scalar1=s1, scalar2=s2, op0=op0, op1=op1), reads, writes,
                     dur=d_dve(o))

        def stt(o, a, s, b, op0, op1, reads, writes):
            P.op("dve", lambda e: e.scalar_tensor_tensor(out=RB(o), in0=RB(a), scalar=s, in1=RB(b), op0=op0, op1=op1), reads, writes,
                 dur=d_dve(o))

        def cp(eng, o, i, reads, writes):
            if eng == "act":
                P.op("act", lambda e: e.activation(out=RB(o), in_=RB(i), func=AF.Identity), reads, writes, dur=d_act(o))
            else:
                P.op(eng, lambda e: e.tensor_copy(out=RB(o), in_=RB(i)), reads, writes, dur=d_dve(o))

        def mm(o, l, r, start, stop, reads, writes, inc=True):
            P.op("pe", lambda e: e.matmul(out=RB(o), lhsT=RB(l), rhs=RB(r), start=start, stop=stop), reads, writes,
                 dur=max(64, fsz(r)) * 0.45 + 12.0)

        def tr(o, i, ident, reads, writes, inc=True):
            P.op("pe", lambda e: e.transpose(out=RB(o), in_=RB(i), identity=ident), reads, writes, dur=70.0)

        def memset(eng, ap, val, writes):
            P.op(eng, lambda e: e.memset(ap, val), (), writes, dur=200.0 + fsz(ap))

        def asel(o, i, pattern, cmp, fill, base, cm, reads, writes):
            P.op("pool", lambda e: e.affine_select(out=o, in_=i, pattern=pattern, compare_op=cmp, fill=fill,
                                                   base=base, channel_multiplier=cm), reads, writes, dur=1000.0)

        def scan(o, d0, d1, init, op0, op1, reads, writes):
            P.op("dve", lambda e: e.tensor_tensor_scan(out=RB(o), data0=RB(d0), data1=RB(d1), initial=init, op0=op0, op1=op1),
                 reads, writes, dur=d_dve(o, 2.0))

        def ck(name):
            if stop == name:
                raise StopBuild()

        def dbg_dump(name, tl_ap, reads):
            if name in dbg_t:
                P.dma("sp", "d_dbg", dbg_t[name], tl_ap, reads=reads)

        dumped = set()

        def dd(name, tl, ap=None, when=True):
            if name in dbg_t and name not in dumped and when:
                dumped.add(name)
                P.dma("sp", "d_dbg", dbg_t[name], tl.ap if ap is None else ap, reads=[tl])

        identf = carve("identf", [128], F32)
        identb = carve("identb", [128], BF16)
        ones = carve("ones", [128], F32)
        zeros = carve("zeros", [128], F32)
        negh = carve("negh", [TB], F32)
        mUI = carve("mUI", [128], BF16)
        mRW = carve("mRW", [4, 128], BF16)
        mL = carve("mL", [128], BF16)
        blk = carve("blk", [128], F32)
        PT = carve("PT", [70], F32)
        CW = carve("CW", [4, NFC * 2], F32)
        DP = carve("DP", [56], F32)
        flg = carve("flg", [1], F32)
        S_hg = carve("S_hg", [4, 128], F32)
        Sb_hg = carve("Sb_hg", [4, 128], BF16)
        H_rw = carve("H_rw", [4, 64], F32)
        Hb_rw = carve("Hb_rw", [4, 64], BF16)
        glob_end = st["off"]

        memset("pool", identf.ap, 0.0, [identf])
        asel(identf.ap, identf.ap, [[-1, 128]], ALU.not_equal, 1.0, 0, 1, [identf], [identf])
        cp("dve", identb.ap, identf.ap, [identf], [identb])
        memset("pool", ones.ap, 1.0, [ones])
        memset("pool", zeros.ap, 0.0, [zeros])
        memset("pool", negh.ap, -0.5, [negh])
        memset("pool", S_hg.ap, 0.0, [S_hg])
        memset("pool", Sb_hg.ap, 0.0, [Sb_hg])
        memset("pool", H_rw.ap, 0.0, [H_rw])
        memset("pool", Hb_rw.ap, 0.0, [Hb_rw])
        tmpm = carve("tmpm", [128], F32)
        memset("pool", tmpm.ap, 1.0, [tmpm])
        asel(tmpm.ap, tmpm.ap, [[1, 128]], ALU.is_ge, 0.0, 0, -1, [tmpm], [tmpm])
        cp("dve", mRW.ap[:, 1, :], tmpm.ap, [tmpm], [mRW])
        cp("dve", mRW.ap[:, 3, :], tmpm.ap, [tmpm], [mRW])
        asel(tmpm.ap[:, 64:128], tmpm.ap[:, 64:128], [[0, 64]], ALU.is_ge, 0.0, -64, 1, [tmpm], [tmpm])
        cp("dve", mUI.ap, tmpm.ap, [tmpm], [mUI])
        memset("pool", tmpm.ap, 1.0, [tmpm])
        asel(tmpm.ap, tmpm.ap, [[1, 128]], ALU.is_gt, 0.0, 0, -1, [tmpm], [tmpm])
        cp("dve", mRW.ap[:, 0, :], tmpm.ap, [tmpm], [mRW])
        cp("dve", mRW.ap[:, 2, :], tmpm.ap, [tmpm], [mRW])
        memset("pool", tmpm.ap, 1.0, [tmpm])
        asel(tmpm.ap, tmpm.ap, [[-1, 128]], ALU.is_gt, 0.0, 0, 1, [tmpm], [tmpm])
        cp("dve", mL.ap, tmpm.ap, [tmpm], [mL])
        memset("pool", blk.ap, 1.0, [blk])
        asel(blk.ap[:, 0:64], blk.ap[:, 0:64], [[0, 64]], ALU.is_ge, 0.0, 63, -1, [blk], [blk])
        asel(blk.ap[:, 64:128], blk.ap[:, 64:128], [[0, 64]], ALU.is_ge, 0.0, -64, 1, [blk], [blk])

        PMa = carve("PMa", [128], F32)
        PMc = carve("PMc", [4, 128], F32)
        rows = [("norm1_w", None, 8), ("norm2_w", None, 8), ("hg_lb_logits", 0, 4), ("hg_lb_logits", 1, 4),
                ("hg_norm_w", None, 4), ("rw_shift_mu", None, 14), ("rw_w0", None, 4), ("rw_a0", None, 4),
                ("rw_k_k", None, 4), ("rw_k_a", None, 4), ("rw_r_k", None, 4), ("rw_ln_w", None, 4),
                ("rw_ln_b", None, 4)]
        r0 = 0
        for name, idx, n in rows:
            src = dr[name] if idx is None else dr[name][idx]
            P.dma("sp", "d_par", PMa.ap[r0:r0 + n, :], src.rearrange("(r p) -> r p", p=128), writes=[PMa])
            r0 += n
        assert r0 == 70
        for j in range(3):
            P.dma("sp", "d_par", PMc.ap[0:2 * NFC, j, :], dr["conv_w"][j].rearrange("(r p) -> r p", p=128), writes=[PMc])
        P.dma("sp", "d_par", PMc.ap[0:2 * NFC, 3, :], dr["conv_b"].rearrange("(r p) -> r p", p=128), writes=[PMc])
        P.dma("sp", "d_par", flg.ap, flag, writes=[flg])
        pp_ = psum()
        tr(pp_.ap[:, 0:70], PMa.ap[0:70, :], identf.ap[0:70, 0:70], [PMa, identf], [pp_])
        cp("dve", PT.ap, pp_.ap[:, 0:70], [pp_], [PT])
        pp_ = psum()
        for j in range(4):
            tr(pp_.ap[:, j * 44:(j + 1) * 44], PMc.ap[0:44, j, :], identf.ap[0:44, 0:44], [PMc, identf], [pp_], inc=(j == 3))
        cp("dve", CW.ap.rearrange("p a b -> p (a b)"), pp_.ap[:, 0:176], [pp_], [CW])
        nw1T = PT.ap[:, 0:8]
        nw2T = PT.ap[:, 8:16]
        hgnw = PT.ap[:, 24:28]
        muT = PT.ap[:, 28:42]
        w0T = PT.ap[:, 42:46]
        a0T = PT.ap[:, 46:50]
        kkT = PT.ap[:, 50:54]
        kaT = PT.ap[:, 54:58]
        rkT = PT.ap[:, 58:62]
        lnwT = PT.ap[:, 62:66]
        lnbT = PT.ap[:, 66:70]
        lbT = DP.ap[:, 0:4]
        lb1T = DP.ap[:, 4:8]
        cq2 = DP.ap[:, 8:12]
        hgnw2 = DP.ap[:, 12:16]
        hw0 = DP.ap[:, 16:20]
        ha0 = DP.ap[:, 20:24]
        hka = DP.ap[:, 24:28]
        nhka = DP.ap[:, 28:32]
        dtmp = DP.ap[:, 32:36]
        thl = DP.ap[:, 36:40]
        tt("dve", dtmp, PT.ap[:, 16:20], PT.ap[:, 20:24], ALU.subtract, [PT], [DP])
        act(thl, dtmp, AF.Exp, [DP], [DP], scale=-1.0)
        ts("dve", thl, thl, 1.0, None, ALU.add, None, [DP], [DP])
        P.op("dve", lambda e: e.reciprocal(out=lbT, in_=thl), [DP], [DP])
        ts("dve", lb1T, lbT, -1.0, 1.0, ALU.mult, ALU.add, [DP], [DP])
        ts("dve", cq2, lb1T, -(128 ** -0.5), None, ALU.mult, None, [DP], [DP])
        ts("dve", hw0, w0T, -1.0, None, ALU.mult, None, [PT], [DP])
        omT = DP.ap[:, 40:54]
        ts("dve", omT, muT, -1.0, 1.0, ALU.mult, ALU.add, [PT], [DP])
        ts("dve", ha0, a0T, -1.0, None, ALU.mult, None, [PT], [DP])
        if "PT" in dbg_t:
            dbg_dump("PT", PT.ap, [PT])

        if stop == "setup":
            P.barrier()
            P.replay(block)
            return nc, P
        P.barrier()
        st["off"] = glob_end
        w_in = carve("w_in", [8, 3840], BF16)
        w_out = carve("w_out", [8, 1024], BF16)
        w2b = carve("w2b", [512], BF16)
        a2b = carve("a2b", [512], BF16)
        g2b = carve("g2b", [512], BF16)
        wgroups = [(512, 1536), (0, 512), (1536, 2048), (3584, 3840), (2560, 3072), (3072, 3584), (2048, 2560)]
        wbufs = []
        wsrc = dr["w_in"].rearrange("(kc p) c -> p kc c", p=128)
        for gi, (c0, c1) in enumerate(wgroups):
            b = Buf(f"win{gi}")
            wbufs.append((c0, c1, b))
            P.dma("pool", f"d_w{gi}", w_in.ap[:, :, c0:c1], wsrc[:, :, c0:c1], writes=[b])
        P.dma("pool", "d_w7", w2b.ap[0:64, :], dr["rw_w2"], writes=[w2b])
        P.dma("pool", "d_w7", a2b.ap[64:128, :], dr["rw_a2"], writes=[a2b])
        P.dma("pool", "d_w7", g2b.ap, dr["rw_g2"], writes=[g2b])
        P.dma("pool", "d_wo", w_out.ap, dr["w_out"].rearrange("(kc p) c -> p kc c", p=128), writes=[w_out])

        def wbuf_of(c):
            col = c * 128
            for c0, c1, b in wbufs:
                if c0 <= col < c1:
                    return b
            raise AssertionError

        xring = ring("xt", 2, [D], F32, ["d_x0", "d_x1"])
        xnr = ring("xn", 2, [D], BF16)
        sm = ring("sm", 8, [16], F32)
        hT = carve("hT", [8, TB + 1], BF16)
        memset("pool", hT.ap, 0.0, [hT])
        tf = ring("tf", NTF, [TB + 4], F32)
        tb = ring("tb", 4, [TB], BF16)
        hg_kT = [carve(f"hg_kT{h}", [TB], BF16) for h in range(4)]
        hg_ktok = [carve(f"hg_ktok{h}", [NS, 128], BF16) for h in range(4)]
        hg_vtok = [carve(f"hg_vtok{h}", [NS, 128], BF16) for h in range(4)]
        hg_qT = [carve(f"hg_qT{h}", [TB], BF16) for h in range(4)]
        hg_sgT = [carve(f"hg_sgT{h}", [TB], BF16) for h in range(4)]
        hg_ebl = carve("hg_ebl", [4, TB // 64], F32)
        rw_kT = [carve(f"rw_kT{c}", [TB], BF16) for c in range(4)]
        rw_bT = [carve(f"rw_bT{c}", [TB], BF16) for c in range(4)]
        rw_ar = [carve(f"rw_ar{c}", [NS * 2, 2, 128], BF16) for c in range(4)]
        rw_ktok = [carve(f"rw_ktok{c}", [NS, 128], BF16) for c in range(4)]
        rw_btok = [carve(f"rw_btok{c}", [NS, 128], BF16) for c in range(4)]
        rw_vtok = [carve(f"rw_vtok{c}", [NS, 128], BF16) for c in range(4)]
        rw_gT = [carve(f"rw_gT{c}", [TB], BF16) for c in range(4)]
        rw_bon = [carve(f"rw_bon{c}", [TB], BF16) for c in range(4)]
        rw_egx = [carve(f"rw_egx{c}", [NS, 129], F32) for c in range(4)]
        twd = carve("twd", [TB], BF16)
        adT = carve("adT", [TB], BF16)
        sgd = carve("sgd", [TB], BF16)
        ATh = ring("ATh", 2, [4, 128], BF16)
        Pq = [carve(f"Pq{i}", [4, 128], BF16) for i in range(2)] * NS
        PTq = [carve(f"PTq{i}", [4, 128], BF16) for i in range(2)] * NS
        TTq = [carve(f"TTq{i}", [4, 128], BF16) for i in range(2)] * NS
        Rb = carve("Rb", [512], BF16)
        Ub = carve("Ub", [512], BF16)
        ysb = ring("ysb", 2, [512], F32)
        ysq = carve("ysq", [512], F32)
        ynb = ring("ynb", 2, [512], BF16)
        oT = ring("oT", 2, [8, 128], BF16)
        x1t = ring("x1t", 1, [D], F32, ["d_st0"])
        for c in range(4):
            memset("pool", rw_egx[c].ap, 1.0, [rw_egx[c]])
            memset("pool", rw_ar[c].ap, 0.0, [rw_ar[c]])
        ATb = [carve(f"ATb{c}", [4, 128], BF16) for c in range(4)]
        ATk = [carve(f"ATk{c}", [4, 128], BF16) for c in range(4)]

        def arv(c, s, j, w):
            return rw_ar[c].ap[:, s * 2 + j, w, :]

        def rsqrt_small(dst_ap, dst_reads, src_ap, n, scale, eps, reads, writes):
            t_ = sm.next()
            act(t_.ap[:, 0:n], src_ap, AF.Ln, reads, [t_], scale=scale, bias=eps)
            act(dst_ap, t_.ap[:, 0:n], AF.Exp, [t_] + list(dst_reads), writes, scale=-0.5)

        def sig3(dst_ap, dst_tl, src_ap, reads, bias=None):
            e_ = tf.next()
            shp = [src_ap.shape[0], int(np.prod(src_ap.shape[1:]))]
            ev = e_.ap[0:shp[0], 0:shp[1]]
            if bias is None:
                act(ev, src_ap, AF.Exp, reads, [e_], scale=-1.0)
            else:
                act(ev, src_ap, AF.Exp, list(reads) + [DP], [e_], scale=-1.0, bias=bias)
            act(ev, ev, AF.Ln, [e_], [e_], bias=1.0)
            act(dst_ap, ev, AF.Exp, [e_], [dst_tl], scale=-1.0)

        def load_norm_T(src_rows_ap, nwT, hT_tile, s, xt=None, src_reads=(), off=0):
            if xt is None:
                xt = xring.next()
                P.dma("sp", xt.sem, xt.ap, src_rows_ap, reads=src_reads, writes=[xt])
            ss = sm.next()
            xn = xnr.next()
            act(xn.ap, xt.ap, AF.Square, [xt], [xn, ss], accum_out=ss.ap[:, 0:1])
            rs = sm.next()
            rsqrt_small(rs.ap[:, 0:1], [], ss.ap[:, 0:1], 1, 1.0 / D, 1e-6, [ss], [rs])
            ts("dve", xn.ap, xt.ap, rs.ap[:, 0:1], None, ALU.mult, None, [xt, rs], [xn])
            dd('ss', ss, ap=ss.ap[:, 0:1]); dd('rs', rs, ap=rs.ap[:, 0:1]); dd('xn', xn); dd('xt', xt)
            pb = psum()
            pbv = pb.ap.bitcast(BF16).rearrange("p (a b) -> p a b", b=128)
            for kc in range(8):
                tr(pbv[:, kc, :], xn.ap[:, kc * 128:(kc + 1) * 128], identb.ap, [xn, identb], [pb], inc=(kc == 7))
            tt("dve", hT_tile.ap[:, :, off + s * 128:off + (s + 1) * 128], pbv, nwT.unsqueeze(2).to_broadcast([128, 8, 128]),
               ALU.mult, [pb, PT], [hT_tile])
            return xt

        def proj(c, halo=False):
            pb = psum()
            wb = wbuf_of(c)
            for kc in range(8):
                if halo:
                    mm(pb.ap[:, 0:TB + 1], w_in.ap[:, kc, c * 128:(c + 1) * 128], hT.ap[:, kc, :], kc == 0, kc == 7,
                       [wb, hT], [pb], inc=(kc == 7))
                else:
                    mm(pb.ap[:, 0:TB], w_in.ap[:, kc, c * 128:(c + 1) * 128], hT.ap[:, kc, 1:TB + 1], kc == 0, kc == 7,
                       [wb, hT], [pb], inc=(kc == 7))
            return pb

        def to_tok(srcT, dst, eng="act"):
            pb = psum("half")
            pbv = pb.ap.bitcast(BF16)[:, 0:NS * 128].rearrange("p (a b) -> p a b", b=128)
            for s in range(NS):
                tr(pbv[:, s, :], srcT.ap[:, s * 128:(s + 1) * 128], identb.ap, [srcT, identb], [pb], inc=(s == NS - 1))
            cp(eng, dst.ap, pbv, [pb], [dst])


        def mixer_block(bi, full, emit_sub0):
            cp("dve", hT.ap[:, :, 0:1], hT.ap[:, :, TB:TB + 1], [hT], [hT])
            for s in range(NS):
                g = bi * NS + s
                load_norm_T(xs[g * 128:(g + 1) * 128, :], nw1T, hT, s, off=1)
            dd('hT', hT, when=(bi == nl))
            ck('A')
            for h in range(4):
                pf = proj(4 + h)
                thf = tf.next()
                sig3(thf.ap[:, 0:TB], thf, pf.ap[:, 0:TB], [pf])
                f_ = tf.next()
                ts("dve", f_.ap[:, 0:TB], thf.ap[:, 0:TB], lb1T[:, h:h + 1], lbT[:, h:h + 1], ALU.mult, ALU.add,
                   [thf, DP], [f_])
                eb = tf.next()
                for c4 in range(TB // 64):
                    sl = slice(c4 * 64, (c4 + 1) * 64)
                    scan(eb.ap[:, sl], f_.ap[:, sl], ones.ap[:, 0:64], 1.0, ALU.mult, ALU.mult, [f_, ones], [eb])
                enb = tf.next()
                P.op("dve", lambda e, o=enb.ap[:, 0:TB], i=eb.ap[:, 0:TB]: e.reciprocal(out=RB(o), in_=RB(i)), [eb], [enb], dur=340.0)
                stt(hg_kT[h].ap, thf.ap[:, 0:TB], 1.0, enb.ap[:, 0:TB], ALU.subtract, ALU.mult, [thf, enb], [hg_kT[h]])
                cp("act", hg_ebl.ap[:, h, :], eb.ap[:, 63:TB:64], [eb], [hg_ebl])
                kh = tb.next()
                tt(POOL_E, kh.ap.rearrange("p (a b) -> p a b", b=64), hg_kT[h].ap.rearrange("p (a b) -> p a b", b=64),
                   hg_ebl.ap[:, h, :].unsqueeze(2).to_broadcast([128, TB // 64, 64]), ALU.mult,
                   [hg_kT[h], hg_ebl], [kh])
                to_tok(kh, hg_ktok[h])
                pi = proj(8 + h)
                vT = tb.next()
                cp("act", vT.ap, pi.ap[:, 0:TB], [pi], [vT])
                to_tok(vT, hg_vtok[h])
                if full:
                    pq = proj(h)
                    thq = tf.next()
                    sig3(thq.ap[:, 0:TB], thq, pq.ap[:, 0:TB], [pq])
                    s1 = tf.next()
                    tt("dve", s1.ap[:, 0:TB], thq.ap[:, 0:TB], pq.ap[:, 0:TB], ALU.mult, [thq, pq], [s1])
                    stt(hg_qT[h].ap, s1.ap[:, 0:TB], cq2[:, h:h + 1], eb.ap[:, 0:TB], ALU.mult, ALU.mult,
                        [s1, eb, DP], [hg_qT[h]])
                    pg = proj(12 + h)
                    thg = tf.next()
                    sig3(thg.ap[:, 0:TB], thg, pg.ap[:, 0:TB], [pg])
                    tt("dve", hg_sgT[h].ap, thg.ap[:, 0:TB], pg.ap[:, 0:TB], ALU.mult, [thg, pg], [hg_sgT[h]])

            dd('hg_kT0', hg_kT[0], when=(bi == nl)); dd('hg_qT0', hg_qT[0], when=(bi == nl)); dd('hg_vtok0', hg_vtok[0], when=(bi == nl)); dd('hg_ktok0', hg_ktok[0], when=(bi == nl)); dd('hg_sgT0', hg_sgT[0], when=(bi == nl)); dd('hg_ebl', hg_ebl, when=(bi == nl))
            ck('hgprep')
            def shift_mix(pb, ci, out_ap, out_tl, rows=slice(0, 128)):
                a1_ = tf.next()
                act(a1_.ap[:, 0:TB], pb.ap[:, 1:TB + 1], AF.Identity, [pb, DP], [a1_], scale=omT[:, ci:ci + 1])
                stt(out_ap, pb.ap[rows, 0:TB], muT[rows, ci:ci + 1], a1_.ap[rows, 0:TB], ALU.mult, ALU.add,
                    [pb, a1_, PT], [out_tl])

            pl = proj(28, halo=True)
            lo = tf.next()
            shift_mix(pl, 12, lo.ap[:, 0:TB], lo)
            e_ = tf.next()
            act(e_.ap[0:64, 0:TB], lo.ap[0:64, 0:TB], AF.Exp, [lo], [e_], scale=-2.0)
            act(e_.ap[0:64, 0:TB], e_.ap[0:64, 0:TB], AF.Ln, [e_], [e_], bias=1.0)
            act(e_.ap[0:64, 0:TB], e_.ap[0:64, 0:TB], AF.Exp, [e_], [e_], scale=-1.0)
            ts("dve", twd.ap[0:64, :], e_.ap[0:64, 0:TB], 2.0, -1.0, ALU.mult, ALU.add, [e_], [twd])
            cp("act", adT.ap[64:128, :], lo.ap[64:128, 0:TB], [lo], [adT])
            if full:
                pg_ = proj(29, halo=True)
                gdm = tf.next()
                shift_mix(pg_, 13, gdm.ap[:, 0:TB], gdm)
                sig3(sgd.ap, sgd, gdm.ap[:, 0:TB], [gdm])
            for c in range(4):
                cs = slice(c * 128, (c + 1) * 128)
                pw = psum("half")
                mm(pw.ap[:, 0:TB], w2b.ap[0:64, cs], twd.ap[0:64, :], True, True, [w2b, twd], [pw])
                ld = tf.next()
                sig3(ld.ap[:, 0:TB], ld, pw.ap[:, 0:TB], [pw], bias=hw0[:, c:c + 1])
                cld = float(np.exp(-0.5))
                lg = tf.next()
                for s in range(NS):
                    sl = slice(s * 128, (s + 1) * 128)
                    scan(lg.ap[:, sl], ld.ap[:, sl], zeros.ap, 0.0, ALU.add, ALU.add, [ld, zeros], [lg])
                act(rw_egx[c].ap[:, :, 1:129], lg.ap[:, 0:TB].rearrange("p (a b) -> p a b", b=128), AF.Exp,
                    [lg], [rw_egx[c]], scale=-cld)
                eng_t = tf.next()
                eng_ap = eng_t.ap[:, 0:TB]
                act(eng_ap, lg.ap[:, 0:TB], AF.Exp, [lg], [eng_t], scale=cld)
                pa = psum("half")
                mm(pa.ap[:, 0:TB], a2b.ap[64:128, cs], adT.ap[64:128, :], True, True, [a2b, adT], [pa])
                tha_t = tf.next()
                tha_ap = tha_t.ap[:, 0:TB]
                sig3(tha_ap, tha_t, pa.ap[:, 0:TB], [pa], bias=ha0[:, c:c + 1])
                if full:
                    pgm = psum("half")
                    mm(pgm.ap[:, 0:TB], g2b.ap[:, cs], sgd.ap, True, True, [g2b, sgd], [pgm])
                    cp("act", rw_gT[c].ap, pgm.ap[:, 0:TB], [pgm], [rw_gT[c]])
                pk = proj(20 + c, halo=True)
                kr = tf.next()
                shift_mix(pk, 4 + c, kr.ap[:, 0:TB], kr)
                kk = tf.next()
                ts("dve", kk.ap[:, 0:TB], kr.ap[:, 0:TB], kkT[:, c:c + 1], None, ALU.mult, None, [kr, PT], [kk])
                sq = tf.next()
                act(sq.ap[:, 0:TB], kk.ap[:, 0:TB], AF.Square, [kk], [sq])
                pss = psum("half")
                mm(pss.ap[:, 0:TB], blk.ap, sq.ap[:, 0:TB], True, True, [blk, sq], [pss])
                mx = tf.next()
                rn = tf.next()
                act(mx.ap[:, 0:TB], pss.ap[:, 0:TB], AF.Ln, [pss], [mx], bias=float(2.0 ** -60))
                act(rn.ap[:, 0:TB], mx.ap[:, 0:TB], AF.Exp, [mx], [rn], scale=-0.5)
                kkn = tf.next()
                tt(POOL_E, kkn.ap[:, 0:TB], kk.ap[:, 0:TB], rn.ap[:, 0:TB], ALU.mult, [kk, rn], [kkn])
                t1 = tf.next()
                ts("dve", t1.ap[:, 0:TB], tha_ap, 1.0, kaT[:, c:c + 1], ALU.subtract, ALU.mult,
                   [tha_t, PT], [t1])
                kp = tf.next()
                stt(kp.ap[:, 0:TB], t1.ap[:, 0:TB], 1.0, kr.ap[:, 0:TB], ALU.add, ALU.mult, [t1, kr], [kp])
                tt(POOL_E, rw_kT[c].ap, kp.ap[:, 0:TB], eng_ap, ALU.mult, [kp, eng_t], [rw_kT[c]])
                b1 = tf.next()
                tt(POOL_E, b1.ap[:, 0:TB], tha_ap, kkn.ap[:, 0:TB], ALU.mult, [tha_t, kkn], [b1])
                tt(POOL_E, rw_bT[c].ap, b1.ap[:, 0:TB], eng_ap, ALU.mult, [b1, eng_t], [rw_bT[c]])
                for j in range(2):
                    pj = slice(64 * j, 64 * j + 64)
                    stt(rw_ar[c].ap[pj, j:2 * NS:2, 0, :], kkn.ap[pj, 0:TB].rearrange("p (a b) -> p a b", b=128), -1.0,
                        rw_egx[c].ap[pj, :, 0:128], ALU.mult, ALU.mult, [kkn, rw_egx[c]], [rw_ar[c]])
                to_tok(rw_kT[c], rw_ktok[c])
                to_tok(rw_bT[c], rw_btok[c])
                pv = proj(24 + c, halo=True)
                vr = tf.next()
                shift_mix(pv, 8 + c, vr.ap[:, 0:TB], vr)
                vTb = tb.next()
                cp("act", vTb.ap, vr.ap[:, 0:TB], [vr], [vTb])
                to_tok(vTb, rw_vtok[c])
                if full:
                    pr = proj(16 + c, halo=True)
                    rr = tf.next()
                    shift_mix(pr, c, rr.ap[:, 0:TB], rr)
                    for j in range(2):
                        pj = slice(64 * j, 64 * j + 64)
                        tt(POOL_E, rw_ar[c].ap[pj, j:2 * NS:2, 1, :], rr.ap[pj, 0:TB].rearrange("p (a b) -> p a b", b=128),
                           rw_egx[c].ap[pj, :, 1:129], ALU.mult, [rr, rw_egx[c]], [rw_ar[c]])
                    rkr = tf.next()
                    stt(rkr.ap[:, 0:TB], rr.ap[:, 0:TB], rkT[:, c:c + 1], kp.ap[:, 0:TB], ALU.mult, ALU.mult,
                        [rr, kp, PT], [rkr])
                    pbs = psum("half")
                    mm(pbs.ap[:, 0:TB], blk.ap, rkr.ap[:, 0:TB], True, True, [blk, rkr], [pbs])
                    tt("dve", rw_bon[c].ap, pbs.ap[:, 0:TB], vr.ap[:, 0:TB], ALU.mult, [pbs, vr], [rw_bon[c]])

            dd('rw_kT0', rw_kT[0], when=(bi == nl)); dd('rw_bT0', rw_bT[0], when=(bi == nl)); dd('rw_ar0', rw_ar[0], when=(bi == nl)); dd('rw_egx0', rw_egx[0], when=(bi == nl)); dd('rw_vtok0', rw_vtok[0], when=(bi == nl)); dd('rw_gT0', rw_gT[0], when=(bi == nl)); dd('rw_bon0', rw_bon[0], when=(bi == nl));
            ck('rwprep')
            for s in range(NS):
                ssl = slice(s * 128, (s + 1) * 128)
                emit = full and (s >= emit_sub0)
                if emit:
                    pa = psum()
                    for h in range(4):
                        mm(pa.ap[:, h * 128:(h + 1) * 128], hg_kT[h].ap[:, ssl], hg_qT[h].ap[:, ssl], True, True,
                           [hg_kT[h], hg_qT[h]], [pa], inc=(h == 3))
                    ath = ATh.next()
                    tt("dve", ath.ap, pa.ap.rearrange("p (a b) -> p a b", b=128),
                       mUI.ap.unsqueeze(1).to_broadcast([128, 4, 128]), ALU.mult, [pa, mUI], [ath])
                    po = psum()
                for cc in range(2):
                    ps_ = slice(64 * cc, 64 * cc + 64)
                    ci = s * 2 + cc
                    if emit:
                        for h in range(4):
                            hs = slice(h * 128, (h + 1) * 128)
                            mm(po.ap[ps_, hs], hg_qT[h].ap[:, s * 128 + 64 * cc:s * 128 + 64 * cc + 64],
                               Sb_hg.ap[:, h, :], True, False, [hg_qT[h], Sb_hg], [po], inc=False)
                            mm(po.ap[ps_, hs], ath.ap[:, h, 64 * cc:64 * cc + 64], hg_vtok[h].ap[:, s, :],
                               False, True, [ath, hg_vtok[h]], [po], inc=(h == 3))
                    pS = psum()
                    for h in range(4):
                        hs = slice(h * 128, (h + 1) * 128)
                        mm(pS.ap[:, hs], hg_ktok[h].ap[ps_, s, :], hg_vtok[h].ap[ps_, s, :], True, True,
                           [hg_ktok[h], hg_vtok[h]], [pS], inc=(h == 3))
                    for h in range(4):
                        hs = slice(h * 128, (h + 1) * 128)
                        stt(S_hg.ap[:, h, :], S_hg.ap[:, h, :], hg_ebl.ap[:, h, ci:ci + 1], pS.ap[:, hs],
                            ALU.mult, ALU.add, [S_hg, hg_ebl, pS], [S_hg])
                    cp("act", Sb_hg.ap, S_hg.ap, [S_hg], [Sb_hg])
                if emit:
                    s2 = sm.next()
                    jk = ysb.next()
                    for h in range(4):
                        act(jk.ap[:, h * 128:(h + 1) * 128], po.ap[:, h * 128:(h + 1) * 128], AF.Square, [po], [jk, s2],
                            accum_out=s2.ap[:, h:h + 1])
                    rs = sm.next()
                    rsqrt_small(rs.ap[:, 0:4], [], s2.ap[:, 0:4], 4, 1.0 / 128, 1e-6, [s2], [rs])
                    on = ynb.next()
                    tt("dve", on.ap.rearrange("p (a b) -> p a b", b=128), po.ap.rearrange("p (a b) -> p a b", b=128),
                       rs.ap[:, 0:4].unsqueeze(2).to_broadcast([128, 4, 128]), ALU.mult, [po, rs], [on])
                    pt_ = psum("half")
                    ptv = pt_.ap.bitcast(BF16)[:, 0:512].rearrange("p (a b) -> p a b", b=128)
                    for h in range(4):
                        tr(ptv[:, h, :], on.ap[:, h * 128:(h + 1) * 128], identb.ap, [on, identb], [pt_], inc=(h == 3))
                    ot = oT.next()
                    for h in range(4):
                        stt(ot.ap[:, h, :], ptv[:, h, :], hgnw[:, h:h + 1], hg_sgT[h].ap[:, ssl], ALU.mult, ALU.mult,
                            [pt_, DP, hg_sgT[h]], [ot])
                if emit:
                    dd('on', on); dd('ot_hg', ot, ap=ot.ap[:, 0:4, :])
                dd('S_hg', S_hg, when=(bi == nl and s == NS - 1))
                ck('hgchain')
                for c in range(4):
                    for (lt, dst) in ((rw_bT[c], ATb[c]), (rw_kT[c], ATk[c])):
                        pA = psum()
                        if emit:
                            mm(pA.ap, lt.ap[:, ssl], rw_ar[c].ap[:, 2 * s:2 * s + 2, :, :], True, True, [lt, rw_ar[c]], [pA])
                            tt("dve", dst.ap, pA.ap.rearrange("p (a b) -> p a b", b=128), mRW.ap, ALU.mult, [pA, mRW], [dst])
                        else:
                            mm(pA.ap[:, 0:256], lt.ap[:, ssl], rw_ar[c].ap[:, 2 * s:2 * s + 2, 0, :], True, True,
                               [lt, rw_ar[c]], [pA])
                            tt("dve", dst.ap[:, 0:3:2, :], pA.ap[:, 0:256].rearrange("p (a b) -> p a b", b=128),
                               mRW.ap[:, 0:3:2, :], ALU.mult, [pA, mRW], [dst])
                for g4 in range(2):
                    pN = psum()
                    for hh in range(4):
                        h = g4 * 4 + hh
                        c, j = h // 2, h % 2
                        pj = slice(64 * j, 64 * j + 64)
                        mm(pN.ap[:, hh * 128:(hh + 1) * 128], arv(c, s, j, 0), rw_bT[c].ap[:, ssl], True, True,
                           [rw_ar[c], rw_bT[c]], [pN], inc=(hh == 3))
                    Pc = Pq[s * 2 + g4]
                    tt("dve", Pc.ap, pN.ap.rearrange("p (a b) -> p a b", b=128),
                       mL.ap.unsqueeze(1).to_broadcast([128, 4, 128]), ALU.mult, [pN, mL], [Pc])
                    TTc = TTq[s * 2 + g4]
                    for cc_ in range(2):
                        c_ = g4 * 2 + cc_
                        tt(POOL_E, TTc.ap[:, 2 * cc_:2 * cc_ + 2, :], ATb[c_].ap[:, 0:3:2, :],
                           identb.ap.unsqueeze(1).to_broadcast([128, 2, 128]), ALU.add, [ATb[c_], identb], [TTc])
                    PTc = None
                    nlev = 6
                    for lev in range(1, nlev + 1):
                        last = (lev == nlev)
                        pP = psum()
                        def ptv_(hh):
                            if PTc is None:
                                h = g4 * 4 + hh
                                return ATb[h // 2].ap[:, 2 * (h % 2), :], ATb[h // 2]
                            return PTc.ap[:, hh, :], PTc
                        for hh in range(4):
                            pa_, pt_l = ptv_(hh)
                            mm(pP.ap[:, hh * 128:(hh + 1) * 128], pa_, Pc.ap[:, hh, :], True, True,
                               [pt_l, Pc], [pP], inc=(hh == 3))
                        if not last:
                            pPT = psum()
                            for hh in range(4):
                                pa_, pt_l = ptv_(hh)
                                mm(pPT.ap[:, hh * 128:(hh + 1) * 128], Pc.ap[:, hh, :], pa_, True, True,
                                   [pt_l, Pc], [pPT], inc=(hh == 3))
                        Pn = Pc
                        cp("act", Pn.ap, pP.ap.rearrange("p (a b) -> p a b", b=128), [pP], [Pn])
                        if not last:
                            PTn = PTq[s * 2 + g4]
                            cp("act", PTn.ap, pPT.ap.rearrange("p (a b) -> p a b", b=128), [pPT], [PTn])
                        pT2 = psum()
                        for hh in range(4):
                            mm(pT2.ap[:, hh * 128:(hh + 1) * 128], Pn.ap[:, hh, :], TTc.ap[:, hh, :], True, True,
                               [Pn, TTc], [pT2], inc=(hh == 3))
                        TTn = TTc
                        tt("dve", TTn.ap, pT2.ap.rearrange("p (a b) -> p a b", b=128), TTc.ap, ALU.add, [pT2, TTc], [TTn])
                        Pc = Pn
                        if not last:
                            PTc = PTn
                        TTc = TTn
                dd('ATb0', ATb[0], when=(bi == nl and s == NS - 1)); dd('ATk0', ATk[0], when=(bi == nl and s == NS - 1))
                ck('rwinv')
                pR = psum()
                for h in range(8):
                    c, j = h // 2, h % 2
                    pj = slice(64 * j, 64 * j + 64)
                    hs = slice(h * 64, (h + 1) * 64)
                    mm(pR.ap[:, hs], arv(c, s, j, 0), Hb_rw.ap[:, c, :], True, False, [rw_ar[c], Hb_rw], [pR],
                       inc=False)
                    mm(pR.ap[:, hs], ATk[c].ap[:, 2 * j, :], rw_vtok[c].ap[:, s, 64 * j:64 * j + 64], False, True,
                       [ATk[c], rw_vtok[c]], [pR], inc=(h == 7))
                cp("act", Rb.ap, pR.ap, [pR], [Rb])
                pU = psum()
                for h in range(8):
                    hs = slice(h * 64, (h + 1) * 64)
                    mm(pU.ap[:, hs], TTq[s * 2 + h // 4].ap[:, h % 4, :], Rb.ap[:, hs], True, True, [TTq[s * 2 + h // 4], Rb], [pU],
                       inc=(h == 7))
                cp("act", Ub.ap, pU.ap, [pU], [Ub])
                if emit:
                    pY = psum()
                    for h in range(8):
                        c, j = h // 2, h % 2
                        pj = slice(64 * j, 64 * j + 64)
                        hs = slice(h * 64, (h + 1) * 64)
                        mm(pY.ap[:, hs], arv(c, s, j, 1), Hb_rw.ap[:, c, :], True, False, [rw_ar[c], Hb_rw],
                           [pY], inc=False)
                        mm(pY.ap[:, hs], ATb[c].ap[:, 2 * j + 1, :], Ub.ap[:, hs], False, False, [ATb[c], Ub], [pY], inc=False)
                        mm(pY.ap[:, hs], ATk[c].ap[:, 2 * j + 1, :], rw_vtok[c].ap[:, s, 64 * j:64 * j + 64], False, True,
                           [ATk[c], rw_vtok[c]], [pY], inc=(h == 7))
                pH = psum()
                for c in range(4):
                    cs_ = slice(c * 128, (c + 1) * 128)
                    mm(pH.ap[:, cs_], rw_btok[c].ap[:, s, :], Ub.ap[:, cs_], True, False, [rw_btok[c], Ub], [pH], inc=False)
                    mm(pH.ap[:, cs_], rw_ktok[c].ap[:, s, :], rw_vtok[c].ap[:, s, :], False, True,
                       [rw_ktok[c], rw_vtok[c]], [pH], inc=(c == 3))
                ht = tf.next()
                htv = ht.ap[:, 0:256].rearrange("p (a b) -> p a b", b=64)
                for j in range(2):
                    pj = slice(64 * j, 64 * j + 64)
                    tt("dve", htv[pj], pH.ap[pj, :].rearrange("p (c x) -> p c x", x=128)[:, :, 64 * j:64 * j + 64],
                       H_rw.ap[pj], ALU.add, [pH, H_rw], [ht])
                for c in range(4):
                    ts("dve", H_rw.ap[:, c, :], htv[:, c, :], rw_egx[c].ap[:, s, 128:129], None, ALU.mult, None,
                       [ht, rw_egx[c]], [H_rw])
                cp("act", Hb_rw.ap, H_rw.ap, [H_rw], [Hb_rw])
                if emit:
                    dd('Ub', Ub); dd('Rb', Rb); dd('H_rw', H_rw)
                    ck('rwchain')
                    yb = ysb.next()
                    cp("act", yb.ap, pY.ap, [pY], [yb])
                    yv = yb.ap.rearrange("p (a b) -> p a b", b=64)
                    s1_ = sm.next()
                    P.op("dve", lambda e, o=s1_.ap[:, 0:8], i=yv: e.tensor_reduce(out=o, in_=i, axis=AX.X, op=ALU.add),
                         [yb], [s1_], dur=600.0)
                    act(ysq.ap, yb.ap, AF.Square, [yb], [ysq])
                    s2_ = sm.next()
                    P.op("dve", lambda e, o=s2_.ap[:, 0:8], i=ysq.ap.rearrange("p (a b) -> p a b", b=64):
                         e.tensor_reduce(out=o, in_=i, axis=AX.X, op=ALU.add), [ysq], [s2_], dur=600.0)
                    mean = sm.next()
                    ts("dve", mean.ap[:, 0:8], s1_.ap[:, 0:8], 1.0 / 64, None, ALU.mult, None, [s1_], [mean])
                    msq = sm.next()
                    tt("dve", msq.ap[:, 0:8], mean.ap[:, 0:8], mean.ap[:, 0:8], ALU.mult, [mean], [msq])
                    var = sm.next()
                    stt(var.ap[:, 0:8], s2_.ap[:, 0:8], 1.0 / 64, msq.ap[:, 0:8], ALU.mult, ALU.subtract, [s2_, msq], [var])
                    rs_ = sm.next()
                    rsqrt_small(rs_.ap[:, 0:8], [], var.ap[:, 0:8], 8, 1.0, 64e-5, [var], [rs_])
                    ycen = ysb.next()
                    tt("dve", ycen.ap.rearrange("p (a b) -> p a b", b=64), yv,
                       mean.ap[:, 0:8].unsqueeze(2).to_broadcast([128, 8, 64]), ALU.subtract, [yb, mean], [ycen])
                    yn = ynb.next()
                    tt("dve", yn.ap.rearrange("p (a b) -> p a b", b=64), ycen.ap.rearrange("p (a b) -> p a b", b=64),
                       rs_.ap[:, 0:8].unsqueeze(2).to_broadcast([128, 8, 64]), ALU.mult, [ycen, rs_], [yn])
                    pt2 = psum("half")
                    ptv2 = pt2.ap.bitcast(BF16)[:, 0:512].rearrange("p (a b) -> p a b", b=128)
                    for c in range(4):
                        tr(ptv2[:, c, :], yn.ap[:, c * 128:(c + 1) * 128], identb.ap, [yn, identb], [pt2], inc=(c == 3))
                    for c in range(4):
                        t_ = tf.next()
                        stt(t_.ap[:, 0:128], ptv2[:, c, :], lnwT[:, c:c + 1], rw_bon[c].ap[:, ssl], ALU.mult, ALU.add,
                            [pt2, PT, rw_bon[c]], [t_])
                        stt(ot.ap[:, 4 + c, :], t_.ap[:, 0:128], lnbT[:, c:c + 1], rw_gT[c].ap[:, ssl], ALU.add, ALU.mult,
                            [t_, PT, rw_gT[c]], [ot])
                    dd('yn', yn); dd('ot', ot)
                    ck('rwout')
                    g = bi * NS + s
                    po1 = psum()
                    po2 = psum()
                    for kc in range(8):
                        mm(po1.ap, ot.ap[:, kc, :], w_out.ap[:, kc, 0:512], kc == 0, kc == 7, [ot, w_out], [po1], inc=False)
                    for kc in range(8):
                        mm(po2.ap, ot.ap[:, kc, :], w_out.ap[:, kc, 512:1024], kc == 0, kc == 7, [ot, w_out], [po2],
                           inc=(kc == 7))
                    xo = x1t.next()
                    P.dma("sp", "d_st1", xo.ap, xs[g * 128:(g + 1) * 128, :], writes=[xo])
                    tt("dve", xo.ap[:, 0:512], po1.ap, xo.ap[:, 0:512], ALU.add, [po1, xo], [xo])
                    tt("dve", xo.ap[:, 512:1024], po2.ap, xo.ap[:, 512:1024], ALU.add, [po2, xo], [xo])
                    row = (g - (nl + 1) * NS + 1) * 128
                    P.dma("sp", xo.sem, x1d[row:row + 128, :], xo.ap, reads=[xo], writes=[x1d_bufs[row // 128]])

        x1d_bufs = [Buf(f"x1d{i}") for i in range(1 + nm * NS)]
        print("arena phase1 used f32 words", st["off"], "of", ARENA_F32)
        try:
            for bi in range(nblk):
                if bi < nl:
                    mixer_block(bi, False, NS)
                elif bi == nl:
                    mixer_block(bi, True, NS - 1)
                else:
                    mixer_block(bi, True, 0)
        except StopBuild:
            P.barrier()
            P.replay(block)
            return nc, P
        if "x1" in dbg_t:
            P.barrier()
            P.dma("sp", "d_dbg", dbg_t["x1"], x1d[:, :], reads=x1d_bufs)

        NPRE = 7
        st_save = st["off"]
        st["off"] = glob_end
        w_up = carve("w_up", [NFC // 2, 8, 512], BF16)
        assert NPRE * 8 * 512 // 2 <= 8 * 3840 // 2
        st["off"] = st_save
        wup_src = dr["w_up"].rearrange("(kc p) c -> p kc c", p=128)
        wupb = [Buf(f"wup{gi}") for gi in range(NFC // 2)]

        def load_wup(gi, extra_reads=()):
            c0 = gi * 256
            hb_ = [Buf(f"wup{gi}a"), Buf(f"wup{gi}b")]
            P.dma("pool", f"d_wup{gi}", w_up.ap[:, gi, :, 0:256], wup_src[:, :, c0:c0 + 256],
                  reads=list(extra_reads), writes=[hb_[0]])
            P.dma("pool", f"d_wup{gi}", w_up.ap[:, gi, :, 256:512], wup_src[:, :, DFF + c0:DFF + c0 + 256],
                  reads=list(extra_reads), writes=[hb_[1]])
            wupb[gi] = hb_

        if stop is None:
            fence = sm.next()
            memset("pool", fence.ap[:, 0:1], 0.0, [fence] + [b for (_, _, b) in wbufs])
            for gi in range(NPRE):
                load_wup(gi, [fence])

        P.barrier()
        if stop == "p1":
            P.replay(block)
            return nc, P
        st["off"] = glob_end
        w_up = carve("w_up", [NFC // 2, 8, 512], BF16)
        w_dn = carve("w_dn", [NFC, D], BF16)
        xring2 = ring("x2t", NS + 1, [D], F32, ["d_x0", "d_x1", "d_x2", "d_x3"][:NS + 1])
        xnr = ring("xn2_", 2, [D], BF16)
        sm = ring("sm2_", 8, [16], F32)
        h2Ts = [carve(f"h2T{i}", [8, TB + 2], BF16) for i in range(2)]
        gT = carve("gT", [NFC, TB], BF16)
        acg = ring("acg", 4, [TB], F32)
        acv = ring("acv", 4, [TB], F32)
        sgl = ring("sgl", 4, [TB], F32)
        xo2 = ring("xo2", 1, [D], F32)
        fnwb = carve("fnwb", [D], F32)
        P.dma("sp", "d_fn", fnwb.ap, dr["final_norm_w"].partition_broadcast(128), writes=[fnwb])
        yo = ring("yo", 2, [D], F32, ["d_st2", "d_st3"])
        xring = xring2
        for gi in range(NPRE, NFC // 2):
            load_wup(gi)
        wdn_src = dr["w_down"].rearrange("(c p) n -> p c n", p=128)
        wdnb = []
        for gi in range(2):
            b = Buf(f"wdn{gi}")
            wdnb.append(b)
            P.dma("pool", f"d_wdn{gi}", w_dn.ap[:, gi * 11:(gi + 1) * 11, :], wdn_src[:, gi * 11:(gi + 1) * 11, :], writes=[b])
        print("arena phase2 used f32 words", st["off"], "of", ARENA_F32)
        cw0 = CW.ap[:, 0, :]
        cw1 = CW.ap[:, 1, :]
        cw2 = CW.ap[:, 2, :]
        cbT = CW.ap[:, 3, :]

        xt0 = xring.next()
        P.dma("sp", xt0.sem, xt0.ap, x1d[0:128, :], reads=[x1d_bufs[0]], writes=[xt0])
        load_norm_T(None, nw2T, h2Ts[1], NS - 1, xt=xt0, off=2)
        ts("dve", h2Ts[1].ap[:, :, TB:TB + 2], h2Ts[1].ap[:, :, TB:TB + 2], flg.ap[:, 0:1], None, ALU.mult, None,
           [h2Ts[1], flg], [h2Ts[1]])

        for m in range(nm):
            x1ts = []
            h2T = h2Ts[m % 2]
            cp("dve", h2T.ap[:, :, 0:2], h2Ts[(m + 1) % 2].ap[:, :, TB:TB + 2], [h2Ts[(m + 1) % 2]], [h2T])
            for s in range(NS):
                row = 128 + (m * NS + s) * 128
                xt = xring.next()
                P.dma("sp", xt.sem, xt.ap, x1d[row:row + 128, :], reads=[x1d_bufs[row // 128]], writes=[xt])
                load_norm_T(None, nw2T, h2T, s, xt=xt, off=2)
                x1ts.append(xt)
            for c in range(NFC):
                res = []
                for (cc, ac_ring) in ((c, acg), (NFC + c, acv)):
                    pb = psum()
                    for kc in range(8):
                        lc = (0 if cc < NFC else 256) + (c % 2) * 128
                        mm(pb.ap[:, 0:TB + 2], w_up.ap[:, c // 2, kc, lc:lc + 128], h2T.ap[:, kc, :], kc == 0, kc == 7,
                           wupb[c // 2] + [h2T], [pb], inc=(kc == 7))
                    ac = ac_ring.next()
                    act(ac.ap, pb.ap[:, 2:TB + 2], AF.Identity, [pb, CW], [ac], scale=cw2[:, cc:cc + 1], bias=cbT[:, cc:cc + 1])
                    stt(ac.ap, pb.ap[:, 1:TB + 1], cw1[:, cc:cc + 1], ac.ap, ALU.mult, ALU.add, [pb, ac, CW], [ac])
                    stt(ac.ap, pb.ap[:, 0:TB], cw0[:, cc:cc + 1], ac.ap, ALU.mult, ALU.add, [pb, ac, CW], [ac])
                    res.append(ac)
                sg_ = sgl.next()
                act(sg_.ap, res[0].ap, AF.Silu, [res[0]], [sg_])
                tt("dve", gT.ap[:, c, :], sg_.ap, res[1].ap, ALU.mult, [sg_, res[1]], [gT])
            for s in range(NS):
                ssl = slice(s * 128, (s + 1) * 128)
                po1 = psum()
                po2 = psum()
                for c in range(NFC):
                    mm(po1.ap, gT.ap[:, c, ssl], w_dn.ap[:, c, 0:512], c == 0, c == NFC - 1, [gT, wdnb[c // 11]], [po1],
                       inc=False)
                for c in range(NFC):
                    mm(po2.ap, gT.ap[:, c, ssl], w_dn.ap[:, c, 512:1024], c == 0, c == NFC - 1, [gT, wdnb[c // 11]], [po2],
                       inc=(c == NFC - 1))
                xo = xo2.next()
                xt = x1ts[s]
                tt("dve", xo.ap[:, 0:512], po1.ap, xt.ap[:, 0:512], ALU.add, [po1, xt], [xo])
                tt("dve", xo.ap[:, 512:1024], po2.ap, xt.ap[:, 512:1024], ALU.add, [po2, xt], [xo])
                ss = sm.next()
                jk = xnr.next()
                act(jk.ap, xo.ap, AF.Square, [xo], [jk, ss], accum_out=ss.ap[:, 0:1])
                rs = sm.next()
                rsqrt_small(rs.ap[:, 0:1], [], ss.ap[:, 0:1], 1, 1.0 / D, 1e-6, [ss], [rs])
                y_ = yo.next()
                stt(y_.ap, xo.ap, rs.ap[:, 0:1], fnwb.ap, ALU.mult, ALU.mult, [xo, rs, fnwb], [y_])
                row = (m * NS + s) * 128
                P.dma("sp", y_.sem, out[row:row + 128, :], y_.ap, reads=[y_])
        P.barrier()
        P.replay(block)
    return nc, P


def _prep_inputs(inputs):
    sq = {}
    for name, shp in PARAM_SHAPES:
        a = np.asarray(inputs[name], dtype=np.float32)
        sq[name] = np.ascontiguousarray(a.reshape(shp))
    return sq


def kernel(**inputs):
    x = np.asarray(inputs["x"], dtype=np.float32)
    B, T, Dm = x.shape
    half = T // 2
    params = _prep_inputs(inputs)
    nl = half // TB - 1
    nm = half // TB
    nc, _ = build(nl, nm)
    in_maps = []
    for c in range(8):
        b, j = c // 2, c % 2
        xs = np.zeros((2 * half, Dm), np.float32)
        if j == 1:
            xs[:half] = x[b, :half]
        xs[half:] = x[b, j * half:(j + 1) * half]
        m = dict(params)
        m["xs"] = xs
        m["flag"] = np.full((128, 1), float(j), np.float32)
        in_maps.append(m)
    res = run_bass_kernel_spmd(nc, in_maps, core_ids=list(range(8)))
    outp = np.empty((B, T, Dm), np.float32)
    for c in range(8):
        b, j = c // 2, c % 2
        outp[b, j * half:(j + 1) * half] = res.results[c]["out"]
    return outp
```

```python
import numpy as np
from contextlib import ExitStack
import concourse.bass as bass
import concourse.mybir as mybir
from concourse.bass_utils import run_bass_kernel_spmd
from concourse.ap import AP

F32 = mybir.dt.float32
BF16 = mybir.dt.bfloat16
AF = mybir.ActivationFunctionType
ALU = mybir.AluOpType
AX = mybir.AxisListType

SELF_SYNC = True
PROFILE = False
NTF = 27
POOL_E = "dve"
PSH_BANKS = 0
VWIN = 100
TB = 256
NS = TB // 128
D = 1024
DFF = 2816
NFC = DFF // 128


class StopBuild(Exception):
    pass


class Buf:
    __slots__ = ("name", "w", "r", "nodes", "valloc")

    def __init__(self, name):
        self.name = name
        self.w = None
        self.r = []
        self.valloc = None
        self.nodes = None


_SM = {"pass": 0, "vallocs": [], "assign": {}}
SMART_RINGS = False


class Tl:
    __slots__ = ("ap", "buf", "sem")

    def __init__(self, ap, name, sem=None):
        self.ap = ap
        self.buf = Buf(name)
        self.sem = sem


class Ring:
    def __init__(self, tiles, name=None):
        self.t = tiles
        self.i = 0
        self.name = name

    def next(self):
        i = self.i
        self.i += 1
        if self.name is not None and len(self.t) > 1:
            if _SM["pass"] == 1:
                prev = _SM["assign"].get(self.name)
                t = self.t[prev[i]] if prev is not None else self.t[i % len(self.t)]
                t.buf.nodes = []
                _SM["vallocs"].append((self.name, i, len(self.t), t.buf.nodes))
                return t
            if _SM["pass"] == 2 and self.name in _SM["assign"]:
                return self.t[_SM["assign"][self.name][i]]
        return self.t[i % len(self.t)]


def _bufs(xs):
    out = []
    for x in xs:
        if x is None:
            continue
        out.append(x.buf if isinstance(x, Tl) else x)
    return out


class Node:
    __slots__ = ("id", "eng", "fn", "preds", "succs", "dur", "kind", "semkey", "lat", "prio", "start", "fin",
                 "ev", "nready", "tag", "seg", "vallocs")

    def __init__(self, id_, eng, fn, dur, kind, semkey, lat):
        self.id = id_
        self.eng = eng
        self.fn = fn
        self.preds = set()
        self.succs = []
        self.dur = dur
        self.kind = kind
        self.semkey = semkey
        self.lat = lat
        self.prio = 0.0
        self.ev = None


class Prog:
    ENG = ("pe", "act", "dve", "pool", "sp")
    XLAT = 120.0

    def __init__(self, nc, sems, group_sems=()):
        self.nc = nc
        self.q = {e: [] for e in self.ENG}
        self.sem = sems
        self.group = set(group_sems)
        self.cnt = {k: 0 for k in sems}
        self.known = {e: {} for e in self.ENG}
        self.snap = {}
        self.nodes = []
        self.seg = 0
        self.valloc_nodes = {}
        self.place = {}
        self.alloc_class = {}
        self.class_slots = {}
        self.nwait = 0
        self.nins = {e: 0 for e in self.ENG}
        self.sim_end = 0.0
        self.sim_total = 0.0
        self.profile = None

    def _add(self, e, fn, reads, writes, dur, kind, semkey, lat):
        n = Node(len(self.nodes), e, fn, dur, kind, semkey, lat)
        n.seg = self.seg
        n.vallocs = None
        for b in list(reads) + list(writes):
            if b.nodes is not None:
                b.nodes.append(n)
            if b.valloc is not None:
                if n.vallocs is None:
                    n.vallocs = []
                if b.valloc not in n.vallocs:
                    n.vallocs.append(b.valloc)
                    self.valloc_nodes.setdefault(b.valloc, []).append(n)
        if self.profile is not None:
            import sys as _sys
            f = _sys._getframe(3)
            n.tag = f.f_lineno if f.f_code.co_name not in ("act", "tt", "ts", "stt", "cp", "mm", "tr", "scan") else _sys._getframe(4).f_lineno
        sg = self.seg
        for b in reads:
            if b.w is not None and b.w[0] == sg:
                n.preds.add(b.w[1])
        for b in writes:
            if b.w is not None and b.w[0] == sg:
                n.preds.add(b.w[1])
            for r in b.r:
                if r[0] == sg:
                    n.preds.add(r[1])
        n.preds.discard(n.id)
        me = (sg, n.id)
        for b in writes:
            b.w = me
            b.r = []
        for b in reads:
            if not b.r or b.r[-1] != me:
                b.r.append(me)
        self.nodes.append(n)
        return n

    def op(self, e, fn, reads=(), writes=(), inc=True, dur=100.0):
        self._add(e, fn, _bufs(reads), _bufs(writes), dur, "op", None, dur if e != "pe" else 60.0)

    def dma(self, e, semkey, out, in_, reads=(), writes=(), nbytes=65536, **kw):
        nbytes = int(np.prod(out.shape)) * 4
        lat = 2000.0 + nbytes / 150.0
        self._add(e, lambda eng: eng.dma_start(out=out, in_=in_, **kw), _bufs(reads), _bufs(writes), 60.0, "dma",
                  semkey, lat)

    def _schedule(self):
        import heapq
        nodes = self.nodes
        for n in nodes:
            for p in n.preds:
                nodes[p].succs.append(n.id)
        for n in reversed(nodes):
            best = 0.0
            for s_ in n.succs:
                sn = nodes[s_]
                lat = 0.0 if (n.eng == "pe" and sn.eng == "pe") else (n.lat + self.XLAT)
                v = sn.prio + lat
                if v > best:
                    best = v
            n.prio = n.dur + best
        free = {e: 0.0 for e in self.ENG}
        waiting = {e: [] for e in self.ENG}
        avail = {e: [] for e in self.ENG}
        blocked = []
        ready_t = [0.0] * len(nodes)
        npred = [len(n.preds) for n in nodes]
        for n in nodes:
            if npred[n.id] == 0:
                heapq.heappush(waiting[n.eng], (0.0, n.id))
        vnodes = self.valloc_nodes
        remaining_v = {k: len(v) for k, v in vnodes.items()}
        cls_of = self.alloc_class
        place = self.place
        owner = {c: [None] * n_ for c, n_ in self.class_slots.items()}
        free_t = {c: [0.0] * n_ for c, n_ in self.class_slots.items()}
        vorder = {c: [] for c in self.class_slots}
        for k in sorted(vnodes.keys()):
            vorder[cls_of[k]].append(k)
        optr = {c: 0 for c in self.class_slots}
        rptr = {c: 0 for c in self.class_slots}
        vfin = {k: 0.0 for k in vnodes}

        placed_flag = [False]

        def try_alloc(n):
            need_ = [k for k in n.vallocs if k not in place]
            if not need_:
                return True
            plan = []
            used = {}
            for k in need_:
                c = cls_of[k]
                vo = vorder[c]
                while optr[c] < len(vo) and vo[optr[c]] in place:
                    optr[c] += 1
                is_oldest = optr[c] < len(vo) and vo[optr[c]] == k
                ro = rptr[c]
                while ro < len(vo) and remaining_v[vo[ro]] == 0:
                    ro += 1
                rptr[c] = ro
                if ro < len(vo) and k > vo[ro] + VWIN:
                    return False
                ow = owner[c]
                freeb = [b for b in range(len(ow)) if (ow[b] is None or remaining_v[ow[b]] == 0)
                         and b not in used.get(c, ())]
                if not freeb or (len(freeb) < 2 and not is_oldest):
                    return False
                b = min(freeb, key=lambda b_: free_t[c][b_])
                used.setdefault(c, set()).add(b)
                plan.append((k, c, b))
            for k, c, b in plan:
                prev = owner[c][b]
                if prev is not None:
                    for p in vnodes[prev]:
                        if p.id not in n.preds:
                            n.preds.add(p.id)
                            p.succs.append(n.id)
                        lat = 0.0 if (p.eng == "pe" and n.eng == "pe") else (p.lat + self.XLAT)
                        rt = p.fin + lat
                        if rt > ready_t[n.id]:
                            ready_t[n.id] = rt
                owner[c][b] = k
                place[k] = b
            placed_flag[0] = True
            return True

        order = []
        remaining = len(nodes)
        while remaining:
            cand = []
            for e in self.ENG:
                w, a = waiting[e], avail[e]
                while w and w[0][0] <= free[e]:
                    t_, i_ = heapq.heappop(w)
                    heapq.heappush(a, (-nodes[i_].prio, i_))
                if a:
                    cand.append((free[e], e))
                elif w:
                    cand.append((w[0][0], e))
            cand.sort()
            picked = None
            for t, e in cand:
                tmp = []
                while True:
                    if avail[e]:
                        item = heapq.heappop(avail[e])
                        i_ = item[1]
                    elif waiting[e]:
                        item = heapq.heappop(waiting[e])
                        i_ = item[1]
                    else:
                        break
                    n = nodes[i_]
                    if n.vallocs is not None and not try_alloc(n):
                        blocked.append(i_)
                        continue
                    picked = n
                    break
                if picked is not None:
                    break
            if picked is None:
                raise RuntimeError("scheduler deadlock: all candidates blocked on resource slots")
            n = picked
            e = n.eng
            i_ = n.id
            start = max(free[e], ready_t[i_])
            n.start = start
            n.fin = start + n.dur
            free[e] = n.fin
            order.append(n)
            remaining -= 1
            if n.vallocs is not None:
                released = placed_flag[0]
                placed_flag[0] = False
                for k in n.vallocs:
                    remaining_v[k] -= 1
                    if n.fin > vfin[k]:
                        vfin[k] = n.fin
                    if remaining_v[k] == 0:
                        free_t[cls_of[k]][place[k]] = vfin[k]
                        released = True
                if released and blocked:
                    for j_ in blocked:
                        heapq.heappush(waiting[nodes[j_].eng], (ready_t[j_], j_))
                    blocked = []
            for s_ in n.succs:
                sn = nodes[s_]
                lat = 0.0 if (n.eng == "pe" and sn.eng == "pe") else (n.lat + self.XLAT)
                rt = start + lat if n.kind == "dma" else n.fin + (0.0 if lat == 0.0 else lat)
                if rt > ready_t[s_]:
                    ready_t[s_] = rt
                npred[s_] -= 1
                if npred[s_] == 0:
                    heapq.heappush(waiting[sn.eng], (ready_t[s_], s_))
        self.sim_end = max(free.values()) if nodes else 0.0
        if self.profile is not None:
            self.profile.append((self.sim_end, [(n.eng, n.tag, n.dur, n.start) for n in nodes]))
        return order

    def _merge(self, e, k, v):
        kn = self.known[e]
        if kn.get(k, 0) >= v:
            return False
        kn[k] = v
        s = self.snap.get((k, v))
        if s:
            for k2, v2 in s.items():
                if kn.get(k2, 0) < v2:
                    kn[k2] = v2
        return True

    def flush(self):
        nodes = self.nodes
        if not nodes:
            return
        order = self._schedule()
        self.sim_total += self.sim_end
        gtot = {}
        for n in nodes:
            if n.kind == "dma" and n.semkey in self.group:
                gtot[n.semkey] = gtot.get(n.semkey, self.cnt[n.semkey]) + 16
        need = [False] * len(nodes)
        for n in nodes:
            if n.kind == "dma":
                need[n.id] = True
                continue
            for s_ in n.succs:
                sn = nodes[s_]
                if not (n.eng == "pe" and sn.eng == "pe"):
                    if sn.eng != n.eng or SELF_SYNC:
                        need[n.id] = True
                        break
        sems = self.sem
        for n in order:
            e = n.eng
            evs = {}
            for p in n.preds:
                pn = nodes[p]
                if pn.eng == "pe" and e == "pe" and pn.kind == "op":
                    continue
                if pn.kind == "op" and pn.eng == e and not SELF_SYNC:
                    continue
                if pn.kind == "dma" and n.kind == "dma" and pn.semkey == n.semkey and n.semkey in self.group:
                    continue
                k, v = pn.ev
                if evs.get(k, 0) < v:
                    evs[k] = v
            waits = []
            for k, v in evs.items():
                if self._merge(e, k, v):
                    waits.append((k, v))
            if n.kind == "dma":
                k = n.semkey
                self.cnt[k] += 16
                n.ev = (k, gtot[k]) if k in self.group else (k, self.cnt[k])
                amount, semkey = 16, k
                if k not in self.group:
                    self.snap[n.ev] = dict(self.known[e])
            elif need[n.id]:
                self.cnt[e] += 1
                n.ev = (e, self.cnt[e])
                self.snap[n.ev] = dict(self.known[e])
                amount, semkey = 1, e
            else:
                n.ev = (e, self.cnt[e] + 1)
                amount, semkey = 0, e
            self.nwait += len(waits)
            self.nins[e] += 1

            def run(eng, waits=waits, fn=n.fn, amount=amount, semkey=semkey):
                for k, v in waits:
                    eng.wait_ge(sems[k], v)
                ins = fn(eng)
                if amount:
                    ins.then_inc(sems[semkey], amount)
            self.q[e].append(run)
        self.nodes = []
        self.valloc_nodes = {}
        self.seg += 1

    def barrier(self):
        self.flush()
        for e in self.ENG:
            waits = []
            for k, v in self.cnt.items():
                if v > 0 and k != e and self.known[e].get(k, 0) < v:
                    self.known[e][k] = v
                    waits.append((k, v))
            sems = self.sem

            def run(eng, waits=waits):
                for k, v in waits:
                    eng.wait_ge(sems[k], v)
            self.q[e].append(run)

    def replay(self, block):
        self.flush()
        if _SM["pass"] == 1:
            return
        q = self.q

        @block.tensor
        def _(eng):
            for f in q["pe"]:
                f(eng)

        @block.scalar
        def _(eng):
            for f in q["act"]:
                f(eng)

        @block.vector
        def _(eng):
            for f in q["dve"]:
                f(eng)

        @block.gpsimd
        def _(eng):
            for f in q["pool"]:
                f(eng)

        @block.sync
        def _(eng):
            for f in q["sp"]:
                f(eng)


PARAM_SHAPES = [
    ("norm1_w", [1024]), ("w_in", [1024, 3840]), ("hg_lb_logits", [2, 512]), ("hg_norm_w", [512]),
    ("rw_shift_mu", [1792]), ("rw_w0", [512]), ("rw_w2", [64, 512]), ("rw_a0", [512]),
    ("rw_a2", [64, 512]), ("rw_g2", [128, 512]), ("rw_k_k", [512]), ("rw_k_a", [512]),
    ("rw_r_k", [512]), ("rw_ln_w", [512]), ("rw_ln_b", [512]), ("w_out", [1024, 1024]),
    ("norm2_w", [1024]), ("w_up", [1024, 5632]), ("conv_w", [3, 5632]), ("conv_b", [5632]),
    ("w_down", [2816, 1024]), ("final_norm_w", [1024]),
]


def _assign_slots():
    by = {}
    for name, i, nslots, nodes in _SM["vallocs"]:
        by.setdefault(name, []).append((i, nslots, nodes))
    assign = {}
    for name, lst in by.items():
        if name not in SMART_SET:
            continue
        nslots = lst[0][1]
        res = [0] * len(lst)
        items = []
        for i, _, nodes in lst:
            if not nodes:
                items.append((0, 0.0, i))
            else:
                items.append((nodes[0].seg, min(n.start for n in nodes), i))
        items.sort()
        for rank, (seg, st, i) in enumerate(items):
            res[i] = rank % nslots
        assign[name] = res
    return assign


SMART_SET = ("ps",)
SMART_ITERS = 1


def build(nl=7, nm=8, dbg=None, stop=None):
    global VWIN
    if not SMART_RINGS:
        _SM["pass"] = 0
        last = None
        for w in (VWIN, 64, 48, 80, 40, 128, 32):
            VWIN = w
            try:
                return _build(nl, nm, dbg, stop)
            except RuntimeError as e_:
                if "scheduler deadlock" not in str(e_):
                    raise
                last = e_
        raise last
    _SM["assign"] = {}
    for _ in range(SMART_ITERS):
        _SM["pass"] = 1
        _SM["vallocs"] = []
        _build(nl, nm, dbg, stop)
        _SM["assign"] = _assign_slots()
    _SM["pass"] = 2
    try:
        return _build(nl, nm, dbg, stop)
    finally:
        _SM["pass"] = 0


def _build(nl=7, nm=8, dbg=None, stop=None):
    nblk = nl + 1 + nm
    nc = bass.Bass("TRN2", target_bir_lowering=False)
    dr = {}
    for name, shp in PARAM_SHAPES:
        dr[name] = nc.dram_tensor(name, shp, F32, kind="ExternalInput").ap()
    xs = nc.dram_tensor("xs", [nblk * TB, D], F32, kind="ExternalInput").ap()
    flag = nc.dram_tensor("flag", [128, 1], F32, kind="ExternalInput").ap()
    out = nc.dram_tensor("out", [nm * TB, D], F32, kind="ExternalOutput").ap()
    x1d = nc.dram_tensor("x1d", [128 + nm * TB, D], F32, kind="Internal").ap()
    dbg_t = {}
    if dbg:
        for name, shp in dbg.items():
            dt_ = F32
            if shp and shp[0] == "bf16":
                dt_ = BF16
                shp = shp[1:]
            dbg_t[name] = nc.dram_tensor("dbg_" + name, shp, dt_, kind="ExternalOutput").ap()

    es = ExitStack()
    with es:
        ARENA_F32 = 53000
        arena = es.enter_context(nc.sbuf_tensor("arena", [128, ARENA_F32], F32))
        psum_all = es.enter_context(nc.psum_tensor("psum_all", [128, 8, 512], F32))
        semkeys = ["pe", "act", "dve", "pool", "sp", "d_par", "d_x0", "d_x1", "d_x2", "d_x3",
                   "d_w0", "d_w1", "d_w2", "d_w3", "d_w4", "d_w5", "d_w6", "d_w7", "d_wo",
                   "d_st0", "d_st1", "d_st2", "d_st3", "d_dbg", "d_fn", "d_wdn0", "d_wdn1"] + [f"d_wup{i}" for i in range(NFC // 2)]
        sems = {k: es.enter_context(nc.semaphore(k)) for k in semkeys}
        block = es.enter_context(nc.Block())
        P = Prog(nc, sems, group_sems=("d_par", "d_w7", "d_dbg"))
        if PROFILE:
            P.profile = []

        st = {"off": 0}

        def carve(name, free_shape, dt, parts=128, sem=None):
            n = int(np.prod(free_shape))
            nf32 = (n + 1) // 2 if dt == BF16 else n
            nf32 = (nf32 + 3) // 4 * 4
            off = st["off"]
            st["off"] = off + nf32
            assert st["off"] <= ARENA_F32, (name, st["off"])
            v = arena[:, off:off + nf32]
            if dt == BF16:
                v = v.bitcast(BF16)[:, 0:n]
            else:
                v = v[:, 0:n]
            if len(free_shape) == 2:
                v = v.rearrange("p (a b) -> p a b", b=free_shape[1])
            elif len(free_shape) == 3:
                v = v.rearrange("p (a b c) -> p a b c", b=free_shape[1], c=free_shape[2])
            if parts != 128:
                v = v[0:parts]
            return Tl(v, name, sem)

        carve_off = {}

        def vring(name, n, free_shape, dt):
            tiles = []
            bases = []
            for i in range(n):
                bases.append(st["off"])
                tiles.append(carve(f"{name}{i}", free_shape, dt))
            return VRing(name, tiles, bases)

        def ring(name, n, free_shape, dt, sems_=None):
            return Ring([carve(f"{name}{i}", free_shape, dt, sem=(sems_[i] if sems_ else None)) for i in range(n)],
                        name=name)

        VS32 = 1 << 24
        ps_b0 = psum_all[:, 0, :]
        v_cnt = [0]
        v_info = {}
        P.class_slots["ps"] = 8

        def valloc(cname, slot0_ap, bases, name):
            k = v_cnt[0]
            v_cnt[0] += 1
            t = Tl(AP(slot0_ap.tensor, slot0_ap.offset + (k + 1) * VS32 * (4 // (2 if slot0_ap.dtype == BF16 else 4)),
                      slot0_ap.ap), f"{name}{k}")
            t.buf.valloc = k
            P.alloc_class[k] = cname
            v_info[k] = bases
            return t

        PS_BASES = [b * 512 for b in range(8 - PSH_BANKS)]
        PSH_BASES = [b * 512 + h * 256 for b in range(8 - PSH_BANKS, 8) for h in range(2)]
        P.class_slots["ps"] = 8 - PSH_BANKS
        if PSH_BANKS:
            P.class_slots["psh"] = 2 * PSH_BANKS
            psh_0 = psum_all[:, 8 - PSH_BANKS, 0:256]

        def psum(kind="full"):
            if kind == "half" and PSH_BANKS:
                return valloc("psh", psh_0, PSH_BASES, "psh")
            return valloc("ps", ps_b0, PS_BASES, "psv")

        class VRing:
            def __init__(self, name, tiles, bases):
                self.name = name
                self.t0 = tiles[0]
                self.bases = bases
                P.class_slots[name] = len(tiles)

            def next(self):
                return valloc(self.name, self.t0.ap, self.bases, self.name + "v")

        def RB(ap):
            if ap is None or not hasattr(ap, "name") or ap.name not in ("psum_all", "arena"):
                return ap
            mul = 2 if ap.dtype == BF16 else 1
            vs = VS32 * mul
            k = ap.offset // vs - 1
            if k < 0:
                return ap
            real = ap.offset % vs
            bases = v_info[k]
            return AP(ap.tensor, real + (bases[P.place[k]] - bases[0]) * mul, ap.ap)

        def fsz(ap):
            return int(np.prod(ap.shape[1:]))

        def d_act(o):
            return 220.0 + 0.72 * fsz(o)

        def d_dve(o, k=1.0):
            return 70.0 + 1.05 * k * fsz(o)

        def act(o, i, func, reads, writes, **kw):
            P.op("act", lambda e: e.activation(out=RB(o), in_=RB(i), func=func, **kw), reads, writes,
                 dur=d_act(o) + (60.0 if "accum_out" in kw else 0.0))

        def tt(eng, o, a, b, op, reads, writes):
            P.op(eng, lambda e: e.tensor_tensor(out=RB(o), in0=RB(a), in1=RB(b), op=op), reads, writes,
                 dur=(d_dve(o) if eng != "pool" else 150.0 + 2.3 * fsz(o)))

        def ts(eng, o, a, s1, s2, op0, op1, reads, writes):
            if s2 is None:
                P.op(eng, lambda e: e.tensor_scalar(out=RB(o), in0=RB(a), scalar1=s1, scalar2=None, op0=op0), reads, writes,
                     dur=d_dve(o))
            else:
                P.op(eng, lambda e: e.tensor_scalar(out=RB(o), in0=RB(a), scalar1=s1, scalar2=s2, op0=op0, op1=op1), reads, writes,
                     dur=d_dve(o))

        def stt(o, a, s, b, op0, op1, reads, writes):
            P.op("dve", lambda e: e.scalar_tensor_tensor(out=RB(o), in0=RB(a), scalar=s, in1=RB(b), op0=op0, op1=op1), reads, writes,
                 dur=d_dve(o))

        def cp(eng, o, i, reads, writes):
            if eng == "act":
                P.op("act", lambda e: e.activation(out=RB(o), in_=RB(i), func=AF.Identity), reads, writes, dur=d_act(o))
            else:
                P.op(eng, lambda e: e.tensor_copy(out=RB(o), in_=RB(i)), reads, writes, dur=d_dve(o))

        def mm(o, l, r, start, stop, reads, writes, inc=True):
            P.op("pe", lambda e: e.matmul(out=RB(o), lhsT=RB(l), rhs=RB(r), start=start, stop=stop), reads, writes,
                 dur=max(64, fsz(r)) * 0.45 + 12.0)

        def tr(o, i, ident, reads, writes, inc=True):
            P.op("pe", lambda e: e.transpose(out=RB(o), in_=RB(i), identity=ident), reads, writes, dur=70.0)

        def memset(eng, ap, val, writes):
            P.op(eng, lambda e: e.memset(ap, val), (), writes, dur=200.0 + fsz(ap))

        def asel(o, i, pattern, cmp, fill, base, cm, reads, writes):
            P.op("pool", lambda e: e.affine_select(out=o, in_=i, pattern=pattern, compare_op=cmp, fill=fill,
                                                   base=base, channel_multiplier=cm), reads, writes, dur=1000.0)

        def scan(o, d0, d1, init, op0, op1, reads, writes):
            P.op("dve", lambda e: e.tensor_tensor_scan(out=RB(o), data0=RB(d0), data1=RB(d1), initial=init, op0=op0, op1=op1),
                 reads, writes, dur=d_dve(o, 2.0))

        def ck(name):
            if stop == name:
                raise StopBuild()

        def dbg_dump(name, tl_ap, reads):
            if name in dbg_t:
                P.dma("sp", "d_dbg", dbg_t[name], tl_ap, reads=reads)

        dumped = set()

        def dd(name, tl, ap=None, when=True):
            if name in dbg_t and name not in dumped and when:
                dumped.add(name)
                P.dma("sp", "d_dbg", dbg_t[name], tl.ap if ap is None else ap, reads=[tl])

        identf = carve("identf", [128], F32)
        identb = carve("identb", [128], BF16)
        ones = carve("ones", [128], F32)
        zeros = carve("zeros", [128], F32)
        negh = carve("negh", [TB], F32)
        mUI = carve("mUI", [128], BF16)
        mRW = carve("mRW", [4, 128], BF16)
        mL = carve("mL", [128], BF16)
        blk = carve("blk", [128], F32)
        PT = carve("PT", [70], F32)
        CW = carve("CW", [4, NFC * 2], F32)
        DP = carve("DP", [56], F32)
        flg = carve("flg", [1], F32)
        S_hg = carve("S_hg", [4, 128], F32)
        Sb_hg = carve("Sb_hg", [4, 128], BF16)
        H_rw = carve("H_rw", [4, 64], F32)
        Hb_rw = carve("Hb_rw", [4, 64], BF16)
        glob_end = st["off"]

        memset("pool", identf.ap, 0.0, [identf])
        asel(identf.ap, identf.ap, [[-1, 128]], ALU.not_equal, 1.0, 0, 1, [identf], [identf])
        cp("dve", identb.ap, identf.ap, [identf], [identb])
        memset("pool", ones.ap, 1.0, [ones])
        memset("pool", zeros.ap, 0.0, [zeros])
        memset("pool", negh.ap, -0.5, [negh])
        memset("pool", S_hg.ap, 0.0, [S_hg])
        memset("pool", Sb_hg.ap, 0.0, [Sb_hg])
        memset("pool", H_rw.ap, 0.0, [H_rw])
        memset("pool", Hb_rw.ap, 0.0, [Hb_rw])
        tmpm = carve("tmpm", [128], F32)
        memset("pool", tmpm.ap, 1.0, [tmpm])
        asel(tmpm.ap, tmpm.ap, [[1, 128]], ALU.is_ge, 0.0, 0, -1, [tmpm], [tmpm])
        cp("dve", mRW.ap[:, 1, :], tmpm.ap, [tmpm], [mRW])
        cp("dve", mRW.ap[:, 3, :], tmpm.ap, [tmpm], [mRW])
        asel(tmpm.ap[:, 64:128], tmpm.ap[:, 64:128], [[0, 64]], ALU.is_ge, 0.0, -64, 1, [tmpm], [tmpm])
        cp("dve", mUI.ap, tmpm.ap, [tmpm], [mUI])
        memset("pool", tmpm.ap, 1.0, [tmpm])
        asel(tmpm.ap, tmpm.ap, [[1, 128]], ALU.is_gt, 0.0, 0, -1, [tmpm], [tmpm])
        cp("dve", mRW.ap[:, 0, :], tmpm.ap, [tmpm], [mRW])
        cp("dve", mRW.ap[:, 2, :], tmpm.ap, [tmpm], [mRW])
        memset("pool", tmpm.ap, 1.0, [tmpm])
        asel(tmpm.ap, tmpm.ap, [[-1, 128]], ALU.is_gt, 0.0, 0, 1, [tmpm], [tmpm])
        cp("dve", mL.ap, tmpm.ap, [tmpm], [mL])
        memset("pool", blk.ap, 1.0, [blk])
        asel(blk.ap[:, 0:64], blk.ap[:, 0:64], [[0, 64]], ALU.is_ge, 0.0, 63, -1, [blk], [blk])
        asel(blk.ap[:, 64:128], blk.ap[:, 64:128], [[0, 64]], ALU.is_ge, 0.0, -64, 1, [blk], [blk])

        PMa = carve("PMa", [128], F32)
        PMc = carve("PMc", [4, 128], F32)
        rows = [("norm1_w", None, 8), ("norm2_w", None, 8), ("hg_lb_logits", 0, 4), ("hg_lb_logits", 1, 4),
                ("hg_norm_w", None, 4), ("rw_shift_mu", None, 14), ("rw_w0", None, 4), ("rw_a0", None, 4),
                ("rw_k_k", None, 4), ("rw_k_a", None, 4), ("rw_r_k", None, 4), ("rw_ln_w", None, 4),
                ("rw_ln_b", None, 4)]
        r0 = 0
        for name, idx, n in rows:
            src = dr[name] if idx is None else dr[name][idx]
            P.dma("sp", "d_par", PMa.ap[r0:r0 + n, :], src.rearrange("(r p) -> r p", p=128), writes=[PMa])
            r0 += n
        assert r0 == 70
        for j in range(3):
            P.dma("sp", "d_par", PMc.ap[0:2 * NFC, j, :], dr["conv_w"][j].rearrange("(r p) -> r p", p=128), writes=[PMc])
        P.dma("sp", "d_par", PMc.ap[0:2 * NFC, 3, :], dr["conv_b"].rearrange("(r p) -> r p", p=128), writes=[PMc])
        P.dma("sp", "d_par", flg.ap, flag, writes=[flg])
        pp_ = psum()
        tr(pp_.ap[:, 0:70], PMa.ap[0:70, :], identf.ap[0:70, 0:70], [PMa, identf], [pp_])
        cp("dve", PT.ap, pp_.ap[:, 0:70], [pp_], [PT])
        pp_ = psum()
        for j in range(4):
            tr(pp_.ap[:, j * 44:(j + 1) * 44], PMc.ap[0:44, j, :], identf.ap[0:44, 0:44], [PMc, identf], [pp_], inc=(j == 3))
        cp("dve", CW.ap.rearrange("p a b -> p (a b)"), pp_.ap[:, 0:176], [pp_], [CW])
        nw1T = PT.ap[:, 0:8]
        nw2T = PT.ap[:, 8:16]
        hgnw = PT.ap[:, 24:28]
        muT = PT.ap[:, 28:42]
        w0T = PT.ap[:, 42:46]
        a0T = PT.ap[:, 46:50]
        kkT = PT.ap[:, 50:54]
        kaT = PT.ap[:, 54:58]
        rkT = PT.ap[:, 58:62]
        lnwT = PT.ap[:, 62:66]
        lnbT = PT.ap[:, 66:70]
        lbT = DP.ap[:, 0:4]
        lb1T = DP.ap[:, 4:8]
        cq2 = DP.ap[:, 8:12]
        hgnw2 = DP.ap[:, 12:16]
        hw0 = DP.ap[:, 16:20]
        ha0 = DP.ap[:, 20:24]
        hka = DP.ap[:, 24:28]
        nhka = DP.ap[:, 28:32]
        dtmp = DP.ap[:, 32:36]
        thl = DP.ap[:, 36:40]
        tt("dve", dtmp, PT.ap[:, 16:20], PT.ap[:, 20:24], ALU.subtract, [PT], [DP])
        act(thl, dtmp, AF.Exp, [DP], [DP], scale=-1.0)
        ts("dve", thl, thl, 1.0, None, ALU.add, None, [DP], [DP])
        P.op("dve", lambda e: e.reciprocal(out=lbT, in_=thl), [DP], [DP])
        ts("dve", lb1T, lbT, -1.0, 1.0, ALU.mult, ALU.add, [DP], [DP])
        ts("dve", cq2, lb1T, -(128 ** -0.5), None, ALU.mult, None, [DP], [DP])
        ts("dve", hw0, w0T, -1.0, None, ALU.mult, None, [PT], [DP])
        omT = DP.ap[:, 40:54]
        ts("dve", omT, muT, -1.0, 1.0, ALU.mult, ALU.add, [PT], [DP])
        ts("dve", ha0, a0T, -1.0, None, ALU.mult, None, [PT], [DP])
        if "PT" in dbg_t:
            dbg_dump("PT", PT.ap, [PT])

        if stop == "setup":
            P.barrier()
            P.replay(block)
            return nc, P
        P.barrier()
        st["off"] = glob_end
        w_in = carve("w_in", [8, 3840], BF16)
        w_out = carve("w_out", [8, 1024], BF16)
        w2b = carve("w2b", [512], BF16)
        a2b = carve("a2b", [512], BF16)
        g2b = carve("g2b", [512], BF16)
        wgroups = [(512, 1536), (3584, 3840), (2560, 3072), (3072, 3584), (0, 512), (1536, 2048), (2048, 2560)]
        wbufs = []
        wsrc = dr["w_in"].rearrange("(kc p) c -> p kc c", p=128)
        for gi, (c0, c1) in enumerate(wgroups):
            b = Buf(f"win{gi}")
            wbufs.append((c0, c1, b))
            P.dma("pool", f"d_w{gi}", w_in.ap[:, :, c0:c1], wsrc[:, :, c0:c1], writes=[b])
        P.dma("pool", "d_w7", w2b.ap[0:64, :], dr["rw_w2"], writes=[w2b])
        P.dma("pool", "d_w7", a2b.ap[64:128, :], dr["rw_a2"], writes=[a2b])
        P.dma("pool", "d_w7", g2b.ap, dr["rw_g2"], writes=[g2b])
        P.dma("pool", "d_wo", w_out.ap, dr["w_out"].rearrange("(kc p) c -> p kc c", p=128), writes=[w_out])

        def wbuf_of(c):
            col = c * 128
            for c0, c1, b in wbufs:
                if c0 <= col < c1:
                    return b
            raise AssertionError

        xring = ring("xt", 2, [D], F32, ["d_x0", "d_x1"])
        xnr = ring("xn", 2, [D], BF16)
        sm = ring("sm", 8, [16], F32)
        hT = carve("hT", [8, TB + 1], BF16)
        memset("pool", hT.ap, 0.0, [hT])
        tf = ring("tf", NTF, [TB + 4], F32)
        tb = ring("tb", 4, [TB], BF16)
        hg_kT = [carve(f"hg_kT{h}", [TB], BF16) for h in range(4)]
        hg_ktok = [carve(f"hg_ktok{h}", [NS, 128], BF16) for h in range(4)]
        hg_vtok = [carve(f"hg_vtok{h}", [NS, 128], BF16) for h in range(4)]
        hg_qT = [carve(f"hg_qT{h}", [TB], BF16) for h in range(4)]
        hg_sgT = [carve(f"hg_sgT{h}", [TB], BF16) for h in range(4)]
        hg_ebl = carve("hg_ebl", [4, TB // 64], F32)
        rw_kT = [carve(f"rw_kT{c}", [TB], BF16) for c in range(4)]
        rw_bT = [carve(f"rw_bT{c}", [TB], BF16) for c in range(4)]
        rw_ar = [carve(f"rw_ar{c}", [NS * 2, 2, 128], BF16) for c in range(4)]
        rw_ktok = [carve(f"rw_ktok{c}", [NS, 128], BF16) for c in range(4)]
        rw_btok = [carve(f"rw_btok{c}", [NS, 128], BF16) for c in range(4)]
        rw_vtok = [carve(f"rw_vtok{c}", [NS, 128], BF16) for c in range(4)]
        rw_gT = [carve(f"rw_gT{c}", [TB], BF16) for c in range(4)]
        rw_bon = [carve(f"rw_bon{c}", [TB], BF16) for c in range(4)]
        rw_egx = [carve(f"rw_egx{c}", [NS, 129], F32) for c in range(4)]
        twd = carve("twd", [TB], BF16)
        adT = carve("adT", [TB], BF16)
        sgd = carve("sgd", [TB], BF16)
        ATh = ring("ATh", 2, [4, 128], BF16)
        Pq = [carve(f"Pq{i}", [4, 128], BF16) for i in range(2)] * NS
        PTq = [carve(f"PTq{i}", [4, 128], BF16) for i in range(2)] * NS
        TTq = [carve(f"TTq{i}", [4, 128], BF16) for i in range(2)] * NS
        Rb = carve("Rb", [512], BF16)
        Ub = carve("Ub", [512], BF16)
        ysb = ring("ysb", 2, [512], F32)
        ysq = carve("ysq", [512], F32)
        ynb = ring("ynb", 2, [512], BF16)
        oT = ring("oT", 2, [8, 128], BF16)
        x1t = ring("x1t", 1, [D], F32, ["d_st0"])
        for c in range(4):
            memset("pool", rw_egx[c].ap, 1.0, [rw_egx[c]])
            memset("pool", rw_ar[c].ap, 0.0, [rw_ar[c]])
        ATb = [carve(f"ATb{c}", [4, 128], BF16) for c in range(4)]
        ATk = [carve(f"ATk{c}", [4, 128], BF16) for c in range(4)]

        def arv(c, s, j, w):
            return rw_ar[c].ap[:, s * 2 + j, w, :]

        def rsqrt_small(dst_ap, dst_reads, src_ap, n, scale, eps, reads, writes):
            t_ = sm.next()
            act(t_.ap[:, 0:n], src_ap, AF.Ln, reads, [t_], scale=scale, bias=eps)
            act(dst_ap, t_.ap[:, 0:n], AF.Exp, [t_] + list(dst_reads), writes, scale=-0.5)

        def sig3(dst_ap, dst_tl, src_ap, reads, bias=None):
            e_ = tf.next()
            shp = [src_ap.shape[0], int(np.prod(src_ap.shape[1:]))]
            ev = e_.ap[0:shp[0], 0:shp[1]]
            if bias is None:
                act(ev, src_ap, AF.Exp, reads, [e_], scale=-1.0)
            else:
                act(ev, src_ap, AF.Exp, list(reads) + [DP], [e_], scale=-1.0, bias=bias)
            act(ev, ev, AF.Ln, [e_], [e_], bias=1.0)
            act(dst_ap, ev, AF.Exp, [e_], [dst_tl], scale=-1.0)

        def load_norm_T(src_rows_ap, nwT, hT_tile, s, xt=None, src_reads=(), off=0):
            if xt is None:
                xt = xring.next()
                P.dma("sp", xt.sem, xt.ap, src_rows_ap, reads=src_reads, writes=[xt])
            ss = sm.next()
            xn = xnr.next()
            act(xn.ap, xt.ap, AF.Square, [xt], [xn, ss], accum_out=ss.ap[:, 0:1])
            rs = sm.next()
            rsqrt_small(rs.ap[:, 0:1], [], ss.ap[:, 0:1], 1, 1.0 / D, 1e-6, [ss], [rs])
            ts("dve", xn.ap, xt.ap, rs.ap[:, 0:1], None, ALU.mult, None, [xt, rs], [xn])
            dd('ss', ss, ap=ss.ap[:, 0:1]); dd('rs', rs, ap=rs.ap[:, 0:1]); dd('xn', xn); dd('xt', xt)
            pb = psum()
            pbv = pb.ap.bitcast(BF16).rearrange("p (a b) -> p a b", b=128)
            for kc in range(8):
                tr(pbv[:, kc, :], xn.ap[:, kc * 128:(kc + 1) * 128], identb.ap, [xn, identb], [pb], inc=(kc == 7))
            tt("dve", hT_tile.ap[:, :, off + s * 128:off + (s + 1) * 128], pbv, nwT.unsqueeze(2).to_broadcast([128, 8, 128]),
               ALU.mult, [pb, PT], [hT_tile])
            return xt

        def proj(c, halo=False):
            pb = psum()
            wb = wbuf_of(c)
            for kc in range(8):
                if halo:
                    mm(pb.ap[:, 0:TB + 1], w_in.ap[:, kc, c * 128:(c + 1) * 128], hT.ap[:, kc, :], kc == 0, kc == 7,
                       [wb, hT], [pb], inc=(kc == 7))
                else:
                    mm(pb.ap[:, 0:TB], w_in.ap[:, kc, c * 128:(c + 1) * 128], hT.ap[:, kc, 1:TB + 1], kc == 0, kc == 7,
                       [wb, hT], [pb], inc=(kc == 7))
            return pb

        def to_tok(srcT, dst, eng="act"):
            pb = psum("half")
            pbv = pb.ap.bitcast(BF16)[:, 0:NS * 128].rearrange("p (a b) -> p a b", b=128)
            for s in range(NS):
                tr(pbv[:, s, :], srcT.ap[:, s * 128:(s + 1) * 128], identb.ap, [srcT, identb], [pb], inc=(s == NS - 1))
            cp(eng, dst.ap, pbv, [pb], [dst])


        def mixer_block(bi, full, emit_sub0):
            cp("dve", hT.ap[:, :, 0:1], hT.ap[:, :, TB:TB + 1], [hT], [hT])
            for s in range(NS):
                g = bi * NS + s
                load_norm_T(xs[g * 128:(g + 1) * 128, :], nw1T, hT, s, off=1)
            dd('hT', hT, when=(bi == nl))
            ck('A')
            for h in range(4):
                pf = proj(4 + h)
                thf = tf.next()
                sig3(thf.ap[:, 0:TB], thf, pf.ap[:, 0:TB], [pf])
                f_ = tf.next()
                ts("dve", f_.ap[:, 0:TB], thf.ap[:, 0:TB], lb1T[:, h:h + 1], lbT[:, h:h + 1], ALU.mult, ALU.add,
                   [thf, DP], [f_])
                eb = tf.next()
                for c4 in range(TB // 64):
                    sl = slice(c4 * 64, (c4 + 1) * 64)
                    scan(eb.ap[:, sl], f_.ap[:, sl], ones.ap[:, 0:64], 1.0, ALU.mult, ALU.mult, [f_, ones], [eb])
                enb = tf.next()
                P.op("dve", lambda e, o=enb.ap[:, 0:TB], i=eb.ap[:, 0:TB]: e.reciprocal(out=RB(o), in_=RB(i)), [eb], [enb], dur=340.0)
                stt(hg_kT[h].ap, thf.ap[:, 0:TB], 1.0, enb.ap[:, 0:TB], ALU.subtract, ALU.mult, [thf, enb], [hg_kT[h]])
                cp("act", hg_ebl.ap[:, h, :], eb.ap[:, 63:TB:64], [eb], [hg_ebl])
                kh = tb.next()
                tt(POOL_E, kh.ap.rearrange("p (a b) -> p a b", b=64), hg_kT[h].ap.rearrange("p (a b) -> p a b", b=64),
                   hg_ebl.ap[:, h, :].unsqueeze(2).to_broadcast([128, TB // 64, 64]), ALU.mult,
                   [hg_kT[h], hg_ebl], [kh])
                to_tok(kh, hg_ktok[h])
                pi = proj(8 + h)
                vT = tb.next()
                cp("act", vT.ap, pi.ap[:, 0:TB], [pi], [vT])
                to_tok(vT, hg_vtok[h])
                if full:
                    pq = proj(h)
                    thq = tf.next()
                    sig3(thq.ap[:, 0:TB], thq, pq.ap[:, 0:TB], [pq])
                    s1 = tf.next()
                    tt("dve", s1.ap[:, 0:TB], thq.ap[:, 0:TB], pq.ap[:, 0:TB], ALU.mult, [thq, pq], [s1])
                    stt(hg_qT[h].ap, s1.ap[:, 0:TB], cq2[:, h:h + 1], eb.ap[:, 0:TB], ALU.mult, ALU.mult,
                        [s1, eb, DP], [hg_qT[h]])
                    pg = proj(12 + h)
                    thg = tf.next()
                    sig3(thg.ap[:, 0:TB], thg, pg.ap[:, 0:TB], [pg])
                    tt("dve", hg_sgT[h].ap, thg.ap[:, 0:TB], pg.ap[:, 0:TB], ALU.mult, [thg, pg], [hg_sgT[h]])

            dd('hg_kT0', hg_kT[0], when=(bi == nl)); dd('hg_qT0', hg_qT[0], when=(bi == nl)); dd('hg_vtok0', hg_vtok[0], when=(bi == nl)); dd('hg_ktok0', hg_ktok[0], when=(bi == nl)); dd('hg_sgT0', hg_sgT[0], when=(bi == nl)); dd('hg_ebl', hg_ebl, when=(bi == nl))
            ck('hgprep')
            def shift_mix(pb, ci, out_ap, out_tl, rows=slice(0, 128)):
                a1_ = tf.next()
                act(a1_.ap[:, 0:TB], pb.ap[:, 1:TB + 1], AF.Identity, [pb, DP], [a1_], scale=omT[:, ci:ci + 1])
                stt(out_ap, pb.ap[rows, 0:TB], muT[rows, ci:ci + 1], a1_.ap[rows, 0:TB], ALU.mult, ALU.add,
                    [pb, a1_, PT], [out_tl])

            pl = proj(28, halo=True)
            lo = tf.next()
            shift_mix(pl, 12, lo.ap[:, 0:TB], lo)
            e_ = tf.next()
            act(e_.ap[0:64, 0:TB], lo.ap[0:64, 0:TB], AF.Exp, [lo], [e_], scale=-2.0)
            act(e_.ap[0:64, 0:TB], e_.ap[0:64, 0:TB], AF.Ln, [e_], [e_], bias=1.0)
            act(e_.ap[0:64, 0:TB], e_.ap[0:64, 0:TB], AF.Exp, [e_], [e_], scale=-1.0)
            ts("dve", twd.ap[0:64, :], e_.ap[0:64, 0:TB], 2.0, -1.0, ALU.mult, ALU.add, [e_], [twd])
            cp("act", adT.ap[64:128, :], lo.ap[64:128, 0:TB], [lo], [adT])
            if full:
                pg_ = proj(29, halo=True)
                gdm = tf.next()
                shift_mix(pg_, 13, gdm.ap[:, 0:TB], gdm)
                sig3(sgd.ap, sgd, gdm.ap[:, 0:TB], [gdm])
            for c in range(4):
                cs = slice(c * 128, (c + 1) * 128)
                pw = psum("half")
                mm(pw.ap[:, 0:TB], w2b.ap[0:64, cs], twd.ap[0:64, :], True, True, [w2b, twd], [pw])
                ld = tf.next()
                sig3(ld.ap[:, 0:TB], ld, pw.ap[:, 0:TB], [pw], bias=hw0[:, c:c + 1])
                cld = float(np.exp(-0.5))
                lg = tf.next()
                for s in range(NS):
                    sl = slice(s * 128, (s + 1) * 128)
                    scan(lg.ap[:, sl], ld.ap[:, sl], zeros.ap, 0.0, ALU.add, ALU.add, [ld, zeros], [lg])
                act(rw_egx[c].ap[:, :, 1:129], lg.ap[:, 0:TB].rearrange("p (a b) -> p a b", b=128), AF.Exp,
                    [lg], [rw_egx[c]], scale=-cld)
                eng_t = tf.next()
                eng_ap = eng_t.ap[:, 0:TB]
                act(eng_ap, lg.ap[:, 0:TB], AF.Exp, [lg], [eng_t], scale=cld)
                pa = psum("half")
                mm(pa.ap[:, 0:TB], a2b.ap[64:128, cs], adT.ap[64:128, :], True, True, [a2b, adT], [pa])
                tha_t = tf.next()
                tha_ap = tha_t.ap[:, 0:TB]
                sig3(tha_ap, tha_t, pa.ap[:, 0:TB], [pa], bias=ha0[:, c:c + 1])
                if full:
                    pgm = psum("half")
                    mm(pgm.ap[:, 0:TB], g2b.ap[:, cs], sgd.ap, True, True, [g2b, sgd], [pgm])
                    cp("act", rw_gT[c].ap, pgm.ap[:, 0:TB], [pgm], [rw_gT[c]])
                pk = proj(20 + c, halo=True)
                kr = tf.next()
                shift_mix(pk, 4 + c, kr.ap[:, 0:TB], kr)
                kk = tf.next()
                ts("dve", kk.ap[:, 0:TB], kr.ap[:, 0:TB], kkT[:, c:c + 1], None, ALU.mult, None, [kr, PT], [kk])
                sq = tf.next()
                act(sq.ap[:, 0:TB], kk.ap[:, 0:TB], AF.Square, [kk], [sq])
                pss = psum("half")
                mm(pss.ap[:, 0:TB], blk.ap, sq.ap[:, 0:TB], True, True, [blk, sq], [pss])
                mx = tf.next()
                rn = tf.next()
                act(mx.ap[:, 0:TB], pss.ap[:, 0:TB], AF.Ln, [pss], [mx], bias=float(2.0 ** -60))
                act(rn.ap[:, 0:TB], mx.ap[:, 0:TB], AF.Exp, [mx], [rn], scale=-0.5)
                kkn = tf.next()
                tt(POOL_E, kkn.ap[:, 0:TB], kk.ap[:, 0:TB], rn.ap[:, 0:TB], ALU.mult, [kk, rn], [kkn])
                t1 = tf.next()
                ts("dve", t1.ap[:, 0:TB], tha_ap, 1.0, kaT[:, c:c + 1], ALU.subtract, ALU.mult,
                   [tha_t, PT], [t1])
                kp = tf.next()
                stt(kp.ap[:, 0:TB], t1.ap[:, 0:TB], 1.0, kr.ap[:, 0:TB], ALU.add, ALU.mult, [t1, kr], [kp])
                tt(POOL_E, rw_kT[c].ap, kp.ap[:, 0:TB], eng_ap, ALU.mult, [kp, eng_t], [rw_kT[c]])
                b1 = tf.next()
                tt(POOL_E, b1.ap[:, 0:TB], tha_ap, kkn.ap[:, 0:TB], ALU.mult, [tha_t, kkn], [b1])
                tt(POOL_E, rw_bT[c].ap, b1.ap[:, 0:TB], eng_ap, ALU.mult, [b1, eng_t], [rw_bT[c]])
                for j in range(2):
                    pj = slice(64 * j, 64 * j + 64)
                    stt(rw_ar[c].ap[pj, j:2 * NS:2, 0, :], kkn.ap[pj, 0:TB].rearrange("p (a b) -> p a b", b=128), -1.0,
                        rw_egx[c].ap[pj, :, 0:128], ALU.mult, ALU.mult, [kkn, rw_egx[c]], [rw_ar[c]])
                to_tok(rw_kT[c], rw_ktok[c])
                to_tok(rw_bT[c], rw_btok[c])
                pv = proj(24 + c, halo=True)
                vr = tf.next()
                shift_mix(pv, 8 + c, vr.ap[:, 0:TB], vr)
                vTb = tb.next()
                cp("act", vTb.ap, vr.ap[:, 0:TB], [vr], [vTb])
                to_tok(vTb, rw_vtok[c])
                if full:
                    pr = proj(16 + c, halo=True)
                    rr = tf.next()
                    shift_mix(pr, c, rr.ap[:, 0:TB], rr)
                    for j in range(2):
                        pj = slice(64 * j, 64 * j + 64)
                        tt(POOL_E, rw_ar[c].ap[pj, j:2 * NS:2, 1, :], rr.ap[pj, 0:TB].rearrange("p (a b) -> p a b", b=128),
                           rw_egx[c].ap[pj, :, 1:129], ALU.mult, [rr, rw_egx[c]], [rw_ar[c]])
                    rkr = tf.next()
                    stt(rkr.ap[:, 0:TB], rr.ap[:, 0:TB], rkT[:, c:c + 1], kp.ap[:, 0:TB], ALU.mult, ALU.mult,
                        [rr, kp, PT], [rkr])
                    pbs = psum("half")
                    mm(pbs.ap[:, 0:TB], blk.ap, rkr.ap[:, 0:TB], True, True, [blk, rkr], [pbs])
                    tt("dve", rw_bon[c].ap, pbs.ap[:, 0:TB], vr.ap[:, 0:TB], ALU.mult, [pbs, vr], [rw_bon[c]])

            dd('rw_kT0', rw_kT[0], when=(bi == nl)); dd('rw_bT0', rw_bT[0], when=(bi == nl)); dd('rw_ar0', rw_ar[0], when=(bi == nl)); dd('rw_egx0', rw_egx[0], when=(bi == nl)); dd('rw_vtok0', rw_vtok[0], when=(bi == nl)); dd('rw_gT0', rw_gT[0], when=(bi == nl)); dd('rw_bon0', rw_bon[0], when=(bi == nl));
            ck('rwprep')
            for s in range(NS):
                ssl = slice(s * 128, (s + 1) * 128)
                emit = full and (s >= emit_sub0)
                if emit:
                    pa = psum()
                    for h in range(4):
                        mm(pa.ap[:, h * 128:(h + 1) * 128], hg_kT[h].ap[:, ssl], hg_qT[h].ap[:, ssl], True, True,
                           [hg_kT[h], hg_qT[h]], [pa], inc=(h == 3))
                    ath = ATh.next()
                    tt("dve", ath.ap, pa.ap.rearrange("p (a b) -> p a b", b=128),
                       mUI.ap.unsqueeze(1).to_broadcast([128, 4, 128]), ALU.mult, [pa, mUI], [ath])
                    po = psum()
                for cc in range(2):
                    ps_ = slice(64 * cc, 64 * cc + 64)
                    ci = s * 2 + cc
                    if emit:
                        for h in range(4):
                            hs = slice(h * 128, (h + 1) * 128)
                            mm(po.ap[ps_, hs], hg_qT[h].ap[:, s * 128 + 64 * cc:s * 128 + 64 * cc + 64],
                               Sb_hg.ap[:, h, :], True, False, [hg_qT[h], Sb_hg], [po], inc=False)
                            mm(po.ap[ps_, hs], ath.ap[:, h, 64 * cc:64 * cc + 64], hg_vtok[h].ap[:, s, :],
                               False, True, [ath, hg_vtok[h]], [po], inc=(h == 3))
                    pS = psum()
                    for h in range(4):
                        hs = slice(h * 128, (h + 1) * 128)
                        mm(pS.ap[:, hs], hg_ktok[h].ap[ps_, s, :], hg_vtok[h].ap[ps_, s, :], True, True,
                           [hg_ktok[h], hg_vtok[h]], [pS], inc=(h == 3))
                    for h in range(4):
                        hs = slice(h * 128, (h + 1) * 128)
                        stt(S_hg.ap[:, h, :], S_hg.ap[:, h, :], hg_ebl.ap[:, h, ci:ci + 1], pS.ap[:, hs],
                            ALU.mult, ALU.add, [S_hg, hg_ebl, pS], [S_hg])
                    cp("act", Sb_hg.ap, S_hg.ap, [S_hg], [Sb_hg])
                if emit:
                    s2 = sm.next()
                    jk = ysb.next()
                    for h in range(4):
                        act(jk.ap[:, h * 128:(h + 1) * 128], po.ap[:, h * 128:(h + 1) * 128], AF.Square, [po], [jk, s2],
                            accum_out=s2.ap[:, h:h + 1])
                    rs = sm.next()
                    rsqrt_small(rs.ap[:, 0:4], [], s2.ap[:, 0:4], 4, 1.0 / 128, 1e-6, [s2], [rs])
                    on = ynb.next()
                    tt("dve", on.ap.rearrange("p (a b) -> p a b", b=128), po.ap.rearrange("p (a b) -> p a b", b=128),
                       rs.ap[:, 0:4].unsqueeze(2).to_broadcast([128, 4, 128]), ALU.mult, [po, rs], [on])
                    pt_ = psum("half")
                    ptv = pt_.ap.bitcast(BF16)[:, 0:512].rearrange("p (a b) -> p a b", b=128)
                    for h in range(4):
                        tr(ptv[:, h, :], on.ap[:, h * 128:(h + 1) * 128], identb.ap, [on, identb], [pt_], inc=(h == 3))
                    ot = oT.next()
                    for h in range(4):
                        stt(ot.ap[:, h, :], ptv[:, h, :], hgnw[:, h:h + 1], hg_sgT[h].ap[:, ssl], ALU.mult, ALU.mult,
                            [pt_, DP, hg_sgT[h]], [ot])
                if emit:
                    dd('on', on); dd('ot_hg', ot, ap=ot.ap[:, 0:4, :])
                dd('S_hg', S_hg, when=(bi == nl and s == NS - 1))
                ck('hgchain')
                for c in range(4):
                    for (lt, dst) in ((rw_bT[c], ATb[c]), (rw_kT[c], ATk[c])):
                        pA = psum()
                        if emit:
                            mm(pA.ap, lt.ap[:, ssl], rw_ar[c].ap[:, 2 * s:2 * s + 2, :, :], True, True, [lt, rw_ar[c]], [pA])
                            tt("dve", dst.ap, pA.ap.rearrange("p (a b) -> p a b", b=128), mRW.ap, ALU.mult, [pA, mRW], [dst])
                        else:
                            mm(pA.ap[:, 0:256], lt.ap[:, ssl], rw_ar[c].ap[:, 2 * s:2 * s + 2, 0, :], True, True,
                               [lt, rw_ar[c]], [pA])
                            tt("dve", dst.ap[:, 0:3:2, :], pA.ap[:, 0:256].rearrange("p (a b) -> p a b", b=128),
                               mRW.ap[:, 0:3:2, :], ALU.mult, [pA, mRW], [dst])
                for g4 in range(2):
                    pN = psum()
                    for hh in range(4):
                        h = g4 * 4 + hh
                        c, j = h // 2, h % 2
                        pj = slice(64 * j, 64 * j + 64)
                        mm(pN.ap[:, hh * 128:(hh + 1) * 128], arv(c, s, j, 0), rw_bT[c].ap[:, ssl], True, True,
                           [rw_ar[c], rw_bT[c]], [pN], inc=(hh == 3))
                    Pc = Pq[s * 2 + g4]
                    tt("dve", Pc.ap, pN.ap.rearrange("p (a b) -> p a b", b=128),
                       mL.ap.unsqueeze(1).to_broadcast([128, 4, 128]), ALU.mult, [pN, mL], [Pc])
                    TTc = TTq[s * 2 + g4]
                    for cc_ in range(2):
                        c_ = g4 * 2 + cc_
                        tt(POOL_E, TTc.ap[:, 2 * cc_:2 * cc_ + 2, :], ATb[c_].ap[:, 0:3:2, :],
                           identb.ap.unsqueeze(1).to_broadcast([128, 2, 128]), ALU.add, [ATb[c_], identb], [TTc])
                    PTc = None
                    nlev = 6
                    for lev in range(1, nlev + 1):
                        last = (lev == nlev)
                        pP = psum()
                        def ptv_(hh):
                            if PTc is None:
                                h = g4 * 4 + hh
                                return ATb[h // 2].ap[:, 2 * (h % 2), :], ATb[h // 2]
                            return PTc.ap[:, hh, :], PTc
                        for hh in range(4):
                            pa_, pt_l = ptv_(hh)
                            mm(pP.ap[:, hh * 128:(hh + 1) * 128], pa_, Pc.ap[:, hh, :], True, True,
                               [pt_l, Pc], [pP], inc=(hh == 3))
                        if not last:
                            pPT = psum()
                            for hh in range(4):
                                pa_, pt_l = ptv_(hh)
                                mm(pPT.ap[:, hh * 128:(hh + 1) * 128], Pc.ap[:, hh, :], pa_, True, True,
                                   [pt_l, Pc], [pPT], inc=(hh == 3))
                        Pn = Pc
                        cp("act", Pn.ap, pP.ap.rearrange("p (a b) -> p a b", b=128), [pP], [Pn])
                        if not last:
                            PTn = PTq[s * 2 + g4]
                            cp("act", PTn.ap, pPT.ap.rearrange("p (a b) -> p a b", b=128), [pPT], [PTn])
                        pT2 = psum()
                        for hh in range(4):
                            mm(pT2.ap[:, hh * 128:(hh + 1) * 128], Pn.ap[:, hh, :], TTc.ap[:, hh, :], True, True,
                               [Pn, TTc], [pT2], inc=(hh == 3))
                        TTn = TTc
                        tt("dve", TTn.ap, pT2.ap.rearrange("p (a b) -> p a b", b=128), TTc.ap, ALU.add, [pT2, TTc], [TTn])
                        Pc = Pn
                        if not last:
                            PTc = PTn
                        TTc = TTn
                dd('ATb0', ATb[0], when=(bi == nl and s == NS - 1)); dd('ATk0', ATk[0], when=(bi == nl and s == NS - 1))
                ck('rwinv')
                pR = psum()
                for h in range(8):
                    c, j = h // 2, h % 2
                    pj = slice(64 * j, 64 * j + 64)
                    hs = slice(h * 64, (h + 1) * 64)
                    mm(pR.ap[:, hs], arv(c, s, j, 0), Hb_rw.ap[:, c, :], True, False, [rw_ar[c], Hb_rw], [pR],
                       inc=False)
                    mm(pR.ap[:, hs], ATk[c].ap[:, 2 * j, :], rw_vtok[c].ap[:, s, 64 * j:64 * j + 64], False, True,
                       [ATk[c], rw_vtok[c]], [pR], inc=(h == 7))
                cp("act", Rb.ap, pR.ap, [pR], [Rb])
                pU = psum()
                for h in range(8):
                    hs = slice(h * 64, (h + 1) * 64)
                    mm(pU.ap[:, hs], TTq[s * 2 + h // 4].ap[:, h % 4, :], Rb.ap[:, hs], True, True, [TTq[s * 2 + h // 4], Rb], [pU],
                       inc=(h == 7))
                cp("act", Ub.ap, pU.ap, [pU], [Ub])
                if emit:
                    pY = psum()
                    for h in range(8):
                        c, j = h // 2, h % 2
                        pj = slice(64 * j, 64 * j + 64)
                        hs = slice(h * 64, (h + 1) * 64)
                        mm(pY.ap[:, hs], arv(c, s, j, 1), Hb_rw.ap[:, c, :], True, False, [rw_ar[c], Hb_rw],
                           [pY], inc=False)
                        mm(pY.ap[:, hs], ATb[c].ap[:, 2 * j + 1, :], Ub.ap[:, hs], False, False, [ATb[c], Ub], [pY], inc=False)
                        mm(pY.ap[:, hs], ATk[c].ap[:, 2 * j + 1, :], rw_vtok[c].ap[:, s, 64 * j:64 * j + 64], False, True,
                           [ATk[c], rw_vtok[c]], [pY], inc=(h == 7))
                pH = psum()
                for c in range(4):
                    cs_ = slice(c * 128, (c + 1) * 128)
                    mm(pH.ap[:, cs_], rw_btok[c].ap[:, s, :], Ub.ap[:, cs_], True, False, [rw_btok[c], Ub], [pH], inc=False)
                    mm(pH.ap[:, cs_], rw_ktok[c].ap[:, s, :], rw_vtok[c].ap[:, s, :], False, True,
                       [rw_ktok[c], rw_vtok[c]], [pH], inc=(c == 3))
                ht = tf.next()
                htv = ht.ap[:, 0:256].rearrange("p (a b) -> p a b", b=64)
                for j in range(2):
                    pj = slice(64 * j, 64 * j + 64)
                    tt("dve", htv[pj], pH.ap[pj, :].rearrange("p (c x) -> p c x", x=128)[:, :, 64 * j:64 * j + 64],
                       H_rw.ap[pj], ALU.add, [pH, H_rw], [ht])
                for c in range(4):
                    ts("dve", H_rw.ap[:, c, :], htv[:, c, :], rw_egx[c].ap[:, s, 128:129], None, ALU.mult, None,
                       [ht, rw_egx[c]], [H_rw])
                cp("act", Hb_rw.ap, H_rw.ap, [H_rw], [Hb_rw])
                if emit:
                    dd('Ub', Ub); dd('Rb', Rb); dd('H_rw', H_rw)
                    ck('rwchain')
                    yb = ysb.next()
                    cp("act", yb.ap, pY.ap, [pY], [yb])
                    yv = yb.ap.rearrange("p (a b) -> p a b", b=64)
                    s1_ = sm.next()
                    P.op("dve", lambda e, o=s1_.ap[:, 0:8], i=yv: e.tensor_reduce(out=o, in_=i, axis=AX.X, op=ALU.add),
                         [yb], [s1_], dur=600.0)
                    act(ysq.ap, yb.ap, AF.Square, [yb], [ysq])
                    s2_ = sm.next()
                    P.op("dve", lambda e, o=s2_.ap[:, 0:8], i=ysq.ap.rearrange("p (a b) -> p a b", b=64):
                         e.tensor_reduce(out=o, in_=i, axis=AX.X, op=ALU.add), [ysq], [s2_], dur=600.0)
                    mean = sm.next()
                    ts("dve", mean.ap[:, 0:8], s1_.ap[:, 0:8], 1.0 / 64, None, ALU.mult, None, [s1_], [mean])
                    msq = sm.next()
                    tt("dve", msq.ap[:, 0:8], mean.ap[:, 0:8], mean.ap[:, 0:8], ALU.mult, [mean], [msq])
                    var = sm.next()
                    stt(var.ap[:, 0:8], s2_.ap[:, 0:8], 1.0 / 64, msq.ap[:, 0:8], ALU.mult, ALU.subtract, [s2_, msq], [var])
                    rs_ = sm.next()
                    rsqrt_small(rs_.ap[:, 0:8], [], var.ap[:, 0:8], 8, 1.0, 64e-5, [var], [rs_])
                    ycen = ysb.next()
                    tt("dve", ycen.ap.rearrange("p (a b) -> p a b", b=64), yv,
                       mean.ap[:, 0:8].unsqueeze(2).to_broadcast([128, 8, 64]), ALU.subtract, [yb, mean], [ycen])
                    yn = ynb.next()
                    tt("dve", yn.ap.rearrange("p (a b) -> p a b", b=64), ycen.ap.rearrange("p (a b) -> p a b", b=64),
                       rs_.ap[:, 0:8].unsqueeze(2).to_broadcast([128, 8, 64]), ALU.mult, [ycen, rs_], [yn])
                    pt2 = psum("half")
                    ptv2 = pt2.ap.bitcast(BF16)[:, 0:512].rearrange("p (a b) -> p a b", b=128)
                    for c in range(4):
                        tr(ptv2[:, c, :], yn.ap[:, c * 128:(c + 1) * 128], identb.ap, [yn, identb], [pt2], inc=(c == 3))
                    for c in range(4):
                        t_ = tf.next()
                        stt(t_.ap[:, 0:128], ptv2[:, c, :], lnwT[:, c:c + 1], rw_bon[c].ap[:, ssl], ALU.mult, ALU.add,
                            [pt2, PT, rw_bon[c]], [t_])
                        stt(ot.ap[:, 4 + c, :], t_.ap[:, 0:128], lnbT[:, c:c + 1], rw_gT[c].ap[:, ssl], ALU.add, ALU.mult,
                            [t_, PT, rw_gT[c]], [ot])
                    dd('yn', yn); dd('ot', ot)
                    ck('rwout')
                    g = bi * NS + s
                    po1 = psum()
                    po2 = psum()
                    for kc in range(8):
                        mm(po1.ap, ot.ap[:, kc, :], w_out.ap[:, kc, 0:512], kc == 0, kc == 7, [ot, w_out], [po1], inc=False)
                    for kc in range(8):
                        mm(po2.ap, ot.ap[:, kc, :], w_out.ap[:, kc, 512:1024], kc == 0, kc == 7, [ot, w_out], [po2],
                           inc=(kc == 7))
                    xo = x1t.next()
                    P.dma("sp", "d_st1", xo.ap, xs[g * 128:(g + 1) * 128, :], writes=[xo])
                    tt("dve", xo.ap[:, 0:512], po1.ap, xo.ap[:, 0:512], ALU.add, [po1, xo], [xo])
                    tt("dve", xo.ap[:, 512:1024], po2.ap, xo.ap[:, 512:1024], ALU.add, [po2, xo], [xo])
                    row = (g - (nl + 1) * NS + 1) * 128
                    P.dma("sp", xo.sem, x1d[row:row + 128, :], xo.ap, reads=[xo], writes=[x1d_bufs[row // 128]])

        x1d_bufs = [Buf(f"x1d{i}") for i in range(1 + nm * NS)]
        print("arena phase1 used f32 words", st["off"], "of", ARENA_F32)
        try:
            for bi in range(nblk):
                if bi < nl:
                    mixer_block(bi, False, NS)
                elif bi == nl:
                    mixer_block(bi, True, NS - 1)
                else:
                    mixer_block(bi, True, 0)
        except StopBuild:
            P.barrier()
            P.replay(block)
            return nc, P
        if "x1" in dbg_t:
            P.barrier()
            P.dma("sp", "d_dbg", dbg_t["x1"], x1d[:, :], reads=x1d_bufs)

        NPRE = 7
        st_save = st["off"]
        st["off"] = glob_end
        w_up = carve("w_up", [NFC // 2, 8, 512], BF16)
        assert NPRE * 8 * 512 // 2 <= 8 * 3840 // 2
        st["off"] = st_save
        wup_src = dr["w_up"].rearrange("(kc p) c -> p kc c", p=128)
        wupb = [Buf(f"wup{gi}") for gi in range(NFC // 2)]

        def load_wup(gi, extra_reads=()):
            c0 = gi * 256
            hb_ = [Buf(f"wup{gi}a"), Buf(f"wup{gi}b")]
            P.dma("pool", f"d_wup{gi}", w_up.ap[:, gi, :, 0:256], wup_src[:, :, c0:c0 + 256],
                  reads=list(extra_reads), writes=[hb_[0]])
            P.dma("pool", f"d_wup{gi}", w_up.ap[:, gi, :, 256:512], wup_src[:, :, DFF + c0:DFF + c0 + 256],
                  reads=list(extra_reads), writes=[hb_[1]])
            wupb[gi] = hb_

        if stop is None:
            fence = sm.next()
            memset("pool", fence.ap[:, 0:1], 0.0, [fence] + [b for (_, _, b) in wbufs])
            for gi in range(NPRE):
                load_wup(gi, [fence])

        P.barrier()
        if stop == "p1":
            P.replay(block)
            return nc, P
        st["off"] = glob_end
        w_up = carve("w_up", [NFC // 2, 8, 512], BF16)
        w_dn = carve("w_dn", [NFC, D], BF16)
        xring2 = ring("x2t", NS + 1, [D], F32, ["d_x0", "d_x1", "d_x2", "d_x3"][:NS + 1])
        xnr = ring("xn2_", 2, [D], BF16)
        sm = ring("sm2_", 8, [16], F32)
        h2Ts = [carve(f"h2T{i}", [8, TB + 2], BF16) for i in range(2)]
        gT = carve("gT", [NFC, TB], BF16)
        acg = ring("acg", 4, [TB], F32)
        acv = ring("acv", 4, [TB], F32)
        sgl = ring("sgl", 4, [TB], F32)
        xo2 = ring("xo2", 1, [D], F32)
        fnwb = carve("fnwb", [D], F32)
        P.dma("sp", "d_fn", fnwb.ap, dr["final_norm_w"].partition_broadcast(128), writes=[fnwb])
        yo = ring("yo", 2, [D], F32, ["d_st2", "d_st3"])
        xring = xring2
        for gi in range(NPRE, NFC // 2):
            load_wup(gi)
        wdn_src = dr["w_down"].rearrange("(c p) n -> p c n", p=128)
        wdnb = []
        for gi in range(2):
            b = Buf(f"wdn{gi}")
            wdnb.append(b)
            P.dma("pool", f"d_wdn{gi}", w_dn.ap[:, gi * 11:(gi + 1) * 11, :], wdn_src[:, gi * 11:(gi + 1) * 11, :], writes=[b])
        print("arena phase2 used f32 words", st["off"], "of", ARENA_F32)
        cw0 = CW.ap[:, 0, :]
        cw1 = CW.ap[:, 1, :]
        cw2 = CW.ap[:, 2, :]
        cbT = CW.ap[:, 3, :]

        xt0 = xring.next()
        P.dma("sp", xt0.sem, xt0.ap, x1d[0:128, :], reads=[x1d_bufs[0]], writes=[xt0])
        load_norm_T(None, nw2T, h2Ts[1], NS - 1, xt=xt0, off=2)
        ts("dve", h2Ts[1].ap[:, :, TB:TB + 2], h2Ts[1].ap[:, :, TB:TB + 2], flg.ap[:, 0:1], None, ALU.mult, None,
           [h2Ts[1], flg], [h2Ts[1]])

        for m in range(nm):
            x1ts = []
            h2T = h2Ts[m % 2]
            cp("dve", h2T.ap[:, :, 0:2], h2Ts[(m + 1) % 2].ap[:, :, TB:TB + 2], [h2Ts[(m + 1) % 2]], [h2T])
            for s in range(NS):
                row = 128 + (m * NS + s) * 128
                xt = xring.next()
                P.dma("sp", xt.sem, xt.ap, x1d[row:row + 128, :], reads=[x1d_bufs[row // 128]], writes=[xt])
                load_norm_T(None, nw2T, h2T, s, xt=xt, off=2)
                x1ts.append(xt)
            for c in range(NFC):
                res = []
                for (cc, ac_ring) in ((c, acg), (NFC + c, acv)):
                    pb = psum()
                    for kc in range(8):
                        lc = (0 if cc < NFC else 256) + (c % 2) * 128
                        mm(pb.ap[:, 0:TB + 2], w_up.ap[:, c // 2, kc, lc:lc + 128], h2T.ap[:, kc, :], kc == 0, kc == 7,
                           wupb[c // 2] + [h2T], [pb], inc=(kc == 7))
                    ac = ac_ring.next()
                    act(ac.ap, pb.ap[:, 2:TB + 2], AF.Identity, [pb, CW], [ac], scale=cw2[:, cc:cc + 1], bias=cbT[:, cc:cc + 1])
                    stt(ac.ap, pb.ap[:, 1:TB + 1], cw1[:, cc:cc + 1], ac.ap, ALU.mult, ALU.add, [pb, ac, CW], [ac])
                    stt(ac.ap, pb.ap[:, 0:TB], cw0[:, cc:cc + 1], ac.ap, ALU.mult, ALU.add, [pb, ac, CW], [ac])
                    res.append(ac)
                sg_ = sgl.next()
                act(sg_.ap, res[0].ap, AF.Silu, [res[0]], [sg_])
                tt("dve", gT.ap[:, c, :], sg_.ap, res[1].ap, ALU.mult, [sg_, res[1]], [gT])
            for s in range(NS):
                ssl = slice(s * 128, (s + 1) * 128)
                po1 = psum()
                po2 = psum()
                for c in range(NFC):
                    mm(po1.ap, gT.ap[:, c, ssl], w_dn.ap[:, c, 0:512], c == 0, c == NFC - 1, [gT, wdnb[c // 11]], [po1],
                       inc=False)
                for c in range(NFC):
                    mm(po2.ap, gT.ap[:, c, ssl], w_dn.ap[:, c, 512:1024], c == 0, c == NFC - 1, [gT, wdnb[c // 11]], [po2],
                       inc=(c == NFC - 1))
                xo = xo2.next()
                xt = x1ts[s]
                tt("dve", xo.ap[:, 0:512], po1.ap, xt.ap[:, 0:512], ALU.add, [po1, xt], [xo])
                tt("dve", xo.ap[:, 512:1024], po2.ap, xt.ap[:, 512:1024], ALU.add, [po2, xt], [xo])
                ss = sm.next()
                jk = xnr.next()
                act(jk.ap, xo.ap, AF.Square, [xo], [jk, ss], accum_out=ss.ap[:, 0:1])
                rs = sm.next()
                rsqrt_small(rs.ap[:, 0:1], [], ss.ap[:, 0:1], 1, 1.0 / D, 1e-6, [ss], [rs])
                y_ = yo.next()
                stt(y_.ap, xo.ap, rs.ap[:, 0:1], fnwb.ap, ALU.mult, ALU.mult, [xo, rs, fnwb], [y_])
                row = (m * NS + s) * 128
                P.dma("sp", y_.sem, out[row:row + 128, :], y_.ap, reads=[y_])
        P.barrier()
        P.replay(block)
    return nc, P


def _prep_inputs(inputs):
    sq = {}
    for name, shp in PARAM_SHAPES:
        a = np.asarray(inputs[name], dtype=np.float32)
        sq[name] = np.ascontiguousarray(a.reshape(shp))
    return sq


def kernel(**inputs):
    x = np.asarray(inputs["x"], dtype=np.float32)
    B, T, Dm = x.shape
    half = T // 2
    params = _prep_inputs(inputs)
    nl = half // TB - 1
    nm = half // TB
    nc, _ = build(nl, nm)
    in_maps = []
    for c in range(8):
        b, j = c // 2, c % 2
        xs = np.zeros((2 * half, Dm), np.float32)
        if j == 1:
            xs[:half] = x[b, :half]
        xs[half:] = x[b, j * half:(j + 1) * half]
        m = dict(params)
        m["xs"] = xs
        m["flag"] = np.full((128, 1), float(j), np.float32)
        in_maps.append(m)
    res = run_bass_kernel_spmd(nc, in_maps, core_ids=list(range(8)))
    outp = np.empty((B, T, Dm), np.float32)
    for c in range(8):
        b, j = c // 2, c % 2
        outp[b, j * half:(j + 1) * half] = res.results[c]["out"]
    return outp
```

```python
import numpy as np
from contextlib import ExitStack
import concourse.bass as bass
import concourse.mybir as mybir
from concourse.bass_utils import run_bass_kernel_spmd
from concourse.ap import AP

F32 = mybir.dt.float32
BF16 = mybir.dt.bfloat16
AF = mybir.ActivationFunctionType
ALU = mybir.AluOpType
AX = mybir.AxisListType

SELF_SYNC = True
PROFILE = False
NTF = 27
POOL_E = "dve"
PSH_BANKS = 0
VWIN = 100
TB = 256
NS = TB // 128
D = 1024
DFF = 2816
NFC = DFF // 128


class StopBuild(Exception):
    pass


class Buf:
    __slots__ = ("name", "w", "r", "nodes", "valloc")

    def __init__(self, name):
        self.name = name
        self.w = None
        self.r = []
        self.valloc = None
        self.nodes = None


_SM = {"pass": 0, "vallocs": [], "assign": {}}
SMART_RINGS = False


class Tl:
    __slots__ = ("ap", "buf", "sem")

    def __init__(self, ap, name, sem=None):
        self.ap = ap
        self.buf = Buf(name)
        self.sem = sem


class Ring:
    def __init__(self, tiles, name=None):
        self.t = tiles
        self.i = 0
        self.name = name

    def next(self):
        i = self.i
        self.i += 1
        if self.name is not None and len(self.t) > 1:
            if _SM["pass"] == 1:
                prev = _SM["assign"].get(self.name)
                t = self.t[prev[i]] if prev is not None else self.t[i % len(self.t)]
                t.buf.nodes = []
                _SM["vallocs"].append((self.name, i, len(self.t), t.buf.nodes))
                return t
            if _SM["pass"] == 2 and self.name in _SM["assign"]:
                return self.t[_SM["assign"][self.name][i]]
        return self.t[i % len(self.t)]


def _bufs(xs):
    out = []
    for x in xs:
        if x is None:
            continue
        out.append(x.buf if isinstance(x, Tl) else x)
    return out


class Node:
    __slots__ = ("id", "eng", "fn", "preds", "succs", "dur", "kind", "semkey", "lat", "prio", "start", "fin",
                 "ev", "nready", "tag", "seg", "vallocs")

    def __init__(self, id_, eng, fn, dur, kind, semkey, lat):
        self.id = id_
        self.eng = eng
        self.fn = fn
        self.preds = set()
        self.succs = []
        self.dur = dur
        self.kind = kind
        self.semkey = semkey
        self.lat = lat
        self.prio = 0.0
        self.ev = None


class Prog:
    ENG = ("pe", "act", "dve", "pool", "sp")
    XLAT = 120.0

    def __init__(self, nc, sems, group_sems=()):
        self.nc = nc
        self.q = {e: [] for e in self.ENG}
        self.sem = sems
        self.group = set(group_sems)
        self.cnt = {k: 0 for k in sems}
        self.known = {e: {} for e in self.ENG}
        self.snap = {}
        self.nodes = []
        self.seg = 0
        self.valloc_nodes = {}
        self.place = {}
        self.alloc_class = {}
        self.class_slots = {}
        self.nwait = 0
        self.nins = {e: 0 for e in self.ENG}
        self.sim_end = 0.0
        self.sim_total = 0.0
        self.profile = None

    def _add(self, e, fn, reads, writes, dur, kind, semkey, lat):
        n = Node(len(self.nodes), e, fn, dur, kind, semkey, lat)
        n.seg = self.seg
        n.vallocs = None
        for b in list(reads) + list(writes):
            if b.nodes is not None:
                b.nodes.append(n)
            if b.valloc is not None:
                if n.vallocs is None:
                    n.vallocs = []
                if b.valloc not in n.vallocs:
                    n.vallocs.append(b.valloc)
                    self.valloc_nodes.setdefault(b.valloc, []).append(n)
        if self.profile is not None:
            import sys as _sys
            f = _sys._getframe(3)
            n.tag = f.f_lineno if f.f_code.co_name not in ("act", "tt", "ts", "stt", "cp", "mm", "tr", "scan") else _sys._getframe(4).f_lineno
        sg = self.seg
        for b in reads:
            if b.w is not None and b.w[0] == sg:
                n.preds.add(b.w[1])
        for b in writes:
            if b.w is not None and b.w[0] == sg:
                n.preds.add(b.w[1])
            for r in b.r:
                if r[0] == sg:
                    n.preds.add(r[1])
        n.preds.discard(n.id)
        me = (sg, n.id)
        for b in writes:
            b.w = me
            b.r = []
        for b in reads:
            if not b.r or b.r[-1] != me:
                b.r.append(me)
        self.nodes.append(n)
        return n

    def op(self, e, fn, reads=(), writes=(), inc=True, dur=100.0):
        self._add(e, fn, _bufs(reads), _bufs(writes), dur, "op", None, dur if e != "pe" else 60.0)

    def dma(self, e, semkey, out, in_, reads=(), writes=(), nbytes=65536, **kw):
        nbytes = int(np.prod(out.shape)) * 4
        lat = 2000.0 + nbytes / 150.0
        self._add(e, lambda eng: eng.dma_start(out=out, in_=in_, **kw), _bufs(reads), _bufs(writes), 60.0, "dma",
                  semkey, lat)

    def _schedule(self):
        import heapq
        nodes = self.nodes
        for n in nodes:
            for p in n.preds:
                nodes[p].succs.append(n.id)
        for n in reversed(nodes):
            best = 0.0
            for s_ in n.succs:
                sn = nodes[s_]
                lat = 0.0 if (n.eng == "pe" and sn.eng == "pe") else (n.lat + self.XLAT)
                v = sn.prio + lat
                if v > best:
                    best = v
            n.prio = n.dur + best
        free = {e: 0.0 for e in self.ENG}
        waiting = {e: [] for e in self.ENG}
        avail = {e: [] for e in self.ENG}
        blocked = []
        ready_t = [0.0] * len(nodes)
        npred = [len(n.preds) for n in nodes]
        for n in nodes:
            if npred[n.id] == 0:
                heapq.heappush(waiting[n.eng], (0.0, n.id))
        vnodes = self.valloc_nodes
        remaining_v = {k: len(v) for k, v in vnodes.items()}
        cls_of = self.alloc_class
        place = self.place
        owner = {c: [None] * n_ for c, n_ in self.class_slots.items()}
        free_t = {c: [0.0] * n_ for c, n_ in self.class_slots.items()}
        vorder = {c: [] for c in self.class_slots}
        for k in sorted(vnodes.keys()):
            vorder[cls_of[k]].append(k)
        optr = {c: 0 for c in self.class_slots}
        rptr = {c: 0 for c in self.class_slots}
        vfin = {k: 0.0 for k in vnodes}

        placed_flag = [False]

        def try_alloc(n):
            need_ = [k for k in n.vallocs if k not in place]
            if not need_:
                return True
            plan = []
            used = {}
            for k in need_:
                c = cls_of[k]
                vo = vorder[c]
                while optr[c] < len(vo) and vo[optr[c]] in place:
                    optr[c] += 1
                is_oldest = optr[c] < len(vo) and vo[optr[c]] == k
                ro = rptr[c]
                while ro < len(vo) and remaining_v[vo[ro]] == 0:
                    ro += 1
                rptr[c] = ro
                if ro < len(vo) and k > vo[ro] + VWIN:
                    return False
                ow = owner[c]
                freeb = [b for b in range(len(ow)) if (ow[b] is None or remaining_v[ow[b]] == 0)
                         and b not in used.get(c, ())]
                if not freeb or (len(freeb) < 2 and not is_oldest):
                    return False
                b = min(freeb, key=lambda b_: free_t[c][b_])
                used.setdefault(c, set()).add(b)
                plan.append((k, c, b))
            for k, c, b in plan:
                prev = owner[c][b]
                if prev is not None:
                    for p in vnodes[prev]:
                        if p.id not in n.preds:
                            n.preds.add(p.id)
                            p.succs.append(n.id)
                        lat = 0.0 if (p.eng == "pe" and n.eng == "pe") else (p.lat + self.XLAT)
                        rt = p.fin + lat
                        if rt > ready_t[n.id]:
                            ready_t[n.id] = rt
                owner[c][b] = k
                place[k] = b
            placed_flag[0] = True
            return True

        order = []
        remaining = len(nodes)
        while remaining:
            cand = []
            for e in self.ENG:
                w, a = waiting[e], avail[e]
                while w and w[0][0] <= free[e]:
                    t_, i_ = heapq.heappop(w)
                    heapq.heappush(a, (-nodes[i_].prio, i_))
                if a:
                    cand.append((free[e], e))
                elif w:
                    cand.append((w[0][0], e))
            cand.sort()
            picked = None
            for t, e in cand:
                tmp = []
                while True:
                    if avail[e]:
                        item = heapq.heappop(avail[e])
                        i_ = item[1]
                    elif waiting[e]:
                        item = heapq.heappop(waiting[e])
                        i_ = item[1]
                    else:
                        break
                    n = nodes[i_]
                    if n.vallocs is not None and not try_alloc(n):
                        blocked.append(i_)
                        continue
                    picked = n
                    break
                if picked is not None:
                    break
            if picked is None:
                raise RuntimeError("scheduler deadlock: all candidates blocked on resource slots")
            n = picked
            e = n.eng
            i_ = n.id
            start = max(free[e], ready_t[i_])
            n.start = start
            n.fin = start + n.dur
            free[e] = n.fin
            order.append(n)
            remaining -= 1
            if n.vallocs is not None:
                released = placed_flag[0]
                placed_flag[0] = False
                for k in n.vallocs:
                    remaining_v[k] -= 1
                    if n.fin > vfin[k]:
                        vfin[k] = n.fin
                    if remaining_v[k] == 0:
                        free_t[cls_of[k]][place[k]] = vfin[k]
                        released = True
                if released and blocked:
                    for j_ in blocked:
                        heapq.heappush(waiting[nodes[j_].eng], (ready_t[j_], j_))
                    blocked = []
            for s_ in n.succs:
                sn = nodes[s_]
                lat = 0.0 if (n.eng == "pe" and sn.eng == "pe") else (n.lat + self.XLAT)
                rt = start + lat if n.kind == "dma" else n.fin + (0.0 if lat == 0.0 else lat)
                if rt > ready_t[s_]:
                    ready_t[s_] = rt
                npred[s_] -= 1
                if npred[s_] == 0:
                    heapq.heappush(waiting[sn.eng], (ready_t[s_], s_))
        self.sim_end = max(free.values()) if nodes else 0.0
        if self.profile is not None:
            self.profile.append((self.sim_end, [(n.eng, n.tag, n.dur, n.start) for n in nodes]))
        return order

    def _merge(self, e, k, v):
        kn = self.known[e]
        if kn.get(k, 0) >= v:
            return False
        kn[k] = v
        s = self.snap.get((k, v))
        if s:
            for k2, v2 in s.items():
                if kn.get(k2, 0) < v2:
                    kn[k2] = v2
        return True

    def flush(self):
        nodes = self.nodes
        if not nodes:
            return
        order = self._schedule()
        self.sim_total += self.sim_end
        gtot = {}
        for n in nodes:
            if n.kind == "dma" and n.semkey in self.group:
                gtot[n.semkey] = gtot.get(n.semkey, self.cnt[n.semkey]) + 16
        need = [False] * len(nodes)
        for n in nodes:
            if n.kind == "dma":
                need[n.id] = True
                continue
            for s_ in n.succs:
                sn = nodes[s_]
                if not (n.eng == "pe" and sn.eng == "pe"):
                    if sn.eng != n.eng or SELF_SYNC:
                        need[n.id] = True
                        break
        sems = self.sem
        for n in order:
            e = n.eng
            evs = {}
            for p in n.preds:
                pn = nodes[p]
                if pn.eng == "pe" and e == "pe" and pn.kind == "op":
                    continue
                if pn.kind == "op" and pn.eng == e and not SELF_SYNC:
                    continue
                if pn.kind == "dma" and n.kind == "dma" and pn.semkey == n.semkey and n.semkey in self.group:
                    continue
                k, v = pn.ev
                if evs.get(k, 0) < v:
                    evs[k] = v
            waits = []
            for k, v in evs.items():
                if self._merge(e, k, v):
                    waits.append((k, v))
            if n.kind == "dma":
                k = n.semkey
                self.cnt[k] += 16
                n.ev = (k, gtot[k]) if k in self.group else (k, self.cnt[k])
                amount, semkey = 16, k
                if k not in self.group:
                    self.snap[n.ev] = dict(self.known[e])
            elif need[n.id]:
                self.cnt[e] += 1
                n.ev = (e, self.cnt[e])
                self.snap[n.ev] = dict(self.known[e])
                amount, semkey = 1, e
            else:
                n.ev = (e, self.cnt[e] + 1)
                amount, semkey = 0, e
            self.nwait += len(waits)
            self.nins[e] += 1

            def run(eng, waits=waits, fn=n.fn, amount=amount, semkey=semkey):
                for k, v in waits:
                    eng.wait_ge(sems[k], v)
                ins = fn(eng)
                if amount:
                    ins.then_inc(sems[semkey], amount)
            self.q[e].append(run)
        self.nodes = []
        self.valloc_nodes = {}
        self.seg += 1

    def barrier(self):
        self.flush()
        for e in self.ENG:
            waits = []
            for k, v in self.cnt.items():
                if v > 0 and k != e and self.known[e].get(k, 0) < v:
                    self.known[e][k] = v
                    waits.append((k, v))
            sems = self.sem

            def run(eng, waits=waits):
                for k, v in waits:
                    eng.wait_ge(sems[k], v)
            self.q[e].append(run)

    def replay(self, block):
        self.flush()
        if _SM["pass"] == 1:
            return
        q = self.q

        @block.tensor
        def _(eng):
            for f in q["pe"]:
                f(eng)

        @block.scalar
        def _(eng):
            for f in q["act"]:
                f(eng)

        @block.vector
        def _(eng):
            for f in q["dve"]:
                f(eng)

        @block.gpsimd
        def _(eng):
            for f in q["pool"]:
                f(eng)

        @block.sync
        def _(eng):
            for f in q["sp"]:
                f(eng)


PARAM_SHAPES = [
    ("norm1_w", [1024]), ("w_in", [1024, 3840]), ("hg_lb_logits", [2, 512]), ("hg_norm_w", [512]),
    ("rw_shift_mu", [1792]), ("rw_w0", [512]), ("rw_w2", [64, 512]), ("rw_a0", [512]),
    ("rw_a2", [64, 512]), ("rw_g2", [128, 512]), ("rw_k_k", [512]), ("rw_k_a", [512]),
    ("rw_r_k", [512]), ("rw_ln_w", [512]), ("rw_ln_b", [512]), ("w_out", [1024, 1024]),
    ("norm2_w", [1024]), ("w_up", [1024, 5632]), ("conv_w", [3, 5632]), ("conv_b", [5632]),
    ("w_down", [2816, 1024]), ("final_norm_w", [1024]),
]


def _assign_slots():
    by = {}
    for name, i, nslots, nodes in _SM["vallocs"]:
        by.setdefault(name, []).append((i, nslots, nodes))
    assign = {}
    for name, lst in by.items():
        if name not in SMART_SET:
            continue
        nslots = lst[0][1]
        res = [0] * len(lst)
        items = []
        for i, _, nodes in lst:
            if not nodes:
                items.append((0, 0.0, i))
            else:
                items.append((nodes[0].seg, min(n.start for n in nodes), i))
        items.sort()
        for rank, (seg, st, i) in enumerate(items):
            res[i] = rank % nslots
        assign[name] = res
    return assign


SMART_SET = ("ps",)
SMART_ITERS = 1


def build(nl=7, nm=8, dbg=None, stop=None):
    global VWIN
    if not SMART_RINGS:
        _SM["pass"] = 0
        last = None
        for w in (VWIN, 64, 48, 80, 40, 128, 32):
            VWIN = w
            try:
                return _build(nl, nm, dbg, stop)
            except RuntimeError as e_:
                if "scheduler deadlock" not in str(e_):
                    raise
                last = e_
        raise last
    _SM["assign"] = {}
    for _ in range(SMART_ITERS):
        _SM["pass"] = 1
        _SM["vallocs"] = []
        _build(nl, nm, dbg, stop)
        _SM["assign"] = _assign_slots()
    _SM["pass"] = 2
    try:
        return _build(nl, nm, dbg, stop)
    finally:
        _SM["pass"] = 0


def _build(nl=7, nm=8, dbg=None, stop=None):
    nblk = nl + 1 + nm
    nc = bass.Bass("TRN2", target_bir_lowering=False)
    dr = {}
    for name, shp in PARAM_SHAPES:
        dr[name] = nc.dram_tensor(name, shp, F32, kind="ExternalInput").ap()
    xs = nc.dram_tensor("xs", [nblk * TB, D], F32, kind="ExternalInput").ap()
    flag = nc.dram_tensor("flag", [128, 1], F32, kind="ExternalInput").ap()
    out = nc.dram_tensor("out", [nm * TB, D], F32, kind="ExternalOutput").ap()
    x1d = nc.dram_tensor("x1d", [128 + nm * TB, D], F32, kind="Internal").ap()
    dbg_t = {}
    if dbg:
        for name, shp in dbg.items():
            dt_ = F32
            if shp and shp[0] == "bf16":
                dt_ = BF16
                shp = shp[1:]
            dbg_t[name] = nc.dram_tensor("dbg_" + name, shp, dt_, kind="ExternalOutput").ap()

    es = ExitStack()
    with es:
        ARENA_F32 = 53000
        arena = es.enter_context(nc.sbuf_tensor("arena", [128, ARENA_F32], F32))
        psum_all = es.enter_context(nc.psum_tensor("psum_all", [128, 8, 512], F32))
        semkeys = ["pe", "act", "dve", "pool", "sp", "d_par", "d_x0", "d_x1", "d_x2", "d_x3",
                   "d_w0", "d_w1", "d_w2", "d_w3", "d_w4", "d_w5", "d_w6", "d_w7", "d_w8", "d_wo",
                   "d_st0", "d_st1", "d_st2", "d_st3", "d_dbg", "d_fn", "d_wdn0", "d_wdn1"] + [f"d_wup{i}" for i in range(NFC // 2)]
        sems = {k: es.enter_context(nc.semaphore(k)) for k in semkeys}
        block = es.enter_context(nc.Block())
        P = Prog(nc, sems, group_sems=("d_par", "d_w7", "d_dbg"))
        if PROFILE:
            P.profile = []

        st = {"off": 0}

        def carve(name, free_shape, dt, parts=128, sem=None):
            n = int(np.prod(free_shape))
            nf32 = (n + 1) // 2 if dt == BF16 else n
            nf32 = (nf32 + 3) // 4 * 4
            off = st["off"]
            st["off"] = off + nf32
            assert st["off"] <= ARENA_F32, (name, st["off"])
            v = arena[:, off:off + nf32]
            if dt == BF16:
                v = v.bitcast(BF16)[:, 0:n]
            else:
                v = v[:, 0:n]
            if len(free_shape) == 2:
                v = v.rearrange("p (a b) -> p a b", b=free_shape[1])
            elif len(free_shape) == 3:
                v = v.rearrange("p (a b c) -> p a b c", b=free_shape[1], c=free_shape[2])
            if parts != 128:
                v = v[0:parts]
            return Tl(v, name, sem)

        carve_off = {}

        def vring(name, n, free_shape, dt):
            tiles = []
            bases = []
            for i in range(n):
                bases.append(st["off"])
                tiles.append(carve(f"{name}{i}", free_shape, dt))
            return VRing(name, tiles, bases)

        def ring(name, n, free_shape, dt, sems_=None):
            return Ring([carve(f"{name}{i}", free_shape, dt, sem=(sems_[i] if sems_ else None)) for i in range(n)],
                        name=name)

        VS32 = 1 << 24
        ps_b0 = psum_all[:, 0, :]
        v_cnt = [0]
        v_info = {}
        P.class_slots["ps"] = 8

        def valloc(cname, slot0_ap, bases, name):
            k = v_cnt[0]
            v_cnt[0] += 1
            t = Tl(AP(slot0_ap.tensor, slot0_ap.offset + (k + 1) * VS32 * (4 // (2 if slot0_ap.dtype == BF16 else 4)),
                      slot0_ap.ap), f"{name}{k}")
            t.buf.valloc = k
            P.alloc_class[k] = cname
            v_info[k] = bases
            return t

        PS_BASES = [b * 512 for b in range(8 - PSH_BANKS)]
        PSH_BASES = [b * 512 + h * 256 for b in range(8 - PSH_BANKS, 8) for h in range(2)]
        P.class_slots["ps"] = 8 - PSH_BANKS
        if PSH_BANKS:
            P.class_slots["psh"] = 2 * PSH_BANKS
            psh_0 = psum_all[:, 8 - PSH_BANKS, 0:256]

        def psum(kind="full"):
            if kind == "half" and PSH_BANKS:
                return valloc("psh", psh_0, PSH_BASES, "psh")
            return valloc("ps", ps_b0, PS_BASES, "psv")

        class VRing:
            def __init__(self, name, tiles, bases):
                self.name = name
                self.t0 = tiles[0]
                self.bases = bases
                P.class_slots[name] = len(tiles)

            def next(self):
                return valloc(self.name, self.t0.ap, self.bases, self.name + "v")

        def RB(ap):
            if ap is None or not hasattr(ap, "name") or ap.name not in ("psum_all", "arena"):
                return ap
            mul = 2 if ap.dtype == BF16 else 1
            vs = VS32 * mul
            k = ap.offset // vs - 1
            if k < 0:
                return ap
            real = ap.offset % vs
            bases = v_info[k]
            return AP(ap.tensor, real + (bases[P.place[k]] - bases[0]) * mul, ap.ap)

        def fsz(ap):
            return int(np.prod(ap.shape[1:]))

        def d_act(o):
            return 220.0 + 0.72 * fsz(o)

        def d_dve(o, k=1.0):
            return 70.0 + 1.05 * k * fsz(o)

        def act(o, i, func, reads, writes, **kw):
            P.op("act", lambda e: e.activation(out=RB(o), in_=RB(i), func=func, **kw), reads, writes,
                 dur=d_act(o) + (60.0 if "accum_out" in kw else 0.0))

        def tt(eng, o, a, b, op, reads, writes):
            P.op(eng, lambda e: e.tensor_tensor(out=RB(o), in0=RB(a), in1=RB(b), op=op), reads, writes,
                 dur=(d_dve(o) if eng != "pool" else 150.0 + 2.3 * fsz(o)))

        def ts(eng, o, a, s1, s2, op0, op1, reads, writes):
            if s2 is None:
                P.op(eng, lambda e: e.tensor_scalar(out=RB(o), in0=RB(a), scalar1=s1, scalar2=None, op0=op0), reads, writes,
                     dur=d_dve(o))
            else:
                P.op(eng, lambda e: e.tensor_scalar(out=RB(o), in0=RB(a), scalar1=s1, scalar2=s2, op0=op0, op1=op1), reads, writes,
                     dur=d_dve(o))

        def stt(o, a, s, b, op0, op1, reads, writes):
            P.op("dve", lambda e: e.scalar_tensor_tensor(out=RB(o), in0=RB(a), scalar=s, in1=RB(b), op0=op0, op1=op1), reads, writes,
                 dur=d_dve(o))

        def cp(eng, o, i, reads, writes):
            if eng == "act":
                P.op("act", lambda e: e.activation(out=RB(o), in_=RB(i), func=AF.Identity), reads, writes, dur=d_act(o))
            else:
                P.op(eng, lambda e: e.tensor_copy(out=RB(o), in_=RB(i)), reads, writes, dur=d_dve(o))

        def mm(o, l, r, start, stop, reads, writes, inc=True):
            P.op("pe", lambda e: e.matmul(out=RB(o), lhsT=RB(l), rhs=RB(r), start=start, stop=stop), reads, writes,
                 dur=max(64, fsz(r)) * 0.45 + 12.0)

        def tr(o, i, ident, reads, writes, inc=True):
            P.op("pe", lambda e: e.transpose(out=RB(o), in_=RB(i), identity=ident), reads, writes, dur=70.0)

        def memset(eng, ap, val, writes):
            P.op(eng, lambda e: e.memset(ap, val), (), writes, dur=200.0 + fsz(ap))

        def asel(o, i, pattern, cmp, fill, base, cm, reads, writes):
            P.op("pool", lambda e: e.affine_select(out=o, in_=i, pattern=pattern, compare_op=cmp, fill=fill,
                                                   base=base, channel_multiplier=cm), reads, writes, dur=1000.0)

        def scan(o, d0, d1, init, op0, op1, reads, writes):
            P.op("dve", lambda e: e.tensor_tensor_scan(out=RB(o), data0=RB(d0), data1=RB(d1), initial=init, op0=op0, op1=op1),
                 reads, writes, dur=d_dve(o, 2.0))

        def ck(name):
            if stop == name:
                raise StopBuild()

        def dbg_dump(name, tl_ap, reads):
            if name in dbg_t:
                P.dma("sp", "d_dbg", dbg_t[name], tl_ap, reads=reads)

        dumped = set()

        def dd(name, tl, ap=None, when=True):
            if name in dbg_t and name not in dumped and when:
                dumped.add(name)
                P.dma("sp", "d_dbg", dbg_t[name], tl.ap if ap is None else ap, reads=[tl])

        identf = carve("identf", [128], F32)
        identb = carve("identb", [128], BF16)
        ones = carve("ones", [128], F32)
        zeros = carve("zeros", [128], F32)
        negh = carve("negh", [TB], F32)
        mUI = carve("mUI", [128], BF16)
        mRW = carve("mRW", [4, 128], BF16)
        mL = carve("mL", [128], BF16)
        blk = carve("blk", [128], F32)
        PT = carve("PT", [70], F32)
        CW = carve("CW", [4, NFC * 2], F32)
        DP = carve("DP", [56], F32)
        flg = carve("flg", [1], F32)
        S_hg = carve("S_hg", [4, 128], F32)
        Sb_hg = carve("Sb_hg", [4, 128], BF16)
        H_rw = carve("H_rw", [4, 64], F32)
        Hb_rw = carve("Hb_rw", [4, 64], BF16)
        glob_end = st["off"]

        memset("pool", identf.ap, 0.0, [identf])
        asel(identf.ap, identf.ap, [[-1, 128]], ALU.not_equal, 1.0, 0, 1, [identf], [identf])
        cp("dve", identb.ap, identf.ap, [identf], [identb])
        memset("pool", ones.ap, 1.0, [ones])
        memset("pool", zeros.ap, 0.0, [zeros])
        memset("pool", negh.ap, -0.5, [negh])
        memset("pool", S_hg.ap, 0.0, [S_hg])
        memset("pool", Sb_hg.ap, 0.0, [Sb_hg])
        memset("pool", H_rw.ap, 0.0, [H_rw])
        memset("pool", Hb_rw.ap, 0.0, [Hb_rw])
        tmpm = carve("tmpm", [128], F32)
        memset("pool", tmpm.ap, 1.0, [tmpm])
        asel(tmpm.ap, tmpm.ap, [[1, 128]], ALU.is_ge, 0.0, 0, -1, [tmpm], [tmpm])
        cp("dve", mRW.ap[:, 1, :], tmpm.ap, [tmpm], [mRW])
        cp("dve", mRW.ap[:, 3, :], tmpm.ap, [tmpm], [mRW])
        asel(tmpm.ap[:, 64:128], tmpm.ap[:, 64:128], [[0, 64]], ALU.is_ge, 0.0, -64, 1, [tmpm], [tmpm])
        cp("dve", mUI.ap, tmpm.ap, [tmpm], [mUI])
        memset("pool", tmpm.ap, 1.0, [tmpm])
        asel(tmpm.ap, tmpm.ap, [[1, 128]], ALU.is_gt, 0.0, 0, -1, [tmpm], [tmpm])
        cp("dve", mRW.ap[:, 0, :], tmpm.ap, [tmpm], [mRW])
        cp("dve", mRW.ap[:, 2, :], tmpm.ap, [tmpm], [mRW])
        memset("pool", tmpm.ap, 1.0, [tmpm])
        asel(tmpm.ap, tmpm.ap, [[-1, 128]], ALU.is_gt, 0.0, 0, 1, [tmpm], [tmpm])
        cp("dve", mL.ap, tmpm.ap, [tmpm], [mL])
        memset("pool", blk.ap, 1.0, [blk])
        asel(blk.ap[:, 0:64], blk.ap[:, 0:64], [[0, 64]], ALU.is_ge, 0.0, 63, -1, [blk], [blk])
        asel(blk.ap[:, 64:128], blk.ap[:, 64:128], [[0, 64]], ALU.is_ge, 0.0, -64, 1, [blk], [blk])

        PMa = carve("PMa", [128], F32)
        PMc = carve("PMc", [4, 128], F32)
        rows = [("norm1_w", None, 8), ("norm2_w", None, 8), ("hg_lb_logits", 0, 4), ("hg_lb_logits", 1, 4),
                ("hg_norm_w", None, 4), ("rw_shift_mu", None, 14), ("rw_w0", None, 4), ("rw_a0", None, 4),
                ("rw_k_k", None, 4), ("rw_k_a", None, 4), ("rw_r_k", None, 4), ("rw_ln_w", None, 4),
                ("rw_ln_b", None, 4)]
        r0 = 0
        for name, idx, n in rows:
            src = dr[name] if idx is None else dr[name][idx]
            P.dma("sp", "d_par", PMa.ap[r0:r0 + n, :], src.rearrange("(r p) -> r p", p=128), writes=[PMa])
            r0 += n
        assert r0 == 70
        for j in range(3):
            P.dma("sp", "d_par", PMc.ap[0:2 * NFC, j, :], dr["conv_w"][j].rearrange("(r p) -> r p", p=128), writes=[PMc])
        P.dma("sp", "d_par", PMc.ap[0:2 * NFC, 3, :], dr["conv_b"].rearrange("(r p) -> r p", p=128), writes=[PMc])
        P.dma("sp", "d_par", flg.ap, flag, writes=[flg])
        pp_ = psum()
        tr(pp_.ap[:, 0:70], PMa.ap[0:70, :], identf.ap[0:70, 0:70], [PMa, identf], [pp_])
        cp("dve", PT.ap, pp_.ap[:, 0:70], [pp_], [PT])
        pp_ = psum()
        for j in range(4):
            tr(pp_.ap[:, j * 44:(j + 1) * 44], PMc.ap[0:44, j, :], identf.ap[0:44, 0:44], [PMc, identf], [pp_], inc=(j == 3))
        cp("dve", CW.ap.rearrange("p a b -> p (a b)"), pp_.ap[:, 0:176], [pp_], [CW])
        nw1T = PT.ap[:, 0:8]
        nw2T = PT.ap[:, 8:16]
        hgnw = PT.ap[:, 24:28]
        muT = PT.ap[:, 28:42]
        w0T = PT.ap[:, 42:46]
        a0T = PT.ap[:, 46:50]
        kkT = PT.ap[:, 50:54]
        kaT = PT.ap[:, 54:58]
        rkT = PT.ap[:, 58:62]
        lnwT = PT.ap[:, 62:66]
        lnbT = PT.ap[:, 66:70]
        lbT = DP.ap[:, 0:4]
        lb1T = DP.ap[:, 4:8]
        cq2 = DP.ap[:, 8:12]
        hgnw2 = DP.ap[:, 12:16]
        hw0 = DP.ap[:, 16:20]
        ha0 = DP.ap[:, 20:24]
        hka = DP.ap[:, 24:28]
        nhka = DP.ap[:, 28:32]
        dtmp = DP.ap[:, 32:36]
        thl = DP.ap[:, 36:40]
        tt("dve", dtmp, PT.ap[:, 16:20], PT.ap[:, 20:24], ALU.subtract, [PT], [DP])
        act(thl, dtmp, AF.Exp, [DP], [DP], scale=-1.0)
        ts("dve", thl, thl, 1.0, None, ALU.add, None, [DP], [DP])
        P.op("dve", lambda e: e.reciprocal(out=lbT, in_=thl), [DP], [DP])
        ts("dve", lb1T, lbT, -1.0, 1.0, ALU.mult, ALU.add, [DP], [DP])
        ts("dve", cq2, lb1T, -(128 ** -0.5), None, ALU.mult, None, [DP], [DP])
        ts("dve", hw0, w0T, -1.0, None, ALU.mult, None, [PT], [DP])
        omT = DP.ap[:, 40:54]
        ts("dve", omT, muT, -1.0, 1.0, ALU.mult, ALU.add, [PT], [DP])
        ts("dve", ha0, a0T, -1.0, None, ALU.mult, None, [PT], [DP])
        if "PT" in dbg_t:
            dbg_dump("PT", PT.ap, [PT])

        if stop == "setup":
            P.barrier()
            P.replay(block)
            return nc, P
        P.barrier()
        st["off"] = glob_end
        w_in = carve("w_in", [8, 3840], BF16)
        w_out = carve("w_out", [8, 1024], BF16)
        w2b = carve("w2b", [512], BF16)
        a2b = carve("a2b", [512], BF16)
        g2b = carve("g2b", [512], BF16)
        wgroups = [(512, 1024), (1024, 1536), (0, 512), (1536, 2048), (3584, 3840), (2560, 3072), (3072, 3584), (2048, 2560)]
        wbufs = []
        wsrc = dr["w_in"].rearrange("(kc p) c -> p kc c", p=128)
        for gi, (c0, c1) in enumerate(wgroups):
            b = Buf(f"win{gi}")
            wbufs.append((c0, c1, b))
            P.dma("pool", f"d_w{gi if gi < 7 else 8}", w_in.ap[:, :, c0:c1], wsrc[:, :, c0:c1], writes=[b])
        P.dma("pool", "d_w7", w2b.ap[0:64, :], dr["rw_w2"], writes=[w2b])
        P.dma("pool", "d_w7", a2b.ap[64:128, :], dr["rw_a2"], writes=[a2b])
        P.dma("pool", "d_w7", g2b.ap, dr["rw_g2"], writes=[g2b])
        P.dma("pool", "d_wo", w_out.ap, dr["w_out"].rearrange("(kc p) c -> p kc c", p=128), writes=[w_out])

        def wbuf_of(c):
            col = c * 128
            for c0, c1, b in wbufs:
                if c0 <= col < c1:
                    return b
            raise AssertionError

        xring = ring("xt", 2, [D], F32, ["d_x0", "d_x1"])
        xnr = ring("xn", 2, [D], BF16)
        sm = ring("sm", 8, [16], F32)
        hT = carve("hT", [8, TB + 1], BF16)
        memset("pool", hT.ap, 0.0, [hT])
        tf = ring("tf", NTF, [TB + 4], F32)
        tb = ring("tb", 4, [TB], BF16)
        hg_kT = [carve(f"hg_kT{h}", [TB], BF16) for h in range(4)]
        hg_ktok = [carve(f"hg_ktok{h}", [NS, 128], BF16) for h in range(4)]
        hg_vtok = [carve(f"hg_vtok{h}", [NS, 128], BF16) for h in range(4)]
        hg_qT = [carve(f"hg_qT{h}", [TB], BF16) for h in range(4)]
        hg_sgT = [carve(f"hg_sgT{h}", [TB], BF16) for h in range(4)]
        hg_ebl = carve("hg_ebl", [4, TB // 64], F32)
        rw_kT = [carve(f"rw_kT{c}", [TB], BF16) for c in range(4)]
        rw_bT = [carve(f"rw_bT{c}", [TB], BF16) for c in range(4)]
        rw_ar = [carve(f"rw_ar{c}", [NS * 2, 2, 128], BF16) for c in range(4)]
        rw_ktok = [carve(f"rw_ktok{c}", [NS, 128], BF16) for c in range(4)]
        rw_btok = [carve(f"rw_btok{c}", [NS, 128], BF16) for c in range(4)]
        rw_vtok = [carve(f"rw_vtok{c}", [NS, 128], BF16) for c in range(4)]
        rw_gT = [carve(f"rw_gT{c}", [TB], BF16) for c in range(4)]
        rw_bon = [carve(f"rw_bon{c}", [TB], BF16) for c in range(4)]
        rw_egx = [carve(f"rw_egx{c}", [NS, 129], F32) for c in range(4)]
        twd = carve("twd", [TB], BF16)
        adT = carve("adT", [TB], BF16)
        sgd = carve("sgd", [TB], BF16)
        ATh = ring("ATh", 2, [4, 128], BF16)
        Pq = [carve(f"Pq{i}", [4, 128], BF16) for i in range(2)] * NS
        PTq = [carve(f"PTq{i}", [4, 128], BF16) for i in range(2)] * NS
        TTq = [carve(f"TTq{i}", [4, 128], BF16) for i in range(2)] * NS
        Rb = carve("Rb", [512], BF16)
        Ub = carve("Ub", [512], BF16)
        ysb = ring("ysb", 2, [512], F32)
        ysq = carve("ysq", [512], F32)
        ynb = ring("ynb", 2, [512], BF16)
        oT = ring("oT", 2, [8, 128], BF16)
        x1t = ring("x1t", 1, [D], F32, ["d_st0"])
        for c in range(4):
            memset("pool", rw_egx[c].ap, 1.0, [rw_egx[c]])
            memset("pool", rw_ar[c].ap, 0.0, [rw_ar[c]])
        ATb = [carve(f"ATb{c}", [4, 128], BF16) for c in range(4)]
        ATk = [carve(f"ATk{c}", [4, 128], BF16) for c in range(4)]

        def arv(c, s, j, w):
            return rw_ar[c].ap[:, s * 2 + j, w, :]

        def rsqrt_small(dst_ap, dst_reads, src_ap, n, scale, eps, reads, writes):
            t_ = sm.next()
            act(t_.ap[:, 0:n], src_ap, AF.Ln, reads, [t_], scale=scale, bias=eps)
            act(dst_ap, t_.ap[:, 0:n], AF.Exp, [t_] + list(dst_reads), writes, scale=-0.5)

        def sig3(dst_ap, dst_tl, src_ap, reads, bias=None):
            e_ = tf.next()
            shp = [src_ap.shape[0], int(np.prod(src_ap.shape[1:]))]
            ev = e_.ap[0:shp[0], 0:shp[1]]
            if bias is None:
                act(ev, src_ap, AF.Exp, reads, [e_], scale=-1.0)
            else:
                act(ev, src_ap, AF.Exp, list(reads) + [DP], [e_], scale=-1.0, bias=bias)
            act(ev, ev, AF.Ln, [e_], [e_], bias=1.0)
            act(dst_ap, ev, AF.Exp, [e_], [dst_tl], scale=-1.0)

        def load_norm_T(src_rows_ap, nwT, hT_tile, s, xt=None, src_reads=(), off=0):
            if xt is None:
                xt = xring.next()
                P.dma("sp", xt.sem, xt.ap, src_rows_ap, reads=src_reads, writes=[xt])
            ss = sm.next()
            xn = xnr.next()
            act(xn.ap, xt.ap, AF.Square, [xt], [xn, ss], accum_out=ss.ap[:, 0:1])
            rs = sm.next()
            rsqrt_small(rs.ap[:, 0:1], [], ss.ap[:, 0:1], 1, 1.0 / D, 1e-6, [ss], [rs])
            ts("dve", xn.ap, xt.ap, rs.ap[:, 0:1], None, ALU.mult, None, [xt, rs], [xn])
            dd('ss', ss, ap=ss.ap[:, 0:1]); dd('rs', rs, ap=rs.ap[:, 0:1]); dd('xn', xn); dd('xt', xt)
            pb = psum()
            pbv = pb.ap.bitcast(BF16).rearrange("p (a b) -> p a b", b=128)
            for kc in range(8):
                tr(pbv[:, kc, :], xn.ap[:, kc * 128:(kc + 1) * 128], identb.ap, [xn, identb], [pb], inc=(kc == 7))
            tt("dve", hT_tile.ap[:, :, off + s * 128:off + (s + 1) * 128], pbv, nwT.unsqueeze(2).to_broadcast([128, 8, 128]),
               ALU.mult, [pb, PT], [hT_tile])
            return xt

        def proj(c, halo=False):
            pb = psum()
            wb = wbuf_of(c)
            for kc in range(8):
                if halo:
                    mm(pb.ap[:, 0:TB + 1], w_in.ap[:, kc, c * 128:(c + 1) * 128], hT.ap[:, kc, :], kc == 0, kc == 7,
                       [wb, hT], [pb], inc=(kc == 7))
                else:
                    mm(pb.ap[:, 0:TB], w_in.ap[:, kc, c * 128:(c + 1) * 128], hT.ap[:, kc, 1:TB + 1], kc == 0, kc == 7,
                       [wb, hT], [pb], inc=(kc == 7))
            return pb

        def to_tok(srcT, dst, eng="act"):
            pb = psum("half")
            pbv = pb.ap.bitcast(BF16)[:, 0:NS * 128].rearrange("p (a b) -> p a b", b=128)
            for s in range(NS):
                tr(pbv[:, s, :], srcT.ap[:, s * 128:(s + 1) * 128], identb.ap, [srcT, identb], [pb], inc=(s == NS - 1))
            cp(eng, dst.ap, pbv, [pb], [dst])


        def mixer_block(bi, full, emit_sub0):
            cp("dve", hT.ap[:, :, 0:1], hT.ap[:, :, TB:TB + 1], [hT], [hT])
            for s in range(NS):
                g = bi * NS + s
                load_norm_T(xs[g * 128:(g + 1) * 128, :], nw1T, hT, s, off=1)
            dd('hT', hT, when=(bi == nl))
            ck('A')
            for h in range(4):
                pf = proj(4 + h)
                thf = tf.next()
                sig3(thf.ap[:, 0:TB], thf, pf.ap[:, 0:TB], [pf])
                f_ = tf.next()
                ts("dve", f_.ap[:, 0:TB], thf.ap[:, 0:TB], lb1T[:, h:h + 1], lbT[:, h:h + 1], ALU.mult, ALU.add,
                   [thf, DP], [f_])
                eb = tf.next()
                for c4 in range(TB // 64):
                    sl = slice(c4 * 64, (c4 + 1) * 64)
                    scan(eb.ap[:, sl], f_.ap[:, sl], ones.ap[:, 0:64], 1.0, ALU.mult, ALU.mult, [f_, ones], [eb])
                enb = tf.next()
                P.op("dve", lambda e, o=enb.ap[:, 0:TB], i=eb.ap[:, 0:TB]: e.reciprocal(out=RB(o), in_=RB(i)), [eb], [enb], dur=340.0)
                stt(hg_kT[h].ap, thf.ap[:, 0:TB], 1.0, enb.ap[:, 0:TB], ALU.subtract, ALU.mult, [thf, enb], [hg_kT[h]])
                cp("act", hg_ebl.ap[:, h, :], eb.ap[:, 63:TB:64], [eb], [hg_ebl])
                kh = tb.next()
                tt(POOL_E, kh.ap.rearrange("p (a b) -> p a b", b=64), hg_kT[h].ap.rearrange("p (a b) -> p a b", b=64),
                   hg_ebl.ap[:, h, :].unsqueeze(2).to_broadcast([128, TB // 64, 64]), ALU.mult,
                   [hg_kT[h], hg_ebl], [kh])
                to_tok(kh, hg_ktok[h])
                pi = proj(8 + h)
                vT = tb.next()
                cp("act", vT.ap, pi.ap[:, 0:TB], [pi], [vT])
                to_tok(vT, hg_vtok[h])
                if full:
                    pq = proj(h)
                    thq = tf.next()
                    sig3(thq.ap[:, 0:TB], thq, pq.ap[:, 0:TB], [pq])
                    s1 = tf.next()
                    tt("dve", s1.ap[:, 0:TB], thq.ap[:, 0:TB], pq.ap[:, 0:TB], ALU.mult, [thq, pq], [s1])
                    stt(hg_qT[h].ap, s1.ap[:, 0:TB], cq2[:, h:h + 1], eb.ap[:, 0:TB], ALU.mult, ALU.mult,
                        [s1, eb, DP], [hg_qT[h]])
                    pg = proj(12 + h)
                    thg = tf.next()
                    sig3(thg.ap[:, 0:TB], thg, pg.ap[:, 0:TB], [pg])
                    tt("dve", hg_sgT[h].ap, thg.ap[:, 0:TB], pg.ap[:, 0:TB], ALU.mult, [thg, pg], [hg_sgT[h]])

            dd('hg_kT0', hg_kT[0], when=(bi == nl)); dd('hg_qT0', hg_qT[0], when=(bi == nl)); dd('hg_vtok0', hg_vtok[0], when=(bi == nl)); dd('hg_ktok0', hg_ktok[0], when=(bi == nl)); dd('hg_sgT0', hg_sgT[0], when=(bi == nl)); dd('hg_ebl', hg_ebl, when=(bi == nl))
            ck('hgprep')
            def shift_mix(pb, ci, out_ap, out_tl, rows=slice(0, 128)):
                a1_ = tf.next()
                act(a1_.ap[:, 0:TB], pb.ap[:, 1:TB + 1], AF.Identity, [pb, DP], [a1_], scale=omT[:, ci:ci + 1])
                stt(out_ap, pb.ap[rows, 0:TB], muT[rows, ci:ci + 1], a1_.ap[rows, 0:TB], ALU.mult, ALU.add,
                    [pb, a1_, PT], [out_tl])

            pl = proj(28, halo=True)
            lo = tf.next()
            shift_mix(pl, 12, lo.ap[:, 0:TB], lo)
            e_ = tf.next()
            act(e_.ap[0:64, 0:TB], lo.ap[0:64, 0:TB], AF.Exp, [lo], [e_], scale=-2.0)
            act(e_.ap[0:64, 0:TB], e_.ap[0:64, 0:TB], AF.Ln, [e_], [e_], bias=1.0)
            act(e_.ap[0:64, 0:TB], e_.ap[0:64, 0:TB], AF.Exp, [e_], [e_], scale=-1.0)
            ts("dve", twd.ap[0:64, :], e_.ap[0:64, 0:TB], 2.0, -1.0, ALU.mult, ALU.add, [e_], [twd])
            cp("act", adT.ap[64:128, :], lo.ap[64:128, 0:TB], [lo], [adT])
            if full:
                pg_ = proj(29, halo=True)
                gdm = tf.next()
                shift_mix(pg_, 13, gdm.ap[:, 0:TB], gdm)
                sig3(sgd.ap, sgd, gdm.ap[:, 0:TB], [gdm])
            for c in range(4):
                cs = slice(c * 128, (c + 1) * 128)
                pw = psum("half")
                mm(pw.ap[:, 0:TB], w2b.ap[0:64, cs], twd.ap[0:64, :], True, True, [w2b, twd], [pw])
                ld = tf.next()
                sig3(ld.ap[:, 0:TB], ld, pw.ap[:, 0:TB], [pw], bias=hw0[:, c:c + 1])
                cld = float(np.exp(-0.5))
                lg = tf.next()
                for s in range(NS):
                    sl = slice(s * 128, (s + 1) * 128)
                    scan(lg.ap[:, sl], ld.ap[:, sl], zeros.ap, 0.0, ALU.add, ALU.add, [ld, zeros], [lg])
                act(rw_egx[c].ap[:, :, 1:129], lg.ap[:, 0:TB].rearrange("p (a b) -> p a b", b=128), AF.Exp,
                    [lg], [rw_egx[c]], scale=-cld)
                eng_t = tf.next()
                eng_ap = eng_t.ap[:, 0:TB]
                act(eng_ap, lg.ap[:, 0:TB], AF.Exp, [lg], [eng_t], scale=cld)
                pa = psum("half")
                mm(pa.ap[:, 0:TB], a2b.ap[64:128, cs], adT.ap[64:128, :], True, True, [a2b, adT], [pa])
                tha_t = tf.next()
                tha_ap = tha_t.ap[:, 0:TB]
                sig3(tha_ap, tha_t, pa.ap[:, 0:TB], [pa], bias=ha0[:, c:c + 1])
                if full:
                    pgm = psum("half")
                    mm(pgm.ap[:, 0:TB], g2b.ap[:, cs], sgd.ap, True, True, [g2b, sgd], [pgm])
                    cp("act", rw_gT[c].ap, pgm.ap[:, 0:TB], [pgm], [rw_gT[c]])
                pk = proj(20 + c, halo=True)
                kr = tf.next()
                shift_mix(pk, 4 + c, kr.ap[:, 0:TB], kr)
                kk = tf.next()
                ts("dve", kk.ap[:, 0:TB], kr.ap[:, 0:TB], kkT[:, c:c + 1], None, ALU.mult, None, [kr, PT], [kk])
                sq = tf.next()
                act(sq.ap[:, 0:TB], kk.ap[:, 0:TB], AF.Square, [kk], [sq])
                pss = psum("half")
                mm(pss.ap[:, 0:TB], blk.ap, sq.ap[:, 0:TB], True, True, [blk, sq], [pss])
                mx = tf.next()
                rn = tf.next()
                act(mx.ap[:, 0:TB], pss.ap[:, 0:TB], AF.Ln, [pss], [mx], bias=float(2.0 ** -60))
                act(rn.ap[:, 0:TB], mx.ap[:, 0:TB], AF.Exp, [mx], [rn], scale=-0.5)
                kkn = tf.next()
                tt(POOL_E, kkn.ap[:, 0:TB], kk.ap[:, 0:TB], rn.ap[:, 0:TB], ALU.mult, [kk, rn], [kkn])
                t1 = tf.next()
                ts("dve", t1.ap[:, 0:TB], tha_ap, 1.0, kaT[:, c:c + 1], ALU.subtract, ALU.mult,
                   [tha_t, PT], [t1])
                kp = tf.next()
                stt(kp.ap[:, 0:TB], t1.ap[:, 0:TB], 1.0, kr.ap[:, 0:TB], ALU.add, ALU.mult, [t1, kr], [kp])
                tt(POOL_E, rw_kT[c].ap, kp.ap[:, 0:TB], eng_ap, ALU.mult, [kp, eng_t], [rw_kT[c]])
                b1 = tf.next()
                tt(POOL_E, b1.ap[:, 0:TB], tha_ap, kkn.ap[:, 0:TB], ALU.mult, [tha_t, kkn], [b1])
                tt(POOL_E, rw_bT[c].ap, b1.ap[:, 0:TB], eng_ap, ALU.mult, [b1, eng_t], [rw_bT[c]])
                for j in range(2):
                    pj = slice(64 * j, 64 * j + 64)
                    stt(rw_ar[c].ap[pj, j:2 * NS:2, 0, :], kkn.ap[pj, 0:TB].rearrange("p (a b) -> p a b", b=128), -1.0,
                        rw_egx[c].ap[pj, :, 0:128], ALU.mult, ALU.mult, [kkn, rw_egx[c]], [rw_ar[c]])
                to_tok(rw_kT[c], rw_ktok[c])
                to_tok(rw_bT[c], rw_btok[c])
                pv = proj(24 + c, halo=True)
                vr = tf.next()
                shift_mix(pv, 8 + c, vr.ap[:, 0:TB], vr)
                vTb = tb.next()
                cp("act", vTb.ap, vr.ap[:, 0:TB], [vr], [vTb])
                to_tok(vTb, rw_vtok[c])
                if full:
                    pr = proj(16 + c, halo=True)
                    rr = tf.next()
                    shift_mix(pr, c, rr.ap[:, 0:TB], rr)
                    for j in range(2):
                        pj = slice(64 * j, 64 * j + 64)
                        tt(POOL_E, rw_ar[c].ap[pj, j:2 * NS:2, 1, :], rr.ap[pj, 0:TB].rearrange("p (a b) -> p a b", b=128),
                           rw_egx[c].ap[pj, :, 1:129], ALU.mult, [rr, rw_egx[c]], [rw_ar[c]])
                    rkr = tf.next()
                    stt(rkr.ap[:, 0:TB], rr.ap[:, 0:TB], rkT[:, c:c + 1], kp.ap[:, 0:TB], ALU.mult, ALU.mult,
                        [rr, kp, PT], [rkr])
                    pbs = psum("half")
                    mm(pbs.ap[:, 0:TB], blk.ap, rkr.ap[:, 0:TB], True, True, [blk, rkr], [pbs])
                    tt("dve", rw_bon[c].ap, pbs.ap[:, 0:TB], vr.ap[:, 0:TB], ALU.mult, [pbs, vr], [rw_bon[c]])

            dd('rw_kT0', rw_kT[0], when=(bi == nl)); dd('rw_bT0', rw_bT[0], when=(bi == nl)); dd('rw_ar0', rw_ar[0], when=(bi == nl)); dd('rw_egx0', rw_egx[0], when=(bi == nl)); dd('rw_vtok0', rw_vtok[0], when=(bi == nl)); dd('rw_gT0', rw_gT[0], when=(bi == nl)); dd('rw_bon0', rw_bon[0], when=(bi == nl));
            ck('rwprep')
            for s in range(NS):
                ssl = slice(s * 128, (s + 1) * 128)
                emit = full and (s >= emit_sub0)
                if emit:
                    pa = psum()
                    for h in range(4):
                        mm(pa.ap[:, h * 128:(h + 1) * 128], hg_kT[h].ap[:, ssl], hg_qT[h].ap[:, ssl], True, True,
                           [hg_kT[h], hg_qT[h]], [pa], inc=(h == 3))
                    ath = ATh.next()
                    tt("dve", ath.ap, pa.ap.rearrange("p (a b) -> p a b", b=128),
                       mUI.ap.unsqueeze(1).to_broadcast([128, 4, 128]), ALU.mult, [pa, mUI], [ath])
                    po = psum()
                for cc in range(2):
                    ps_ = slice(64 * cc, 64 * cc + 64)
                    ci = s * 2 + cc
                    if emit:
                        for h in range(4):
                            hs = slice(h * 128, (h + 1) * 128)
                            mm(po.ap[ps_, hs], hg_qT[h].ap[:, s * 128 + 64 * cc:s * 128 + 64 * cc + 64],
                               Sb_hg.ap[:, h, :], True, False, [hg_qT[h], Sb_hg], [po], inc=False)
                            mm(po.ap[ps_, hs], ath.ap[:, h, 64 * cc:64 * cc + 64], hg_vtok[h].ap[:, s, :],
                               False, True, [ath, hg_vtok[h]], [po], inc=(h == 3))
                    pS = psum()
                    for h in range(4):
                        hs = slice(h * 128, (h + 1) * 128)
                        mm(pS.ap[:, hs], hg_ktok[h].ap[ps_, s, :], hg_vtok[h].ap[ps_, s, :], True, True,
                           [hg_ktok[h], hg_vtok[h]], [pS], inc=(h == 3))
                    for h in range(4):
                        hs = slice(h * 128, (h + 1) * 128)
                        stt(S_hg.ap[:, h, :], S_hg.ap[:, h, :], hg_ebl.ap[:, h, ci:ci + 1], pS.ap[:, hs],
                            ALU.mult, ALU.add, [S_hg, hg_ebl, pS], [S_hg])
                    cp("act", Sb_hg.ap, S_hg.ap, [S_hg], [Sb_hg])
                if emit:
                    s2 = sm.next()
                    jk = ysb.next()
                    for h in range(4):
                        act(jk.ap[:, h * 128:(h + 1) * 128], po.ap[:, h * 128:(h + 1) * 128], AF.Square, [po], [jk, s2],
                            accum_out=s2.ap[:, h:h + 1])
                    rs = sm.next()
                    rsqrt_small(rs.ap[:, 0:4], [], s2.ap[:, 0:4], 4, 1.0 / 128, 1e-6, [s2], [rs])
                    on = ynb.next()
                    tt("dve", on.ap.rearrange("p (a b) -> p a b", b=128), po.ap.rearrange("p (a b) -> p a b", b=128),
                       rs.ap[:, 0:4].unsqueeze(2).to_broadcast([128, 4, 128]), ALU.mult, [po, rs], [on])
                    pt_ = psum("half")
                    ptv = pt_.ap.bitcast(BF16)[:, 0:512].rearrange("p (a b) -> p a b", b=128)
                    for h in range(4):
                        tr(ptv[:, h, :], on.ap[:, h * 128:(h + 1) * 128], identb.ap, [on, identb], [pt_], inc=(h == 3))
                    ot = oT.next()
                    for h in range(4):
                        stt(ot.ap[:, h, :], ptv[:, h, :], hgnw[:, h:h + 1], hg_sgT[h].ap[:, ssl], ALU.mult, ALU.mult,
                            [pt_, DP, hg_sgT[h]], [ot])
                if emit:
                    dd('on', on); dd('ot_hg', ot, ap=ot.ap[:, 0:4, :])
                dd('S_hg', S_hg, when=(bi == nl and s == NS - 1))
                ck('hgchain')
                for c in range(4):
                    for (lt, dst) in ((rw_bT[c], ATb[c]), (rw_kT[c], ATk[c])):
                        pA = psum()
                        if emit:
                            mm(pA.ap, lt.ap[:, ssl], rw_ar[c].ap[:, 2 * s:2 * s + 2, :, :], True, True, [lt, rw_ar[c]], [pA])
                            tt("dve", dst.ap, pA.ap.rearrange("p (a b) -> p a b", b=128), mRW.ap, ALU.mult, [pA, mRW], [dst])
                        else:
                            mm(pA.ap[:, 0:256], lt.ap[:, ssl], rw_ar[c].ap[:, 2 * s:2 * s + 2, 0, :], True, True,
                               [lt, rw_ar[c]], [pA])
                            tt("dve", dst.ap[:, 0:3:2, :], pA.ap[:, 0:256].rearrange("p (a b) -> p a b", b=128),
                               mRW.ap[:, 0:3:2, :], ALU.mult, [pA, mRW], [dst])
                for g4 in range(2):
                    pN = psum()
                    for hh in range(4):
                        h = g4 * 4 + hh
                        c, j = h // 2, h % 2
                        pj = slice(64 * j, 64 * j + 64)
                        mm(pN.ap[:, hh * 128:(hh + 1) * 128], arv(c, s, j, 0), rw_bT[c].ap[:, ssl], True, True,
                           [rw_ar[c], rw_bT[c]], [pN], inc=(hh == 3))
                    Pc = Pq[s * 2 + g4]
                    tt("dve", Pc.ap, pN.ap.rearrange("p (a b) -> p a b", b=128),
                       mL.ap.unsqueeze(1).to_broadcast([128, 4, 128]), ALU.mult, [pN, mL], [Pc])
                    TTc = TTq[s * 2 + g4]
                    for cc_ in range(2):
                        c_ = g4 * 2 + cc_
                        tt(POOL_E, TTc.ap[:, 2 * cc_:2 * cc_ + 2, :], ATb[c_].ap[:, 0:3:2, :],
                           identb.ap.unsqueeze(1).to_broadcast([128, 2, 128]), ALU.add, [ATb[c_], identb], [TTc])
                    PTc = None
                    nlev = 6
                    for lev in range(1, nlev + 1):
                        last = (lev == nlev)
                        pP = psum()
                        def ptv_(hh):
                            if PTc is None:
                                h = g4 * 4 + hh
                                return ATb[h // 2].ap[:, 2 * (h % 2), :], ATb[h // 2]
                            return PTc.ap[:, hh, :], PTc
                        for hh in range(4):
                            pa_, pt_l = ptv_(hh)
                            mm(pP.ap[:, hh * 128:(hh + 1) * 128], pa_, Pc.ap[:, hh, :], True, True,
                               [pt_l, Pc], [pP], inc=(hh == 3))
                        if not last:
                            pPT = psum()
                            for hh in range(4):
                                pa_, pt_l = ptv_(hh)
                                mm(pPT.ap[:, hh * 128:(hh + 1) * 128], Pc.ap[:, hh, :], pa_, True, True,
                                   [pt_l, Pc], [pPT], inc=(hh == 3))
                        Pn = Pc
                        cp("act", Pn.ap, pP.ap.rearrange("p (a b) -> p a b", b=128), [pP], [Pn])
                        if not last:
                            PTn = PTq[s * 2 + g4]
                            cp("act", PTn.ap, pPT.ap.rearrange("p (a b) -> p a b", b=128), [pPT], [PTn])
                        pT2 = psum()
                        for hh in range(4):
                            mm(pT2.ap[:, hh * 128:(hh + 1) * 128], Pn.ap[:, hh, :], TTc.ap[:, hh, :], True, True,
                               [Pn, TTc], [pT2], inc=(hh == 3))
                        TTn = TTc
                        tt("dve", TTn.ap, pT2.ap.rearrange("p (a b) -> p a b", b=128), TTc.ap, ALU.add, [pT2, TTc], [TTn])
                        Pc = Pn
                        if not last:
                            PTc = PTn
                        TTc = TTn
                dd('ATb0', ATb[0], when=(bi == nl and s == NS - 1)); dd('ATk0', ATk[0], when=(bi == nl and s == NS - 1))
                ck('rwinv')
                pR = psum()
                for h in range(8):
                    c, j = h // 2, h % 2
                    pj = slice(64 * j, 64 * j + 64)
                    hs = slice(h * 64, (h + 1) * 64)
                    mm(pR.ap[:, hs], arv(c, s, j, 0), Hb_rw.ap[:, c, :], True, False, [rw_ar[c], Hb_rw], [pR],
                       inc=False)
                    mm(pR.ap[:, hs], ATk[c].ap[:, 2 * j, :], rw_vtok[c].ap[:, s, 64 * j:64 * j + 64], False, True,
                       [ATk[c], rw_vtok[c]], [pR], inc=(h == 7))
                cp("act", Rb.ap, pR.ap, [pR], [Rb])
                pU = psum()
                for h in range(8):
                    hs = slice(h * 64, (h + 1) * 64)
                    mm(pU.ap[:, hs], TTq[s * 2 + h // 4].ap[:, h % 4, :], Rb.ap[:, hs], True, True, [TTq[s * 2 + h // 4], Rb], [pU],
                       inc=(h == 7))
                cp("act", Ub.ap, pU.ap, [pU], [Ub])
                if emit:
                    pY = psum()
                    for h in range(8):
                        c, j = h // 2, h % 2
                        pj = slice(64 * j, 64 * j + 64)
                        hs = slice(h * 64, (h + 1) * 64)
                        mm(pY.ap[:, hs], arv(c, s, j, 1), Hb_rw.ap[:, c, :], True, False, [rw_ar[c], Hb_rw],
                           [pY], inc=False)
                        mm(pY.ap[:, hs], ATb[c].ap[:, 2 * j + 1, :], Ub.ap[:, hs], False, False, [ATb[c], Ub], [pY], inc=False)
                        mm(pY.ap[:, hs], ATk[c].ap[:, 2 * j + 1, :], rw_vtok[c].ap[:, s, 64 * j:64 * j + 64], False, True,
                           [ATk[c], rw_vtok[c]], [pY], inc=(h == 7))
                pH = psum()
                for c in range(4):
                    cs_ = slice(c * 128, (c + 1) * 128)
                    mm(pH.ap[:, cs_], rw_btok[c].ap[:, s, :], Ub.ap[:, cs_], True, False, [rw_btok[c], Ub], [pH], inc=False)
                    mm(pH.ap[:, cs_], rw_ktok[c].ap[:, s, :], rw_vtok[c].ap[:, s, :], False, True,
                       [rw_ktok[c], rw_vtok[c]], [pH], inc=(c == 3))
                ht = tf.next()
                htv = ht.ap[:, 0:256].rearrange("p (a b) -> p a b", b=64)
                for j in range(2):
                    pj = slice(64 * j, 64 * j + 64)
                    tt("dve", htv[pj], pH.ap[pj, :].rearrange("p (c x) -> p c x", x=128)[:, :, 64 * j:64 * j + 64],
                       H_rw.ap[pj], ALU.add, [pH, H_rw], [ht])
                for c in range(4):
                    ts("dve", H_rw.ap[:, c, :], htv[:, c, :], rw_egx[c].ap[:, s, 128:129], None, ALU.mult, None,
                       [ht, rw_egx[c]], [H_rw])
                cp("act", Hb_rw.ap, H_rw.ap, [H_rw], [Hb_rw])
                if emit:
                    dd('Ub', Ub); dd('Rb', Rb); dd('H_rw', H_rw)
                    ck('rwchain')
                    yb = ysb.next()
                    cp("act", yb.ap, pY.ap, [pY], [yb])
                    yv = yb.ap.rearrange("p (a b) -> p a b", b=64)
                    s1_ = sm.next()
                    P.op("dve", lambda e, o=s1_.ap[:, 0:8], i=yv: e.tensor_reduce(out=o, in_=i, axis=AX.X, op=ALU.add),
                         [yb], [s1_], dur=600.0)
                    act(ysq.ap, yb.ap, AF.Square, [yb], [ysq])
                    s2_ = sm.next()
                    P.op("dve", lambda e, o=s2_.ap[:, 0:8], i=ysq.ap.rearrange("p (a b) -> p a b", b=64):
                         e.tensor_reduce(out=o, in_=i, axis=AX.X, op=ALU.add), [ysq], [s2_], dur=600.0)
                    mean = sm.next()
                    ts("dve", mean.ap[:, 0:8], s1_.ap[:, 0:8], 1.0 / 64, None, ALU.mult, None, [s1_], [mean])
                    msq = sm.next()
                    tt("dve", msq.ap[:, 0:8], mean.ap[:, 0:8], mean.ap[:, 0:8], ALU.mult, [mean], [msq])
                    var = sm.next()
                    stt(var.ap[:, 0:8], s2_.ap[:, 0:8], 1.0 / 64, msq.ap[:, 0:8], ALU.mult, ALU.subtract, [s2_, msq], [var])
                    rs_ = sm.next()
                    rsqrt_small(rs_.ap[:, 0:8], [], var.ap[:, 0:8], 8, 1.0, 64e-5, [var], [rs_])
                    ycen = ysb.next()
                    tt("dve", ycen.ap.rearrange("p (a b) -> p a b", b=64), yv,
                       mean.ap[:, 0:8].unsqueeze(2).to_broadcast([128, 8, 64]), ALU.subtract, [yb, mean], [ycen])
                    yn = ynb.next()
                    tt("dve", yn.ap.rearrange("p (a b) -> p a b", b=64), ycen.ap.rearrange("p (a b) -> p a b", b=64),
                       rs_.ap[:, 0:8].unsqueeze(2).to_broadcast([128, 8, 64]), ALU.mult, [ycen, rs_], [yn])
                    pt2 = psum("half")
                    ptv2 = pt2.ap.bitcast(BF16)[:, 0:512].rearrange("p (a b) -> p a b", b=128)
                    for c in range(4):
                        tr(ptv2[:, c, :], yn.ap[:, c * 128:(c + 1) * 128], identb.ap, [yn, identb], [pt2], inc=(c == 3))
                    for c in range(4):
                        t_ = tf.next()
                        stt(t_.ap[:, 0:128], ptv2[:, c, :], lnwT[:, c:c + 1], rw_bon[c].ap[:, ssl], ALU.mult, ALU.add,
                            [pt2, PT, rw_bon[c]], [t_])
                        stt(ot.ap[:, 4 + c, :], t_.ap[:, 0:128], lnbT[:, c:c + 1], rw_gT[c].ap[:, ssl], ALU.add, ALU.mult,
                            [t_, PT, rw_gT[c]], [ot])
                    dd('yn', yn); dd('ot', ot)
                    ck('rwout')
                    g = bi * NS + s
                    po1 = psum()
                    po2 = psum()
                    for kc in range(8):
                        mm(po1.ap, ot.ap[:, kc, :], w_out.ap[:, kc, 0:512], kc == 0, kc == 7, [ot, w_out], [po1], inc=False)
                    for kc in range(8):
                        mm(po2.ap, ot.ap[:, kc, :], w_out.ap[:, kc, 512:1024], kc == 0, kc == 7, [ot, w_out], [po2],
                           inc=(kc == 7))
                    xo = x1t.next()
                    P.dma("sp", "d_st1", xo.ap, xs[g * 128:(g + 1) * 128, :], writes=[xo])
                    tt("dve", xo.ap[:, 0:512], po1.ap, xo.ap[:, 0:512], ALU.add, [po1, xo], [xo])
                    tt("dve", xo.ap[:, 512:1024], po2.ap, xo.ap[:, 512:1024], ALU.add, [po2, xo], [xo])
                    row = (g - (nl + 1) * NS + 1) * 128
                    P.dma("sp", xo.sem, x1d[row:row + 128, :], xo.ap, reads=[xo], writes=[x1d_bufs[row // 128]])

        x1d_bufs = [Buf(f"x1d{i}") for i in range(1 + nm * NS)]
        print("arena phase1 used f32 words", st["off"], "of", ARENA_F32)
        try:
            for bi in range(nblk):
                if bi < nl:
                    mixer_block(bi, False, NS)
                elif bi == nl:
                    mixer_block(bi, True, NS - 1)
                else:
                    mixer_block(bi, True, 0)
        except StopBuild:
            P.barrier()
            P.replay(block)
            return nc, P
        if "x1" in dbg_t:
            P.barrier()
            P.dma("sp", "d_dbg", dbg_t["x1"], x1d[:, :], reads=x1d_bufs)

        NPRE = 7
        st_save = st["off"]
        st["off"] = glob_end
        w_up = carve("w_up", [NFC // 2, 8, 512], BF16)
        assert NPRE * 8 * 512 // 2 <= 8 * 3840 // 2
        st["off"] = st_save
        wup_src = dr["w_up"].rearrange("(kc p) c -> p kc c", p=128)
        wupb = [Buf(f"wup{gi}") for gi in range(NFC // 2)]

        def load_wup(gi, extra_reads=()):
            c0 = gi * 256
            hb_ = [Buf(f"wup{gi}a"), Buf(f"wup{gi}b")]
            P.dma("pool", f"d_wup{gi}", w_up.ap[:, gi, :, 0:256], wup_src[:, :, c0:c0 + 256],
                  reads=list(extra_reads), writes=[hb_[0]])
            P.dma("pool", f"d_wup{gi}", w_up.ap[:, gi, :, 256:512], wup_src[:, :, DFF + c0:DFF + c0 + 256],
                  reads=list(extra_reads), writes=[hb_[1]])
            wupb[gi] = hb_

        if stop is None:
            fence = sm.next()
            memset("pool", fence.ap[:, 0:1], 0.0, [fence] + [b for (_, _, b) in wbufs])
            for gi in range(NPRE):
                load_wup(gi, [fence])

        P.barrier()
        if stop == "p1":
            P.replay(block)
            return nc, P
        st["off"] = glob_end
        w_up = carve("w_up", [NFC // 2, 8, 512], BF16)
        w_dn = carve("w_dn", [NFC, D], BF16)
        xring2 = ring("x2t", NS + 1, [D], F32, ["d_x0", "d_x1", "d_x2", "d_x3"][:NS + 1])
        xnr = ring("xn2_", 2, [D], BF16)
        sm = ring("sm2_", 8, [16], F32)
        h2Ts = [carve(f"h2T{i}", [8, TB + 2], BF16) for i in range(2)]
        gT = carve("gT", [NFC, TB], BF16)
        acg = ring("acg", 4, [TB], F32)
        acv = ring("acv", 4, [TB], F32)
        sgl = ring("sgl", 4, [TB], F32)
        xo2 = ring("xo2", 1, [D], F32)
        fnwb = carve("fnwb", [D], F32)
        P.dma("sp", "d_fn", fnwb.ap, dr["final_norm_w"].partition_broadcast(128), writes=[fnwb])
        yo = ring("yo", 2, [D], F32, ["d_st2", "d_st3"])
        xring = xring2
        for gi in range(NPRE, NFC // 2):
            load_wup(gi)
        wdn_src = dr["w_down"].rearrange("(c p) n -> p c n", p=128)
        wdnb = []
        for gi in range(2):
            b = Buf(f"wdn{gi}")
            wdnb.append(b)
            P.dma("pool", f"d_wdn{gi}", w_dn.ap[:, gi * 11:(gi + 1) * 11, :], wdn_src[:, gi * 11:(gi + 1) * 11, :], writes=[b])
        print("arena phase2 used f32 words", st["off"], "of", ARENA_F32)
        cw0 = CW.ap[:, 0, :]
        cw1 = CW.ap[:, 1, :]
        cw2 = CW.ap[:, 2, :]
        cbT = CW.ap[:, 3, :]

        xt0 = xring.next()
        P.dma("sp", xt0.sem, xt0.ap, x1d[0:128, :], reads=[x1d_bufs[0]], writes=[xt0])
        load_norm_T(None, nw2T, h2Ts[1], NS - 1, xt=xt0, off=2)
        ts("dve", h2Ts[1].ap[:, :, TB:TB + 2], h2Ts[1].ap[:, :, TB:TB + 2], flg.ap[:, 0:1], None, ALU.mult, None,
           [h2Ts[1], flg], [h2Ts[1]])

        for m in range(nm):
            x1ts = []
            h2T = h2Ts[m % 2]
            cp("dve", h2T.ap[:, :, 0:2], h2Ts[(m + 1) % 2].ap[:, :, TB:TB + 2], [h2Ts[(m + 1) % 2]], [h2T])
            for s in range(NS):
                row = 128 + (m * NS + s) * 128
                xt = xring.next()
                P.dma("sp", xt.sem, xt.ap, x1d[row:row + 128, :], reads=[x1d_bufs[row // 128]], writes=[xt])
                load_norm_T(None, nw2T, h2T, s, xt=xt, off=2)
                x1ts.append(xt)
            for c in range(NFC):
                res = []
                for (cc, ac_ring) in ((c, acg), (NFC + c, acv)):
                    pb = psum()
                    for kc in range(8):
                        lc = (0 if cc < NFC else 256) + (c % 2) * 128
                        mm(pb.ap[:, 0:TB + 2], w_up.ap[:, c // 2, kc, lc:lc + 128], h2T.ap[:, kc, :], kc == 0, kc == 7,
                           wupb[c // 2] + [h2T], [pb], inc=(kc == 7))
                    ac = ac_ring.next()
                    act(ac.ap, pb.ap[:, 2:TB + 2], AF.Identity, [pb, CW], [ac], scale=cw2[:, cc:cc + 1], bias=cbT[:, cc:cc + 1])
                    stt(ac.ap, pb.ap[:, 1:TB + 1], cw1[:, cc:cc + 1], ac.ap, ALU.mult, ALU.add, [pb, ac, CW], [ac])
                    stt(ac.ap, pb.ap[:, 0:TB], cw0[:, cc:cc + 1], ac.ap, ALU.mult, ALU.add, [pb, ac, CW], [ac])
                    res.append(ac)
                sg_ = sgl.next()
                act(sg_.ap, res[0].ap, AF.Silu, [res[0]], [sg_])
                tt("dve", gT.ap[:, c, :], sg_.ap, res[1].ap, ALU.mult, [sg_, res[1]], [gT])
            for s in range(NS):
                ssl = slice(s * 128, (s + 1) * 128)
                po1 = psum()
                po2 = psum()
                for c in range(NFC):
                    mm(po1.ap, gT.ap[:, c, ssl], w_dn.ap[:, c, 0:512], c == 0, c == NFC - 1, [gT, wdnb[c // 11]], [po1],
                       inc=False)
                for c in range(NFC):
                    mm(po2.ap, gT.ap[:, c, ssl], w_dn.ap[:, c, 512:1024], c == 0, c == NFC - 1, [gT, wdnb[c // 11]], [po2],
                       inc=(c == NFC - 1))
                xo = xo2.next()
                xt = x1ts[s]
                tt("dve", xo.ap[:, 0:512], po1.ap, xt.ap[:, 0:512], ALU.add, [po1, xt], [xo])
                tt("dve", xo.ap[:, 512:1024], po2.ap, xt.ap[:, 512:1024], ALU.add, [po2, xt], [xo])
                ss = sm.next()
                jk = xnr.next()
                act(jk.ap, xo.ap, AF.Square, [xo], [jk, ss], accum_out=ss.ap[:, 0:1])
                rs = sm.next()
                rsqrt_small(rs.ap[:, 0:1], [], ss.ap[:, 0:1], 1, 1.0 / D, 1e-6, [ss], [rs])
                y_ = yo.next()
                stt(y_.ap, xo.ap, rs.ap[:, 0:1], fnwb.ap, ALU.mult, ALU.mult, [xo, rs, fnwb], [y_])
                row = (m * NS + s) * 128
                P.dma("sp", y_.sem, out[row:row + 128, :], y_.ap, reads=[y_])
        P.barrier()
        P.replay(block)
    return nc, P


def _prep_inputs(inputs):
    sq = {}
    for name, shp in PARAM_SHAPES:
        a = np.asarray(inputs[name], dtype=np.float32)
        sq[name] = np.ascontiguousarray(a.reshape(shp))
    return sq


def kernel(**inputs):
    x = np.asarray(inputs["x"], dtype=np.float32)
    B, T, Dm = x.shape
    half = T // 2
    params = _prep_inputs(inputs)
    nl = half // TB - 1
    nm = half // TB
    nc, _ = build(nl, nm)
    in_maps = []
    for c in range(8):
        b, j = c // 2, c % 2
        xs = np.zeros((2 * half, Dm), np.float32)
        if j == 1:
            xs[:half] = x[b, :half]
        xs[half:] = x[b, j * half:(j + 1) * half]
        m = dict(params)
        m["xs"] = xs
        m["flag"] = np.full((128, 1), float(j), np.float32)
        in_maps.append(m)
    res = run_bass_kernel_spmd(nc, in_maps, core_ids=list(range(8)))
    outp = np.empty((B, T, Dm), np.float32)
    for c in range(8):
        b, j = c // 2, c % 2
        outp[b, j * half:(j + 1) * half] = res.results[c]["out"]
    return outp
```

```python
import numpy as np
from contextlib import ExitStack
import concourse.bass as bass
import concourse.mybir as mybir
from concourse.bass_utils import run_bass_kernel_spmd
from concourse.ap import AP

F32 = mybir.dt.float32
BF16 = mybir.dt.bfloat16
AF = mybir.ActivationFunctionType
ALU = mybir.AluOpType
AX = mybir.AxisListType

SELF_SYNC = True
PROFILE = False
NTF = 27
POOL_E = "dve"
PSH_BANKS = 0
VWIN = 100
TB = 256
NS = TB // 128
D = 1024
DFF = 2816
NFC = DFF // 128


class StopBuild(Exception):
    pass


class Buf:
    __slots__ = ("name", "w", "r", "nodes", "valloc")

    def __init__(self, name):
        self.name = name
        self.w = None
        self.r = []
        self.valloc = None
        self.nodes = None


_SM = {"pass": 0, "vallocs": [], "assign": {}}
SMART_RINGS = False


class Tl:
    __slots__ = ("ap", "buf", "sem")

    def __init__(self, ap, name, sem=None):
        self.ap = ap
        self.buf = Buf(name)
        self.sem = sem


class Ring:
    def __init__(self, tiles, name=None):
        self.t = tiles
        self.i = 0
        self.name = name

    def next(self):
        i = self.i
        self.i += 1
        if self.name is not None and len(self.t) > 1:
            if _SM["pass"] == 1:
                prev = _SM["assign"].get(self.name)
                t = self.t[prev[i]] if prev is not None else self.t[i % len(self.t)]
                t.buf.nodes = []
                _SM["vallocs"].append((self.name, i, len(self.t), t.buf.nodes))
                return t
            if _SM["pass"] == 2 and self.name in _SM["assign"]:
                return self.t[_SM["assign"][self.name][i]]
        return self.t[i % len(self.t)]


def _bufs(xs):
    out = []
    for x in xs:
        if x is None:
            continue
        out.append(x.buf if isinstance(x, Tl) else x)
    return out


class Node:
    __slots__ = ("id", "eng", "fn", "preds", "succs", "dur", "kind", "semkey", "lat", "prio", "start", "fin",
                 "ev", "nready", "tag", "seg", "vallocs")

    def __init__(self, id_, eng, fn, dur, kind, semkey, lat):
        self.id = id_
        self.eng = eng
        self.fn = fn
        self.preds = set()
        self.succs = []
        self.dur = dur
        self.kind = kind
        self.semkey = semkey
        self.lat = lat
        self.prio = 0.0
        self.ev = None


class Prog:
    ENG = ("pe", "act", "dve", "pool", "sp")
    XLAT = 120.0

    def __init__(self, nc, sems, group_sems=()):
        self.nc = nc
        self.q = {e: [] for e in self.ENG}
        self.sem = sems
        self.group = set(group_sems)
        self.cnt = {k: 0 for k in sems}
        self.known = {e: {} for e in self.ENG}
        self.snap = {}
        self.nodes = []
        self.seg = 0
        self.valloc_nodes = {}
        self.place = {}
        self.alloc_class = {}
        self.class_slots = {}
        self.nwait = 0
        self.nins = {e: 0 for e in self.ENG}
        self.sim_end = 0.0
        self.sim_total = 0.0
        self.profile = None

    def _add(self, e, fn, reads, writes, dur, kind, semkey, lat):
        n = Node(len(self.nodes), e, fn, dur, kind, semkey, lat)
        n.seg = self.seg
        n.vallocs = None
        for b in list(reads) + list(writes):
            if b.nodes is not None:
                b.nodes.append(n)
            if b.valloc is not None:
                if n.vallocs is None:
                    n.vallocs = []
                if b.valloc not in n.vallocs:
                    n.vallocs.append(b.valloc)
                    self.valloc_nodes.setdefault(b.valloc, []).append(n)
        if self.profile is not None:
            import sys as _sys
            f = _sys._getframe(3)
            n.tag = f.f_lineno if f.f_code.co_name not in ("act", "tt", "ts", "stt", "cp", "mm", "tr", "scan") else _sys._getframe(4).f_lineno
        sg = self.seg
        for b in reads:
            if b.w is not None and b.w[0] == sg:
                n.preds.add(b.w[1])
        for b in writes:
            if b.w is not None and b.w[0] == sg:
                n.preds.add(b.w[1])
            for r in b.r:
                if r[0] == sg:
                    n.preds.add(r[1])
        n.preds.discard(n.id)
        me = (sg, n.id)
        for b in writes:
            b.w = me
            b.r = []
        for b in reads:
            if not b.r or b.r[-1] != me:
                b.r.append(me)
        self.nodes.append(n)
        return n

    def op(self, e, fn, reads=(), writes=(), inc=True, dur=100.0):
        self._add(e, fn, _bufs(reads), _bufs(writes), dur, "op", None, dur if e != "pe" else 60.0)

    def dma(self, e, semkey, out, in_, reads=(), writes=(), nbytes=65536, **kw):
        nbytes = int(np.prod(out.shape)) * 4
        lat = 2000.0 + nbytes / 150.0
        self._add(e, lambda eng: eng.dma_start(out=out, in_=in_, **kw), _bufs(reads), _bufs(writes), 60.0, "dma",
                  semkey, lat)

    def _schedule(self):
        import heapq
        nodes = self.nodes
        for n in nodes:
            for p in n.preds:
                nodes[p].succs.append(n.id)
        for n in reversed(nodes):
            best = 0.0
            for s_ in n.succs:
                sn = nodes[s_]
                lat = 0.0 if (n.eng == "pe" and sn.eng == "pe") else (n.lat + self.XLAT)
                v = sn.prio + lat
                if v > best:
                    best = v
            n.prio = n.dur + best
        free = {e: 0.0 for e in self.ENG}
        waiting = {e: [] for e in self.ENG}
        avail = {e: [] for e in self.ENG}
        blocked = []
        ready_t = [0.0] * len(nodes)
        npred = [len(n.preds) for n in nodes]
        for n in nodes:
            if npred[n.id] == 0:
                heapq.heappush(waiting[n.eng], (0.0, n.id))
        vnodes = self.valloc_nodes
        remaining_v = {k: len(v) for k, v in vnodes.items()}
        cls_of = self.alloc_class
        place = self.place
        owner = {c: [None] * n_ for c, n_ in self.class_slots.items()}
        free_t = {c: [0.0] * n_ for c, n_ in self.class_slots.items()}
        vorder = {c: [] for c in self.class_slots}
        for k in sorted(vnodes.keys()):
            vorder[cls_of[k]].append(k)
        optr = {c: 0 for c in self.class_slots}
        rptr = {c: 0 for c in self.class_slots}
        vfin = {k: 0.0 for k in vnodes}

        placed_flag = [False]

        def try_alloc(n):
            need_ = [k for k in n.vallocs if k not in place]
            if not need_:
                return True
            plan = []
            used = {}
            for k in need_:
                c = cls_of[k]
                vo = vorder[c]
                while optr[c] < len(vo) and vo[optr[c]] in place:
                    optr[c] += 1
                is_oldest = optr[c] < len(vo) and vo[optr[c]] == k
                ro = rptr[c]
                while ro < len(vo) and remaining_v[vo[ro]] == 0:
                    ro += 1
                rptr[c] = ro
                if ro < len(vo) and k > vo[ro] + VWIN:
                    return False
                ow = owner[c]
                freeb = [b for b in range(len(ow)) if (ow[b] is None or remaining_v[ow[b]] == 0)
                         and b not in used.get(c, ())]
                if not freeb or (len(freeb) < 2 and not is_oldest):
                    return False
                b = min(freeb, key=lambda b_: free_t[c][b_])
                used.setdefault(c, set()).add(b)
                plan.append((k, c, b))
            for k, c, b in plan:
                prev = owner[c][b]
                if prev is not None:
                    for p in vnodes[prev]:
                        if p.id not in n.preds:
                            n.preds.add(p.id)
                            p.succs.append(n.id)
                        lat = 0.0 if (p.eng == "pe" and n.eng == "pe") else (p.lat + self.XLAT)
                        rt = p.fin + lat
                        if rt > ready_t[n.id]:
                            ready_t[n.id] = rt
                owner[c][b] = k
                place[k] = b
            placed_flag[0] = True
            return True

        order = []
        remaining = len(nodes)
        while remaining:
            cand = []
            for e in self.ENG:
                w, a = waiting[e], avail[e]
                while w and w[0][0] <= free[e]:
                    t_, i_ = heapq.heappop(w)
                    heapq.heappush(a, (-nodes[i_].prio, i_))
                if a:
                    cand.append((free[e], e))
                elif w:
                    cand.append((w[0][0], e))
            cand.sort()
            picked = None
            for t, e in cand:
                tmp = []
                while True:
                    if avail[e]:
                        item = heapq.heappop(avail[e])
                        i_ = item[1]
                    elif waiting[e]:
                        item = heapq.heappop(waiting[e])
                        i_ = item[1]
                    else:
                        break
                    n = nodes[i_]
                    if n.vallocs is not None and not try_alloc(n):
                        blocked.append(i_)
                        continue
                    picked = n
                    break
                if picked is not None:
                    break
            if picked is None:
                raise RuntimeError("scheduler deadlock: all candidates blocked on resource slots")
            n = picked
            e = n.eng
            i_ = n.id
            start = max(free[e], ready_t[i_])
            n.start = start
            n.fin = start + n.dur
            free[e] = n.fin
            order.append(n)
            remaining -= 1
            if n.vallocs is not None:
                released = placed_flag[0]
                placed_flag[0] = False
                for k in n.vallocs:
                    remaining_v[k] -= 1
                    if n.fin > vfin[k]:
                        vfin[k] = n.fin
                    if remaining_v[k] == 0:
                        free_t[cls_of[k]][place[k]] = vfin[k]
                        released = True
                if released and blocked:
                    for j_ in blocked:
                        heapq.heappush(waiting[nodes[j_].eng], (ready_t[j_], j_))
                    blocked = []
            for s_ in n.succs:
                sn = nodes[s_]
                lat = 0.0 if (n.eng == "pe" and sn.eng == "pe") else (n.lat + self.XLAT)
                rt = start + lat if n.kind == "dma" else n.fin + (0.0 if lat == 0.0 else lat)
                if rt > ready_t[s_]:
                    ready_t[s_] = rt
                npred[s_] -= 1
                if npred[s_] == 0:
                    heapq.heappush(waiting[sn.eng], (ready_t[s_], s_))
        self.sim_end = max(free.values()) if nodes else 0.0
        if self.profile is not None:
            self.profile.append((self.sim_end, [(n.eng, n.tag, n.dur, n.start) for n in nodes]))
        return order

    def _merge(self, e, k, v):
        kn = self.known[e]
        if kn.get(k, 0) >= v:
            return False
        kn[k] = v
        s = self.snap.get((k, v))
        if s:
            for k2, v2 in s.items():
                if kn.get(k2, 0) < v2:
                    kn[k2] = v2
        return True

    def flush(self):
        nodes = self.nodes
        if not nodes:
            return
        order = self._schedule()
        self.sim_total += self.sim_end
        gtot = {}
        for n in nodes:
            if n.kind == "dma" and n.semkey in self.group:
                gtot[n.semkey] = gtot.get(n.semkey, self.cnt[n.semkey]) + 16
        need = [False] * len(nodes)
        for n in nodes:
            if n.kind == "dma":
                need[n.id] = True
                continue
            for s_ in n.succs:
                sn = nodes[s_]
                if not (n.eng == "pe" and sn.eng == "pe"):
                    if sn.eng != n.eng or SELF_SYNC:
                        need[n.id] = True
                        break
        sems = self.sem
        for n in order:
            e = n.eng
            evs = {}
            for p in n.preds:
                pn = nodes[p]
                if pn.eng == "pe" and e == "pe" and pn.kind == "op":
                    continue
                if pn.kind == "op" and pn.eng == e and not SELF_SYNC:
                    continue
                if pn.kind == "dma" and n.kind == "dma" and pn.semkey == n.semkey and n.semkey in self.group:
                    continue
                k, v = pn.ev
                if evs.get(k, 0) < v:
                    evs[k] = v
            waits = []
            for k, v in evs.items():
                if self._merge(e, k, v):
                    waits.append((k, v))
            if n.kind == "dma":
                k = n.semkey
                self.cnt[k] += 16
                n.ev = (k, gtot[k]) if k in self.group else (k, self.cnt[k])
                amount, semkey = 16, k
                if k not in self.group:
                    self.snap[n.ev] = dict(self.known[e])
            elif need[n.id]:
                self.cnt[e] += 1
                n.ev = (e, self.cnt[e])
                self.snap[n.ev] = dict(self.known[e])
                amount, semkey = 1, e
            else:
                n.ev = (e, self.cnt[e] + 1)
                amount, semkey = 0, e
            self.nwait += len(waits)
            self.nins[e] += 1

            def run(eng, waits=waits, fn=n.fn, amount=amount, semkey=semkey):
                for k, v in waits:
                    eng.wait_ge(sems[k], v)
                ins = fn(eng)
                if amount:
                    ins.then_inc(sems[semkey], amount)
            self.q[e].append(run)
        self.nodes = []
        self.valloc_nodes = {}
        self.seg += 1

    def barrier(self):
        self.flush()
        for e in self.ENG:
            waits = []
            for k, v in self.cnt.items():
                if v > 0 and k != e and self.known[e].get(k, 0) < v:
                    self.known[e][k] = v
                    waits.append((k, v))
            sems = self.sem

            def run(eng, waits=waits):
                for k, v in waits:
                    eng.wait_ge(sems[k], v)
            self.q[e].append(run)

    def replay(self, block):
        self.flush()
        if _SM["pass"] == 1:
            return
        q = self.q

        @block.tensor
        def _(eng):
            for f in q["pe"]:
                f(eng)

        @block.scalar
        def _(eng):
            for f in q["act"]:
                f(eng)

        @block.vector
        def _(eng):
            for f in q["dve"]:
                f(eng)

        @block.gpsimd
        def _(eng):
            for f in q["pool"]:
                f(eng)

        @block.sync
        def _(eng):
            for f in q["sp"]:
                f(eng)


PARAM_SHAPES = [
    ("norm1_w", [1024]), ("w_in", [1024, 3840]), ("hg_lb_logits", [2, 512]), ("hg_norm_w", [512]),
    ("rw_shift_mu", [1792]), ("rw_w0", [512]), ("rw_w2", [64, 512]), ("rw_a0", [512]),
    ("rw_a2", [64, 512]), ("rw_g2", [128, 512]), ("rw_k_k", [512]), ("rw_k_a", [512]),
    ("rw_r_k", [512]), ("rw_ln_w", [512]), ("rw_ln_b", [512]), ("w_out", [1024, 1024]),
    ("norm2_w", [1024]), ("w_up", [1024, 5632]), ("conv_w", [3, 5632]), ("conv_b", [5632]),
    ("w_down", [2816, 1024]), ("final_norm_w", [1024]),
]


def _assign_slots():
    by = {}
    for name, i, nslots, nodes in _SM["vallocs"]:
        by.setdefault(name, []).append((i, nslots, nodes))
    assign = {}
    for name, lst in by.items():
        if name not in SMART_SET:
            continue
        nslots = lst[0][1]
        res = [0] * len(lst)
        items = []
        for i, _, nodes in lst:
            if not nodes:
                items.append((0, 0.0, i))
            else:
                items.append((nodes[0].seg, min(n.start for n in nodes), i))
        items.sort()
        for rank, (seg, st, i) in enumerate(items):
            res[i] = rank % nslots
        assign[name] = res
    return assign


SMART_SET = ("ps",)
SMART_ITERS = 1


def build(nl=7, nm=8, dbg=None, stop=None):
    global VWIN
    if not SMART_RINGS:
        _SM["pass"] = 0
        last = None
        for w in (VWIN, 64, 48, 80, 40, 128, 32):
            VWIN = w
            try:
                return _build(nl, nm, dbg, stop)
            except RuntimeError as e_:
                if "scheduler deadlock" not in str(e_):
                    raise
                last = e_
        raise last
    _SM["assign"] = {}
    for _ in range(SMART_ITERS):
        _SM["pass"] = 1
        _SM["vallocs"] = []
        _build(nl, nm, dbg, stop)
        _SM["assign"] = _assign_slots()
    _SM["pass"] = 2
    try:
        return _build(nl, nm, dbg, stop)
    finally:
        _SM["pass"] = 0


def _build(nl=7, nm=8, dbg=None, stop=None):
    nblk = nl + 1 + nm
    nc = bass.Bass("TRN2", target_bir_lowering=False)
    dr = {}
    for name, shp in PARAM_SHAPES:
        dr[name] = nc.dram_tensor(name, shp, F32, kind="ExternalInput").ap()
    xs = nc.dram_tensor("xs", [nblk * TB, D], F32, kind="ExternalInput").ap()
    flag = nc.dram_tensor("flag", [128, 1], F32, kind="ExternalInput").ap()
    out = nc.dram_tensor("out", [nm * TB, D], F32, kind="ExternalOutput").ap()
    x1d = nc.dram_tensor("x1d", [128 + nm * TB, D], F32, kind="Internal").ap()
    dbg_t = {}
    if dbg:
        for name, shp in dbg.items():
            dt_ = F32
            if shp and shp[0] == "bf16":
                dt_ = BF16
                shp = shp[1:]
            dbg_t[name] = nc.dram_tensor("dbg_" + name, shp, dt_, kind="ExternalOutput").ap()

    es = ExitStack()
    with es:
        ARENA_F32 = 53000
        arena = es.enter_context(nc.sbuf_tensor("arena", [128, ARENA_F32], F32))
        psum_all = es.enter_context(nc.psum_tensor("psum_all", [128, 8, 512], F32))
        semkeys = ["pe", "act", "dve", "pool", "sp", "d_par", "d_x0", "d_x1", "d_x2", "d_x3",
                   "d_w0", "d_w1", "d_w2", "d_w3", "d_w4", "d_w5", "d_w6", "d_w7", "d_w8", "d_wo",
                   "d_st0", "d_st1", "d_st2", "d_st3", "d_dbg", "d_fn", "d_wdn0", "d_wdn1"] + [f"d_wup{i}" for i in range(NFC // 2)]
        sems = {k: es.enter_context(nc.semaphore(k)) for k in semkeys}
        block = es.enter_context(nc.Block())
        P = Prog(nc, sems, group_sems=("d_par", "d_w7", "d_dbg"))
        if PROFILE:
            P.profile = []

        st = {"off": 0}

        def carve(name, free_shape, dt, parts=128, sem=None):
            n = int(np.prod(free_shape))
            nf32 = (n + 1) // 2 if dt == BF16 else n
            nf32 = (nf32 + 3) // 4 * 4
            off = st["off"]
            st["off"] = off + nf32
            assert st["off"] <= ARENA_F32, (name, st["off"])
            v = arena[:, off:off + nf32]
            if dt == BF16:
                v = v.bitcast(BF16)[:, 0:n]
            else:
                v = v[:, 0:n]
            if len(free_shape) == 2:
                v = v.rearrange("p (a b) -> p a b", b=free_shape[1])
            elif len(free_shape) == 3:
                v = v.rearrange("p (a b c) -> p a b c", b=free_shape[1], c=free_shape[2])
            if parts != 128:
                v = v[0:parts]
            return Tl(v, name, sem)

        carve_off = {}

        def vring(name, n, free_shape, dt):
            tiles = []
            bases = []
            for i in range(n):
                bases.append(st["off"])
                tiles.append(carve(f"{name}{i}", free_shape, dt))
            return VRing(name, tiles, bases)

        def ring(name, n, free_shape, dt, sems_=None):
            return Ring([carve(f"{name}{i}", free_shape, dt, sem=(sems_[i] if sems_ else None)) for i in range(n)],
                        name=name)

        VS32 = 1 << 24
        ps_b0 = psum_all[:, 0, :]
        v_cnt = [0]
        v_info = {}
        P.class_slots["ps"] = 8

        def valloc(cname, slot0_ap, bases, name):
            k = v_cnt[0]
            v_cnt[0] += 1
            t = Tl(AP(slot0_ap.tensor, slot0_ap.offset + (k + 1) * VS32 * (4 // (2 if slot0_ap.dtype == BF16 else 4)),
                      slot0_ap.ap), f"{name}{k}")
            t.buf.valloc = k
            P.alloc_class[k] = cname
            v_info[k] = bases
            return t

        PS_BASES = [b * 512 for b in range(8 - PSH_BANKS)]
        PSH_BASES = [b * 512 + h * 256 for b in range(8 - PSH_BANKS, 8) for h in range(2)]
        P.class_slots["ps"] = 8 - PSH_BANKS
        if PSH_BANKS:
            P.class_slots["psh"] = 2 * PSH_BANKS
            psh_0 = psum_all[:, 8 - PSH_BANKS, 0:256]

        def psum(kind="full"):
            if kind == "half" and PSH_BANKS:
                return valloc("psh", psh_0, PSH_BASES, "psh")
            return valloc("ps", ps_b0, PS_BASES, "psv")

        class VRing:
            def __init__(self, name, tiles, bases):
                self.name = name
                self.t0 = tiles[0]
                self.bases = bases
                P.class_slots[name] = len(tiles)

            def next(self):
                return valloc(self.name, self.t0.ap, self.bases, self.name + "v")

        def RB(ap):
            if ap is None or not hasattr(ap, "name") or ap.name not in ("psum_all", "arena"):
                return ap
            mul = 2 if ap.dtype == BF16 else 1
            vs = VS32 * mul
            k = ap.offset // vs - 1
            if k < 0:
                return ap
            real = ap.offset % vs
            bases = v_info[k]
            return AP(ap.tensor, real + (bases[P.place[k]] - bases[0]) * mul, ap.ap)

        def fsz(ap):
            return int(np.prod(ap.shape[1:]))

        def d_act(o):
            return 220.0 + 0.72 * fsz(o)

        def d_dve(o, k=1.0):
            return 70.0 + 1.05 * k * fsz(o)

        def act(o, i, func, reads, writes, **kw):
            P.op("act", lambda e: e.activation(out=RB(o), in_=RB(i), func=func, **kw), reads, writes,
                 dur=d_act(o) + (60.0 if "accum_out" in kw else 0.0))

        def tt(eng, o, a, b, op, reads, writes):
            P.op(eng, lambda e: e.tensor_tensor(out=RB(o), in0=RB(a), in1=RB(b), op=op), reads, writes,
                 dur=(d_dve(o) if eng != "pool" else 150.0 + 2.3 * fsz(o)))

        def ts(eng, o, a, s1, s2, op0, op1, reads, writes):
            if s2 is None:
                P.op(eng, lambda e: e.tensor_scalar(out=RB(o), in0=RB(a), scalar1=s1, scalar2=None, op0=op0), reads, writes,
                     dur=d_dve(o))
            else:
                P.op(eng, lambda e: e.tensor_scalar(out=RB(o), in0=RB(a), scalar1=s1, scalar2=s2, op0=op0, op1=op1), reads, writes,
                     dur=d_dve(o))

        def stt(o, a, s, b, op0, op1, reads, writes):
            P.op("dve", lambda e: e.scalar_tensor_tensor(out=RB(o), in0=RB(a), scalar=s, in1=RB(b), op0=op0, op1=op1), reads, writes,
                 dur=d_dve(o))

        def cp(eng, o, i, reads, writes):
            if eng == "act":
                P.op("act", lambda e: e.activation(out=RB(o), in_=RB(i), func=AF.Identity), reads, writes, dur=d_act(o))
            else:
                P.op(eng, lambda e: e.tensor_copy(out=RB(o), in_=RB(i)), reads, writes, dur=d_dve(o))

        def mm(o, l, r, start, stop, reads, writes, inc=True):
            P.op("pe", lambda e: e.matmul(out=RB(o), lhsT=RB(l), rhs=RB(r), start=start, stop=stop), reads, writes,
                 dur=max(64, fsz(r)) * 0.45 + 12.0)

        def tr(o, i, ident, reads, writes, inc=True):
            P.op("pe", lambda e: e.transpose(out=RB(o), in_=RB(i), identity=ident), reads, writes, dur=70.0)

        def memset(eng, ap, val, writes):
            P.op(eng, lambda e: e.memset(ap, val), (), writes, dur=200.0 + fsz(ap))

        def asel(o, i, pattern, cmp, fill, base, cm, reads, writes):
            P.op("pool", lambda e: e.affine_select(out=o, in_=i, pattern=pattern, compare_op=cmp, fill=fill,
                                                   base=base, channel_multiplier=cm), reads, writes, dur=1000.0)

        def scan(o, d0, d1, init, op0, op1, reads, writes):
            P.op("dve", lambda e: e.tensor_tensor_scan(out=RB(o), data0=RB(d0), data1=RB(d1), initial=init, op0=op0, op1=op1),
                 reads, writes, dur=d_dve(o, 2.0))

        def ck(name):
            if stop == name:
                raise StopBuild()

        def dbg_dump(name, tl_ap, reads):
            if name in dbg_t:
                P.dma("sp", "d_dbg", dbg_t[name], tl_ap, reads=reads)

        dumped = set()

        def dd(name, tl, ap=None, when=True):
            if name in dbg_t and name not in dumped and when:
                dumped.add(name)
                P.dma("sp", "d_dbg", dbg_t[name], tl.ap if ap is None else ap, reads=[tl])

        identf = carve("identf", [128], F32)
        identb = carve("identb", [128], BF16)
        ones = carve("ones", [128], F32)
        zeros = carve("zeros", [128], F32)
        negh = carve("negh", [TB], F32)
        mUI = carve("mUI", [128], BF16)
        mRW = carve("mRW", [4, 128], BF16)
        mL = carve("mL", [128], BF16)
        blk = carve("blk", [128], F32)
        PT = carve("PT", [70], F32)
        CW = carve("CW", [4, NFC * 2], F32)
        DP = carve("DP", [56], F32)
        flg = carve("flg", [1], F32)
        S_hg = carve("S_hg", [4, 128], F32)
        Sb_hg = carve("Sb_hg", [4, 128], BF16)
        H_rw = carve("H_rw", [4, 64], F32)
        Hb_rw = carve("Hb_rw", [4, 64], BF16)
        glob_end = st["off"]

        memset("pool", identf.ap, 0.0, [identf])
        asel(identf.ap, identf.ap, [[-1, 128]], ALU.not_equal, 1.0, 0, 1, [identf], [identf])
        cp("dve", identb.ap, identf.ap, [identf], [identb])
        memset("pool", ones.ap, 1.0, [ones])
        memset("pool", zeros.ap, 0.0, [zeros])
        memset("pool", negh.ap, -0.5, [negh])
        memset("pool", S_hg.ap, 0.0, [S_hg])
        memset("pool", Sb_hg.ap, 0.0, [Sb_hg])
        memset("pool", H_rw.ap, 0.0, [H_rw])
        memset("pool", Hb_rw.ap, 0.0, [Hb_rw])
        tmpm = carve("tmpm", [128], F32)
        memset("pool", tmpm.ap, 1.0, [tmpm])
        asel(tmpm.ap, tmpm.ap, [[1, 128]], ALU.is_ge, 0.0, 0, -1, [tmpm], [tmpm])
        cp("dve", mRW.ap[:, 1, :], tmpm.ap, [tmpm], [mRW])
        cp("dve", mRW.ap[:, 3, :], tmpm.ap, [tmpm], [mRW])
        asel(tmpm.ap[:, 64:128], tmpm.ap[:, 64:128], [[0, 64]], ALU.is_ge, 0.0, -64, 1, [tmpm], [tmpm])
        cp("dve", mUI.ap, tmpm.ap, [tmpm], [mUI])
        memset("pool", tmpm.ap, 1.0, [tmpm])
        asel(tmpm.ap, tmpm.ap, [[1, 128]], ALU.is_gt, 0.0, 0, -1, [tmpm], [tmpm])
        cp("dve", mRW.ap[:, 0, :], tmpm.ap, [tmpm], [mRW])
        cp("dve", mRW.ap[:, 2, :], tmpm.ap, [tmpm], [mRW])
        memset("pool", tmpm.ap, 1.0, [tmpm])
        asel(tmpm.ap, tmpm.ap, [[-1, 128]], ALU.is_gt, 0.0, 0, 1, [tmpm], [tmpm])
        cp("dve", mL.ap, tmpm.ap, [tmpm], [mL])
        memset("pool", blk.ap, 1.0, [blk])
        asel(blk.ap[:, 0:64], blk.ap[:, 0:64], [[0, 64]], ALU.is_ge, 0.0, 63, -1, [blk], [blk])
        asel(blk.ap[:, 64:128], blk.ap[:, 64:128], [[0, 64]], ALU.is_ge, 0.0, -64, 1, [blk], [blk])

        PMa = carve("PMa", [128], F32)
        PMc = carve("PMc", [4, 128], F32)
        rows = [("norm1_w", None, 8), ("norm2_w", None, 8), ("hg_lb_logits", 0, 4), ("hg_lb_logits", 1, 4),
                ("hg_norm_w", None, 4), ("rw_shift_mu", None, 14), ("rw_w0", None, 4), ("rw_a0", None, 4),
                ("rw_k_k", None, 4), ("rw_k_a", None, 4), ("rw_r_k", None, 4), ("rw_ln_w", None, 4),
                ("rw_ln_b", None, 4)]
        r0 = 0
        for name, idx, n in rows:
            src = dr[name] if idx is None else dr[name][idx]
            P.dma("sp", "d_par", PMa.ap[r0:r0 + n, :], src.rearrange("(r p) -> r p", p=128), writes=[PMa])
            r0 += n
        assert r0 == 70
        for j in range(3):
            P.dma("sp", "d_par", PMc.ap[0:2 * NFC, j, :], dr["conv_w"][j].rearrange("(r p) -> r p", p=128), writes=[PMc])
        P.dma("sp", "d_par", PMc.ap[0:2 * NFC, 3, :], dr["conv_b"].rearrange("(r p) -> r p", p=128), writes=[PMc])
        P.dma("sp", "d_par", flg.ap, flag, writes=[flg])
        pp_ = psum()
        tr(pp_.ap[:, 0:70], PMa.ap[0:70, :], identf.ap[0:70, 0:70], [PMa, identf], [pp_])
        cp("dve", PT.ap, pp_.ap[:, 0:70], [pp_], [PT])
        pp_ = psum()
        for j in range(4):
            tr(pp_.ap[:, j * 44:(j + 1) * 44], PMc.ap[0:44, j, :], identf.ap[0:44, 0:44], [PMc, identf], [pp_], inc=(j == 3))
        cp("dve", CW.ap.rearrange("p a b -> p (a b)"), pp_.ap[:, 0:176], [pp_], [CW])
        nw1T = PT.ap[:, 0:8]
        nw2T = PT.ap[:, 8:16]
        hgnw = PT.ap[:, 24:28]
        muT = PT.ap[:, 28:42]
        w0T = PT.ap[:, 42:46]
        a0T = PT.ap[:, 46:50]
        kkT = PT.ap[:, 50:54]
        kaT = PT.ap[:, 54:58]
        rkT = PT.ap[:, 58:62]
        lnwT = PT.ap[:, 62:66]
        lnbT = PT.ap[:, 66:70]
        lbT = DP.ap[:, 0:4]
        lb1T = DP.ap[:, 4:8]
        cq2 = DP.ap[:, 8:12]
        hgnw2 = DP.ap[:, 12:16]
        hw0 = DP.ap[:, 16:20]
        ha0 = DP.ap[:, 20:24]
        hka = DP.ap[:, 24:28]
        nhka = DP.ap[:, 28:32]
        dtmp = DP.ap[:, 32:36]
        thl = DP.ap[:, 36:40]
        tt("dve", dtmp, PT.ap[:, 16:20], PT.ap[:, 20:24], ALU.subtract, [PT], [DP])
        act(thl, dtmp, AF.Exp, [DP], [DP], scale=-1.0)
        ts("dve", thl, thl, 1.0, None, ALU.add, None, [DP], [DP])
        P.op("dve", lambda e: e.reciprocal(out=lbT, in_=thl), [DP], [DP])
        ts("dve", lb1T, lbT, -1.0, 1.0, ALU.mult, ALU.add, [DP], [DP])
        ts("dve", cq2, lb1T, -(128 ** -0.5), None, ALU.mult, None, [DP], [DP])
        ts("dve", hw0, w0T, -1.0, None, ALU.mult, None, [PT], [DP])
        omT = DP.ap[:, 40:54]
        ts("dve", omT, muT, -1.0, 1.0, ALU.mult, ALU.add, [PT], [DP])
        ts("dve", ha0, a0T, -1.0, None, ALU.mult, None, [PT], [DP])
        if "PT" in dbg_t:
            dbg_dump("PT", PT.ap, [PT])

        if stop == "setup":
            P.barrier()
            P.replay(block)
            return nc, P
        P.barrier()
        st["off"] = glob_end
        w_in = carve("w_in", [8, 3840], BF16)
        w_out = carve("w_out", [8, 1024], BF16)
        w2b = carve("w2b", [512], BF16)
        a2b = carve("a2b", [512], BF16)
        g2b = carve("g2b", [512], BF16)
        wgroups = [(512, 1024), (1024, 1536), (3584, 3840), (2560, 3072), (3072, 3584), (0, 512), (1536, 2048), (2048, 2560)]
        wbufs = []
        wsrc = dr["w_in"].rearrange("(kc p) c -> p kc c", p=128)
        for gi, (c0, c1) in enumerate(wgroups):
            b = Buf(f"win{gi}")
            wbufs.append((c0, c1, b))
            P.dma("pool", f"d_w{gi if gi < 7 else 8}", w_in.ap[:, :, c0:c1], wsrc[:, :, c0:c1], writes=[b])
        P.dma("pool", "d_w7", w2b.ap[0:64, :], dr["rw_w2"], writes=[w2b])
        P.dma("pool", "d_w7", a2b.ap[64:128, :], dr["rw_a2"], writes=[a2b])
        P.dma("pool", "d_w7", g2b.ap, dr["rw_g2"], writes=[g2b])
        P.dma("pool", "d_wo", w_out.ap, dr["w_out"].rearrange("(kc p) c -> p kc c", p=128), writes=[w_out])

        def wbuf_of(c):
            col = c * 128
            for c0, c1, b in wbufs:
                if c0 <= col < c1:
                    return b
            raise AssertionError

        xring = ring("xt", 2, [D], F32, ["d_x0", "d_x1"])
        xnr = ring("xn", 2, [D], BF16)
        sm = ring("sm", 8, [16], F32)
        hT = carve("hT", [8, TB + 1], BF16)
        memset("pool", hT.ap, 0.0, [hT])
        tf = ring("tf", NTF, [TB + 4], F32)
        tb = ring("tb", 4, [TB], BF16)
        hg_kT = [carve(f"hg_kT{h}", [TB], BF16) for h in range(4)]
        hg_ktok = [carve(f"hg_ktok{h}", [NS, 128], BF16) for h in range(4)]
        hg_vtok = [carve(f"hg_vtok{h}", [NS, 128], BF16) for h in range(4)]
        hg_qT = [carve(f"hg_qT{h}", [TB], BF16) for h in range(4)]
        hg_sgT = [carve(f"hg_sgT{h}", [TB], BF16) for h in range(4)]
        hg_ebl = carve("hg_ebl", [4, TB // 64], F32)
        rw_kT = [carve(f"rw_kT{c}", [TB], BF16) for c in range(4)]
        rw_bT = [carve(f"rw_bT{c}", [TB], BF16) for c in range(4)]
        rw_ar = [carve(f"rw_ar{c}", [NS * 2, 2, 128], BF16) for c in range(4)]
        rw_ktok = [carve(f"rw_ktok{c}", [NS, 128], BF16) for c in range(4)]
        rw_btok = [carve(f"rw_btok{c}", [NS, 128], BF16) for c in range(4)]
        rw_vtok = [carve(f"rw_vtok{c}", [NS, 128], BF16) for c in range(4)]
        rw_gT = [carve(f"rw_gT{c}", [TB], BF16) for c in range(4)]
        rw_bon = [carve(f"rw_bon{c}", [TB], BF16) for c in range(4)]
        rw_egx = [carve(f"rw_egx{c}", [NS, 129], F32) for c in range(4)]
        twd = carve("twd", [TB], BF16)
        adT = carve("adT", [TB], BF16)
        sgd = carve("sgd", [TB], BF16)
        ATh = ring("ATh", 2, [4, 128], BF16)
        Pq = [carve(f"Pq{i}", [4, 128], BF16) for i in range(2)] * NS
        PTq = [carve(f"PTq{i}", [4, 128], BF16) for i in range(2)] * NS
        TTq = [carve(f"TTq{i}", [4, 128], BF16) for i in range(2)] * NS
        Rb = carve("Rb", [512], BF16)
        Ub = carve("Ub", [512], BF16)
        ysb = ring("ysb", 2, [512], F32)
        ysq = carve("ysq", [512], F32)
        ynb = ring("ynb", 2, [512], BF16)
        oT = ring("oT", 2, [8, 128], BF16)
        x1t = ring("x1t", 1, [D], F32, ["d_st0"])
        for c in range(4):
            memset("pool", rw_egx[c].ap, 1.0, [rw_egx[c]])
            memset("pool", rw_ar[c].ap, 0.0, [rw_ar[c]])
        ATb = [carve(f"ATb{c}", [4, 128], BF16) for c in range(4)]
        ATk = [carve(f"ATk{c}", [4, 128], BF16) for c in range(4)]

        def arv(c, s, j, w):
            return rw_ar[c].ap[:, s * 2 + j, w, :]

        def rsqrt_small(dst_ap, dst_reads, src_ap, n, scale, eps, reads, writes):
            t_ = sm.next()
            act(t_.ap[:, 0:n], src_ap, AF.Ln, reads, [t_], scale=scale, bias=eps)
            act(dst_ap, t_.ap[:, 0:n], AF.Exp, [t_] + list(dst_reads), writes, scale=-0.5)

        def sig3(dst_ap, dst_tl, src_ap, reads, bias=None):
            e_ = tf.next()
            shp = [src_ap.shape[0], int(np.prod(src_ap.shape[1:]))]
            ev = e_.ap[0:shp[0], 0:shp[1]]
            if bias is None:
                act(ev, src_ap, AF.Exp, reads, [e_], scale=-1.0)
            else:
                act(ev, src_ap, AF.Exp, list(reads) + [DP], [e_], scale=-1.0, bias=bias)
            act(ev, ev, AF.Ln, [e_], [e_], bias=1.0)
            act(dst_ap, ev, AF.Exp, [e_], [dst_tl], scale=-1.0)

        def load_norm_T(src_rows_ap, nwT, hT_tile, s, xt=None, src_reads=(), off=0):
            if xt is None:
                xt = xring.next()
                P.dma("sp", xt.sem, xt.ap, src_rows_ap, reads=src_reads, writes=[xt])
            ss = sm.next()
            xn = xnr.next()
            act(xn.ap, xt.ap, AF.Square, [xt], [xn, ss], accum_out=ss.ap[:, 0:1])
            rs = sm.next()
            rsqrt_small(rs.ap[:, 0:1], [], ss.ap[:, 0:1], 1, 1.0 / D, 1e-6, [ss], [rs])
            ts("dve", xn.ap, xt.ap, rs.ap[:, 0:1], None, ALU.mult, None, [xt, rs], [xn])
            dd('ss', ss, ap=ss.ap[:, 0:1]); dd('rs', rs, ap=rs.ap[:, 0:1]); dd('xn', xn); dd('xt', xt)
            pb = psum()
            pbv = pb.ap.bitcast(BF16).rearrange("p (a b) -> p a b", b=128)
            for kc in range(8):
                tr(pbv[:, kc, :], xn.ap[:, kc * 128:(kc + 1) * 128], identb.ap, [xn, identb], [pb], inc=(kc == 7))
            tt("dve", hT_tile.ap[:, :, off + s * 128:off + (s + 1) * 128], pbv, nwT.unsqueeze(2).to_broadcast([128, 8, 128]),
               ALU.mult, [pb, PT], [hT_tile])
            return xt

        def proj(c, halo=False):
            pb = psum()
            wb = wbuf_of(c)
            for kc in range(8):
                if halo:
                    mm(pb.ap[:, 0:TB + 1], w_in.ap[:, kc, c * 128:(c + 1) * 128], hT.ap[:, kc, :], kc == 0, kc == 7,
                       [wb, hT], [pb], inc=(kc == 7))
                else:
                    mm(pb.ap[:, 0:TB], w_in.ap[:, kc, c * 128:(c + 1) * 128], hT.ap[:, kc, 1:TB + 1], kc == 0, kc == 7,
                       [wb, hT], [pb], inc=(kc == 7))
            return pb

        def to_tok(srcT, dst, eng="act"):
            pb = psum("half")
            pbv = pb.ap.bitcast(BF16)[:, 0:NS * 128].rearrange("p (a b) -> p a b", b=128)
            for s in range(NS):
                tr(pbv[:, s, :], srcT.ap[:, s * 128:(s + 1) * 128], identb.ap, [srcT, identb], [pb], inc=(s == NS - 1))
            cp(eng, dst.ap, pbv, [pb], [dst])


        def mixer_block(bi, full, emit_sub0):
            cp("dve", hT.ap[:, :, 0:1], hT.ap[:, :, TB:TB + 1], [hT], [hT])
            for s in range(NS):
                g = bi * NS + s
                load_norm_T(xs[g * 128:(g + 1) * 128, :], nw1T, hT, s, off=1)
            dd('hT', hT, when=(bi == nl))
            ck('A')
            for h in range(4):
                pf = proj(4 + h)
                thf = tf.next()
                sig3(thf.ap[:, 0:TB], thf, pf.ap[:, 0:TB], [pf])
                f_ = tf.next()
                ts("dve", f_.ap[:, 0:TB], thf.ap[:, 0:TB], lb1T[:, h:h + 1], lbT[:, h:h + 1], ALU.mult, ALU.add,
                   [thf, DP], [f_])
                eb = tf.next()
                for c4 in range(TB // 64):
                    sl = slice(c4 * 64, (c4 + 1) * 64)
                    scan(eb.ap[:, sl], f_.ap[:, sl], ones.ap[:, 0:64], 1.0, ALU.mult, ALU.mult, [f_, ones], [eb])
                enb = tf.next()
                P.op("dve", lambda e, o=enb.ap[:, 0:TB], i=eb.ap[:, 0:TB]: e.reciprocal(out=RB(o), in_=RB(i)), [eb], [enb], dur=340.0)
                stt(hg_kT[h].ap, thf.ap[:, 0:TB], 1.0, enb.ap[:, 0:TB], ALU.subtract, ALU.mult, [thf, enb], [hg_kT[h]])
                cp("act", hg_ebl.ap[:, h, :], eb.ap[:, 63:TB:64], [eb], [hg_ebl])
                kh = tb.next()
                tt(POOL_E, kh.ap.rearrange("p (a b) -> p a b", b=64), hg_kT[h].ap.rearrange("p (a b) -> p a b", b=64),
                   hg_ebl.ap[:, h, :].unsqueeze(2).to_broadcast([128, TB // 64, 64]), ALU.mult,
                   [hg_kT[h], hg_ebl], [kh])
                to_tok(kh, hg_ktok[h])
                pi = proj(8 + h)
                vT = tb.next()
                cp("act", vT.ap, pi.ap[:, 0:TB], [pi], [vT])
                to_tok(vT, hg_vtok[h])
                if full:
                    pq = proj(h)
                    thq = tf.next()
                    sig3(thq.ap[:, 0:TB], thq, pq.ap[:, 0:TB], [pq])
                    s1 = tf.next()
                    tt("dve", s1.ap[:, 0:TB], thq.ap[:, 0:TB], pq.ap[:, 0:TB], ALU.mult, [thq, pq], [s1])
                    stt(hg_qT[h].ap, s1.ap[:, 0:TB], cq2[:, h:h + 1], eb.ap[:, 0:TB], ALU.mult, ALU.mult,
                        [s1, eb, DP], [hg_qT[h]])
                    pg = proj(12 + h)
                    thg = tf.next()
                    sig3(thg.ap[:, 0:TB], thg, pg.ap[:, 0:TB], [pg])
                    tt("dve", hg_sgT[h].ap, thg.ap[:, 0:TB], pg.ap[:, 0:TB], ALU.mult, [thg, pg], [hg_sgT[h]])

            dd('hg_kT0', hg_kT[0], when=(bi == nl)); dd('hg_qT0', hg_qT[0], when=(bi == nl)); dd('hg_vtok0', hg_vtok[0], when=(bi == nl)); dd('hg_ktok0', hg_ktok[0], when=(bi == nl)); dd('hg_sgT0', hg_sgT[0], when=(bi == nl)); dd('hg_ebl', hg_ebl, when=(bi == nl))
            ck('hgprep')
            def shift_mix(pb, ci, out_ap, out_tl, rows=slice(0, 128)):
                a1_ = tf.next()
                act(a1_.ap[:, 0:TB], pb.ap[:, 1:TB + 1], AF.Identity, [pb, DP], [a1_], scale=omT[:, ci:ci + 1])
                stt(out_ap, pb.ap[rows, 0:TB], muT[rows, ci:ci + 1], a1_.ap[rows, 0:TB], ALU.mult, ALU.add,
                    [pb, a1_, PT], [out_tl])

            pl = proj(28, halo=True)
            lo = tf.next()
            shift_mix(pl, 12, lo.ap[:, 0:TB], lo)
            e_ = tf.next()
            act(e_.ap[0:64, 0:TB], lo.ap[0:64, 0:TB], AF.Exp, [lo], [e_], scale=-2.0)
            act(e_.ap[0:64, 0:TB], e_.ap[0:64, 0:TB], AF.Ln, [e_], [e_], bias=1.0)
            act(e_.ap[0:64, 0:TB], e_.ap[0:64, 0:TB], AF.Exp, [e_], [e_], scale=-1.0)
            ts("dve", twd.ap[0:64, :], e_.ap[0:64, 0:TB], 2.0, -1.0, ALU.mult, ALU.add, [e_], [twd])
            cp("act", adT.ap[64:128, :], lo.ap[64:128, 0:TB], [lo], [adT])
            if full:
                pg_ = proj(29, halo=True)
                gdm = tf.next()
                shift_mix(pg_, 13, gdm.ap[:, 0:TB], gdm)
                sig3(sgd.ap, sgd, gdm.ap[:, 0:TB], [gdm])
            for c in range(4):
                cs = slice(c * 128, (c + 1) * 128)
                pw = psum("half")
                mm(pw.ap[:, 0:TB], w2b.ap[0:64, cs], twd.ap[0:64, :], True, True, [w2b, twd], [pw])
                ld = tf.next()
                sig3(ld.ap[:, 0:TB], ld, pw.ap[:, 0:TB], [pw], bias=hw0[:, c:c + 1])
                cld = float(np.exp(-0.5))
                lg = tf.next()
                for s in range(NS):
                    sl = slice(s * 128, (s + 1) * 128)
                    scan(lg.ap[:, sl], ld.ap[:, sl], zeros.ap, 0.0, ALU.add, ALU.add, [ld, zeros], [lg])
                act(rw_egx[c].ap[:, :, 1:129], lg.ap[:, 0:TB].rearrange("p (a b) -> p a b", b=128), AF.Exp,
                    [lg], [rw_egx[c]], scale=-cld)
                eng_t = tf.next()
                eng_ap = eng_t.ap[:, 0:TB]
                act(eng_ap, lg.ap[:, 0:TB], AF.Exp, [lg], [eng_t], scale=cld)
                pa = psum("half")
                mm(pa.ap[:, 0:TB], a2b.ap[64:128, cs], adT.ap[64:128, :], True, True, [a2b, adT], [pa])
                tha_t = tf.next()
                tha_ap = tha_t.ap[:, 0:TB]
                sig3(tha_ap, tha_t, pa.ap[:, 0:TB], [pa], bias=ha0[:, c:c + 1])
                if full:
                    pgm = psum("half")
                    mm(pgm.ap[:, 0:TB], g2b.ap[:, cs], sgd.ap, True, True, [g2b, sgd], [pgm])
                    cp("act", rw_gT[c].ap, pgm.ap[:, 0:TB], [pgm], [rw_gT[c]])
                pk = proj(20 + c, halo=True)
                kr = tf.next()
                shift_mix(pk, 4 + c, kr.ap[:, 0:TB], kr)
                kk = tf.next()
                ts("dve", kk.ap[:, 0:TB], kr.ap[:, 0:TB], kkT[:, c:c + 1], None, ALU.mult, None, [kr, PT], [kk])
                sq = tf.next()
                act(sq.ap[:, 0:TB], kk.ap[:, 0:TB], AF.Square, [kk], [sq])
                pss = psum("half")
                mm(pss.ap[:, 0:TB], blk.ap, sq.ap[:, 0:TB], True, True, [blk, sq], [pss])
                mx = tf.next()
                rn = tf.next()
                act(mx.ap[:, 0:TB], pss.ap[:, 0:TB], AF.Ln, [pss], [mx], bias=float(2.0 ** -60))
                act(rn.ap[:, 0:TB], mx.ap[:, 0:TB], AF.Exp, [mx], [rn], scale=-0.5)
                kkn = tf.next()
                tt(POOL_E, kkn.ap[:, 0:TB], kk.ap[:, 0:TB], rn.ap[:, 0:TB], ALU.mult, [kk, rn], [kkn])
                t1 = tf.next()
                ts("dve", t1.ap[:, 0:TB], tha_ap, 1.0, kaT[:, c:c + 1], ALU.subtract, ALU.mult,
                   [tha_t, PT], [t1])
                kp = tf.next()
                stt(kp.ap[:, 0:TB], t1.ap[:, 0:TB], 1.0, kr.ap[:, 0:TB], ALU.add, ALU.mult, [t1, kr], [kp])
                tt(POOL_E, rw_kT[c].ap, kp.ap[:, 0:TB], eng_ap, ALU.mult, [kp, eng_t], [rw_kT[c]])
                b1 = tf.next()
                tt(POOL_E, b1.ap[:, 0:TB], tha_ap, kkn.ap[:, 0:TB], ALU.mult, [tha_t, kkn], [b1])
                tt(POOL_E, rw_bT[c].ap, b1.ap[:, 0:TB], eng_ap, ALU.mult, [b1, eng_t], [rw_bT[c]])
                for j in range(2):
                    pj = slice(64 * j, 64 * j + 64)
                    stt(rw_ar[c].ap[pj, j:2 * NS:2, 0, :], kkn.ap[pj, 0:TB].rearrange("p (a b) -> p a b", b=128), -1.0,
                        rw_egx[c].ap[pj, :, 0:128], ALU.mult, ALU.mult, [kkn, rw_egx[c]], [rw_ar[c]])
                to_tok(rw_kT[c], rw_ktok[c])
                to_tok(rw_bT[c], rw_btok[c])
                pv = proj(24 + c, halo=True)
                vr = tf.next()
                shift_mix(pv, 8 + c, vr.ap[:, 0:TB], vr)
                vTb = tb.next()
                cp("act", vTb.ap, vr.ap[:, 0:TB], [vr], [vTb])
                to_tok(vTb, rw_vtok[c])
                if full:
                    pr = proj(16 + c, halo=True)
                    rr = tf.next()
                    shift_mix(pr, c, rr.ap[:, 0:TB], rr)
                    for j in range(2):
                        pj = slice(64 * j, 64 * j + 64)
                        tt(POOL_E, rw_ar[c].ap[pj, j:2 * NS:2, 1, :], rr.ap[pj, 0:TB].rearrange("p (a b) -> p a b", b=128),
                           rw_egx[c].ap[pj, :, 1:129], ALU.mult, [rr, rw_egx[c]], [rw_ar[c]])
                    rkr = tf.next()
                    stt(rkr.ap[:, 0:TB], rr.ap[:, 0:TB], rkT[:, c:c + 1], kp.ap[:, 0:TB], ALU.mult, ALU.mult,
                        [rr, kp, PT], [rkr])
                    pbs = psum("half")
                    mm(pbs.ap[:, 0:TB], blk.ap, rkr.ap[:, 0:TB], True, True, [blk, rkr], [pbs])
                    tt("dve", rw_bon[c].ap, pbs.ap[:, 0:TB], vr.ap[:, 0:TB], ALU.mult, [pbs, vr], [rw_bon[c]])

            dd('rw_kT0', rw_kT[0], when=(bi == nl)); dd('rw_bT0', rw_bT[0], when=(bi == nl)); dd('rw_ar0', rw_ar[0], when=(bi == nl)); dd('rw_egx0', rw_egx[0], when=(bi == nl)); dd('rw_vtok0', rw_vtok[0], when=(bi == nl)); dd('rw_gT0', rw_gT[0], when=(bi == nl)); dd('rw_bon0', rw_bon[0], when=(bi == nl));
            ck('rwprep')
            for s in range(NS):
                ssl = slice(s * 128, (s + 1) * 128)
                emit = full and (s >= emit_sub0)
                if emit:
                    pa = psum()
                    for h in range(4):
                        mm(pa.ap[:, h * 128:(h + 1) * 128], hg_kT[h].ap[:, ssl], hg_qT[h].ap[:, ssl], True, True,
                           [hg_kT[h], hg_qT[h]], [pa], inc=(h == 3))
                    ath = ATh.next()
                    tt("dve", ath.ap, pa.ap.rearrange("p (a b) -> p a b", b=128),
                       mUI.ap.unsqueeze(1).to_broadcast([128, 4, 128]), ALU.mult, [pa, mUI], [ath])
                    po = psum()
                for cc in range(2):
                    ps_ = slice(64 * cc, 64 * cc + 64)
                    ci = s * 2 + cc
                    if emit:
                        for h in range(4):
                            hs = slice(h * 128, (h + 1) * 128)
                            mm(po.ap[ps_, hs], hg_qT[h].ap[:, s * 128 + 64 * cc:s * 128 + 64 * cc + 64],
                               Sb_hg.ap[:, h, :], True, False, [hg_qT[h], Sb_hg], [po], inc=False)
                            mm(po.ap[ps_, hs], ath.ap[:, h, 64 * cc:64 * cc + 64], hg_vtok[h].ap[:, s, :],
                               False, True, [ath, hg_vtok[h]], [po], inc=(h == 3))
                    pS = psum()
                    for h in range(4):
                        hs = slice(h * 128, (h + 1) * 128)
                        mm(pS.ap[:, hs], hg_ktok[h].ap[ps_, s, :], hg_vtok[h].ap[ps_, s, :], True, True,
                           [hg_ktok[h], hg_vtok[h]], [pS], inc=(h == 3))
                    for h in range(4):
                        hs = slice(h * 128, (h + 1) * 128)
                        stt(S_hg.ap[:, h, :], S_hg.ap[:, h, :], hg_ebl.ap[:, h, ci:ci + 1], pS.ap[:, hs],
                            ALU.mult, ALU.add, [S_hg, hg_ebl, pS], [S_hg])
                    cp("act", Sb_hg.ap, S_hg.ap, [S_hg], [Sb_hg])
                if emit:
                    s2 = sm.next()
                    jk = ysb.next()
                    for h in range(4):
                        act(jk.ap[:, h * 128:(h + 1) * 128], po.ap[:, h * 128:(h + 1) * 128], AF.Square, [po], [jk, s2],
                            accum_out=s2.ap[:, h:h + 1])
                    rs = sm.next()
                    rsqrt_small(rs.ap[:, 0:4], [], s2.ap[:, 0:4], 4, 1.0 / 128, 1e-6, [s2], [rs])
                    on = ynb.next()
                    tt("dve", on.ap.rearrange("p (a b) -> p a b", b=128), po.ap.rearrange("p (a b) -> p a b", b=128),
                       rs.ap[:, 0:4].unsqueeze(2).to_broadcast([128, 4, 128]), ALU.mult, [po, rs], [on])
                    pt_ = psum("half")
                    ptv = pt_.ap.bitcast(BF16)[:, 0:512].rearrange("p (a b) -> p a b", b=128)
                    for h in range(4):
                        tr(ptv[:, h, :], on.ap[:, h * 128:(h + 1) * 128], identb.ap, [on, identb], [pt_], inc=(h == 3))
                    ot = oT.next()
                    for h in range(4):
                        stt(ot.ap[:, h, :], ptv[:, h, :], hgnw[:, h:h + 1], hg_sgT[h].ap[:, ssl], ALU.mult, ALU.mult,
                            [pt_, DP, hg_sgT[h]], [ot])
                if emit:
                    dd('on', on); dd('ot_hg', ot, ap=ot.ap[:, 0:4, :])
                dd('S_hg', S_hg, when=(bi == nl and s == NS - 1))
                ck('hgchain')
                for c in range(4):
                    for (lt, dst) in ((rw_bT[c], ATb[c]), (rw_kT[c], ATk[c])):
                        pA = psum()
                        if emit:
                            mm(pA.ap, lt.ap[:, ssl], rw_ar[c].ap[:, 2 * s:2 * s + 2, :, :], True, True, [lt, rw_ar[c]], [pA])
                            tt("dve", dst.ap, pA.ap.rearrange("p (a b) -> p a b", b=128), mRW.ap, ALU.mult, [pA, mRW], [dst])
                        else:
                            mm(pA.ap[:, 0:256], lt.ap[:, ssl], rw_ar[c].ap[:, 2 * s:2 * s + 2, 0, :], True, True,
                               [lt, rw_ar[c]], [pA])
                            tt("dve", dst.ap[:, 0:3:2, :], pA.ap[:, 0:256].rearrange("p (a b) -> p a b", b=128),
                               mRW.ap[:, 0:3:2, :], ALU.mult, [pA, mRW], [dst])
                for g4 in range(2):
                    pN = psum()
                    for hh in range(4):
                        h = g4 * 4 + hh
                        c, j = h // 2, h % 2
                        pj = slice(64 * j, 64 * j + 64)
                        mm(pN.ap[:, hh * 128:(hh + 1) * 128], arv(c, s, j, 0), rw_bT[c].ap[:, ssl], True, True,
                           [rw_ar[c], rw_bT[c]], [pN], inc=(hh == 3))
                    Pc = Pq[s * 2 + g4]
                    tt("dve", Pc.ap, pN.ap.rearrange("p (a b) -> p a b", b=128),
                       mL.ap.unsqueeze(1).to_broadcast([128, 4, 128]), ALU.mult, [pN, mL], [Pc])
                    TTc = TTq[s * 2 + g4]
                    for cc_ in range(2):
                        c_ = g4 * 2 + cc_
                        tt(POOL_E, TTc.ap[:, 2 * cc_:2 * cc_ + 2, :], ATb[c_].ap[:, 0:3:2, :],
                           identb.ap.unsqueeze(1).to_broadcast([128, 2, 128]), ALU.add, [ATb[c_], identb], [TTc])
                    PTc = None
                    nlev = 6
                    for lev in range(1, nlev + 1):
                        last = (lev == nlev)
                        pP = psum()
                        def ptv_(hh):
                            if PTc is None:
                                h = g4 * 4 + hh
                                return ATb[h // 2].ap[:, 2 * (h % 2), :], ATb[h // 2]
                            return PTc.ap[:, hh, :], PTc
                        for hh in range(4):
                            pa_, pt_l = ptv_(hh)
                            mm(pP.ap[:, hh * 128:(hh + 1) * 128], pa_, Pc.ap[:, hh, :], True, True,
                               [pt_l, Pc], [pP], inc=(hh == 3))
                        if not last:
                            pPT = psum()
                            for hh in range(4):
                                pa_, pt_l = ptv_(hh)
                                mm(pPT.ap[:, hh * 128:(hh + 1) * 128], Pc.ap[:, hh, :], pa_, True, True,
                                   [pt_l, Pc], [pPT], inc=(hh == 3))
                        Pn = Pc
                        cp("act", Pn.ap, pP.ap.rearrange("p (a b) -> p a b", b=128), [pP], [Pn])
                        if not last:
                            PTn = PTq[s * 2 + g4]
                            cp("act", PTn.ap, pPT.ap.rearrange("p (a b) -> p a b", b=128), [pPT], [PTn])
                        pT2 = psum()
                        for hh in range(4):
                            mm(pT2.ap[:, hh * 128:(hh + 1) * 128], Pn.ap[:, hh, :], TTc.ap[:, hh, :], True, True,
                               [Pn, TTc], [pT2], inc=(hh == 3))
                        TTn = TTc
                        tt("dve", TTn.ap, pT2.ap.rearrange("p (a b) -> p a b", b=128), TTc.ap, ALU.add, [pT2, TTc], [TTn])
                        Pc = Pn
                        if not last:
                            PTc = PTn
                        TTc = TTn
                dd('ATb0', ATb[0], when=(bi == nl and s == NS - 1)); dd('ATk0', ATk[0], when=(bi == nl and s == NS - 1))
                ck('rwinv')
                pR = psum()
                for h in range(8):
                    c, j = h // 2, h % 2
                    pj = slice(64 * j, 64 * j + 64)
                    hs = slice(h * 64, (h + 1) * 64)
                    mm(pR.ap[:, hs], arv(c, s, j, 0), Hb_rw.ap[:, c, :], True, False, [rw_ar[c], Hb_rw], [pR],
                       inc=False)
                    mm(pR.ap[:, hs], ATk[c].ap[:, 2 * j, :], rw_vtok[c].ap[:, s, 64 * j:64 * j + 64], False, True,
                       [ATk[c], rw_vtok[c]], [pR], inc=(h == 7))
                cp("act", Rb.ap, pR.ap, [pR], [Rb])
                pU = psum()
                for h in range(8):
                    hs = slice(h * 64, (h + 1) * 64)
                    mm(pU.ap[:, hs], TTq[s * 2 + h // 4].ap[:, h % 4, :], Rb.ap[:, hs], True, True, [TTq[s * 2 + h // 4], Rb], [pU],
                       inc=(h == 7))
                cp("act", Ub.ap, pU.ap, [pU], [Ub])
                if emit:
                    pY = psum()
                    for h in range(8):
                        c, j = h // 2, h % 2
                        pj = slice(64 * j, 64 * j + 64)
                        hs = slice(h * 64, (h + 1) * 64)
                        mm(pY.ap[:, hs], arv(c, s, j, 1), Hb_rw.ap[:, c, :], True, False, [rw_ar[c], Hb_rw],
                           [pY], inc=False)
                        mm(pY.ap[:, hs], ATb[c].ap[:, 2 * j + 1, :], Ub.ap[:, hs], False, False, [ATb[c], Ub], [pY], inc=False)
                        mm(pY.ap[:, hs], ATk[c].ap[:, 2 * j + 1, :], rw_vtok[c].ap[:, s, 64 * j:64 * j + 64], False, True,
                           [ATk[c], rw_vtok[c]], [pY], inc=(h == 7))
                pH = psum()
                for c in range(4):
                    cs_ = slice(c * 128, (c + 1) * 128)
                    mm(pH.ap[:, cs_], rw_btok[c].ap[:, s, :], Ub.ap[:, cs_], True, False, [rw_btok[c], Ub], [pH], inc=False)
                    mm(pH.ap[:, cs_], rw_ktok[c].ap[:, s, :], rw_vtok[c].ap[:, s, :], False, True,
                       [rw_ktok[c], rw_vtok[c]], [pH], inc=(c == 3))
                ht = tf.next()
                htv = ht.ap[:, 0:256].rearrange("p (a b) -> p a b", b=64)
                for j in range(2):
                    pj = slice(64 * j, 64 * j + 64)
                    tt("dve", htv[pj], pH.ap[pj, :].rearrange("p (c x) -> p c x", x=128)[:, :, 64 * j:64 * j + 64],
                       H_rw.ap[pj], ALU.add, [pH, H_rw], [ht])
                for c in range(4):
                    ts("dve", H_rw.ap[:, c, :], htv[:, c, :], rw_egx[c].ap[:, s, 128:129], None, ALU.mult, None,
                       [ht, rw_egx[c]], [H_rw])
                cp("act", Hb_rw.ap, H_rw.ap, [H_rw], [Hb_rw])
                if emit:
                    dd('Ub', Ub); dd('Rb', Rb); dd('H_rw', H_rw)
                    ck('rwchain')
                    yb = ysb.next()
                    cp("act", yb.ap, pY.ap, [pY], [yb])
                    yv = yb.ap.rearrange("p (a b) -> p a b", b=64)
                    s1_ = sm.next()
                    P.op("dve", lambda e, o=s1_.ap[:, 0:8], i=yv: e.tensor_reduce(out=o, in_=i, axis=AX.X, op=ALU.add),
                         [yb], [s1_], dur=600.0)
                    act(ysq.ap, yb.ap, AF.Square, [yb], [ysq])
                    s2_ = sm.next()
                    P.op("dve", lambda e, o=s2_.ap[:, 0:8], i=ysq.ap.rearrange("p (a b) -> p a b", b=64):
                         e.tensor_reduce(out=o, in_=i, axis=AX.X, op=ALU.add), [ysq], [s2_], dur=600.0)
                    mean = sm.next()
                    ts("dve", mean.ap[:, 0:8], s1_.ap[:, 0:8], 1.0 / 64, None, ALU.mult, None, [s1_], [mean])
                    msq = sm.next()
                    tt("dve", msq.ap[:, 0:8], mean.ap[:, 0:8], mean.ap[:, 0:8], ALU.mult, [mean], [msq])
                    var = sm.next()
                    stt(var.ap[:, 0:8], s2_.ap[:, 0:8], 1.0 / 64, msq.ap[:, 0:8], ALU.mult, ALU.subtract, [s2_, msq], [var])
                    rs_ = sm.next()
                    rsqrt_small(rs_.ap[:, 0:8], [], var.ap[:, 0:8], 8, 1.0, 64e-5, [var], [rs_])
                    ycen = ysb.next()
                    tt("dve", ycen.ap.rearrange("p (a b) -> p a b", b=64), yv,
                       mean.ap[:, 0:8].unsqueeze(2).to_broadcast([128, 8, 64]), ALU.subtract, [yb, mean], [ycen])
                    yn = ynb.next()
                    tt("dve", yn.ap.rearrange("p (a b) -> p a b", b=64), ycen.ap.rearrange("p (a b) -> p a b", b=64),
                       rs_.ap[:, 0:8].unsqueeze(2).to_broadcast([128, 8, 64]), ALU.mult, [ycen, rs_], [yn])
                    pt2 = psum("half")
                    ptv2 = pt2.ap.bitcast(BF16)[:, 0:512].rearrange("p (a b) -> p a b", b=128)
                    for c in range(4):
                        tr(ptv2[:, c, :], yn.ap[:, c * 128:(c + 1) * 128], identb.ap, [yn, identb], [pt2], inc=(c == 3))
                    for c in range(4):
                        t_ = tf.next()
                        stt(t_.ap[:, 0:128], ptv2[:, c, :], lnwT[:, c:c + 1], rw_bon[c].ap[:, ssl], ALU.mult, ALU.add,
                            [pt2, PT, rw_bon[c]], [t_])
                        stt(ot.ap[:, 4 + c, :], t_.ap[:, 0:128], lnbT[:, c:c + 1], rw_gT[c].ap[:, ssl], ALU.add, ALU.mult,
                            [t_, PT, rw_gT[c]], [ot])
                    dd('yn', yn); dd('ot', ot)
                    ck('rwout')
                    g = bi * NS + s
                    po1 = psum()
                    po2 = psum()
                    for kc in range(8):
                        mm(po1.ap, ot.ap[:, kc, :], w_out.ap[:, kc, 0:512], kc == 0, kc == 7, [ot, w_out], [po1], inc=False)
                    for kc in range(8):
                        mm(po2.ap, ot.ap[:, kc, :], w_out.ap[:, kc, 512:1024], kc == 0, kc == 7, [ot, w_out], [po2],
                           inc=(kc == 7))
                    xo = x1t.next()
                    P.dma("sp", "d_st1", xo.ap, xs[g * 128:(g + 1) * 128, :], writes=[xo])
                    tt("dve", xo.ap[:, 0:512], po1.ap, xo.ap[:, 0:512], ALU.add, [po1, xo], [xo])
                    tt("dve", xo.ap[:, 512:1024], po2.ap, xo.ap[:, 512:1024], ALU.add, [po2, xo], [xo])
                    row = (g - (nl + 1) * NS + 1) * 128
                    P.dma("sp", xo.sem, x1d[row:row + 128, :], xo.ap, reads=[xo], writes=[x1d_bufs[row // 128]])

        x1d_bufs = [Buf(f"x1d{i}") for i in range(1 + nm * NS)]
        print("arena phase1 used f32 words", st["off"], "of", ARENA_F32)
        try:
            for bi in range(nblk):
                if bi < nl:
                    mixer_block(bi, False, NS)
                elif bi == nl:
                    mixer_block(bi, True, NS - 1)
                else:
                    mixer_block(bi, True, 0)
        except StopBuild:
            P.barrier()
            P.replay(block)
            return nc, P
        if "x1" in dbg_t:
            P.barrier()
            P.dma("sp", "d_dbg", dbg_t["x1"], x1d[:, :], reads=x1d_bufs)

        NPRE = 7
        st_save = st["off"]
        st["off"] = glob_end
        w_up = carve("w_up", [NFC // 2, 8, 512], BF16)
        assert NPRE * 8 * 512 // 2 <= 8 * 3840 // 2
        st["off"] = st_save
        wup_src = dr["w_up"].rearrange("(kc p) c -> p kc c", p=128)
        wupb = [Buf(f"wup{gi}") for gi in range(NFC // 2)]

        def load_wup(gi, extra_reads=()):
            c0 = gi * 256
            hb_ = [Buf(f"wup{gi}a"), Buf(f"wup{gi}b")]
            P.dma("pool", f"d_wup{gi}", w_up.ap[:, gi, :, 0:256], wup_src[:, :, c0:c0 + 256],
                  reads=list(extra_reads), writes=[hb_[0]])
            P.dma("pool", f"d_wup{gi}", w_up.ap[:, gi, :, 256:512], wup_src[:, :, DFF + c0:DFF + c0 + 256],
                  reads=list(extra_reads), writes=[hb_[1]])
            wupb[gi] = hb_

        if stop is None:
            fence = sm.next()
            memset("pool", fence.ap[:, 0:1], 0.0, [fence] + [b for (_, _, b) in wbufs])
            for gi in range(NPRE):
                load_wup(gi, [fence])

        P.barrier()
        if stop == "p1":
            P.replay(block)
            return nc, P
        st["off"] = glob_end
        w_up = carve("w_up", [NFC // 2, 8, 512], BF16)
        w_dn = carve("w_dn", [NFC, D], BF16)
        xring2 = ring("x2t", NS + 1, [D], F32, ["d_x0", "d_x1", "d_x2", "d_x3"][:NS + 1])
        xnr = ring("xn2_", 2, [D], BF16)
        sm = ring("sm2_", 8, [16], F32)
        h2Ts = [carve(f"h2T{i}", [8, TB + 2], BF16) for i in range(2)]
        gT = carve("gT", [NFC, TB], BF16)
        acg = ring("acg", 4, [TB], F32)
        acv = ring("acv", 4, [TB], F32)
        sgl = ring("sgl", 4, [TB], F32)
        xo2 = ring("xo2", 1, [D], F32)
        fnwb = carve("fnwb", [D], F32)
        P.dma("sp", "d_fn", fnwb.ap, dr["final_norm_w"].partition_broadcast(128), writes=[fnwb])
        yo = ring("yo", 2, [D], F32, ["d_st2", "d_st3"])
        xring = xring2
        for gi in range(NPRE, NFC // 2):
            load_wup(gi)
        wdn_src = dr["w_down"].rearrange("(c p) n -> p c n", p=128)
        wdnb = []
        for gi in range(2):
            b = Buf(f"wdn{gi}")
            wdnb.append(b)
            P.dma("pool", f"d_wdn{gi}", w_dn.ap[:, gi * 11:(gi + 1) * 11, :], wdn_src[:, gi * 11:(gi + 1) * 11, :], writes=[b])
        print("arena phase2 used f32 words", st["off"], "of", ARENA_F32)
        cw0 = CW.ap[:, 0, :]
        cw1 = CW.ap[:, 1, :]
        cw2 = CW.ap[:, 2, :]
        cbT = CW.ap[:, 3, :]

        xt0 = xring.next()
        P.dma("sp", xt0.sem, xt0.ap, x1d[0:128, :], reads=[x1d_bufs[0]], writes=[xt0])
        load_norm_T(None, nw2T, h2Ts[1], NS - 1, xt=xt0, off=2)
        ts("dve", h2Ts[1].ap[:, :, TB:TB + 2], h2Ts[1].ap[:, :, TB:TB + 2], flg.ap[:, 0:1], None, ALU.mult, None,
           [h2Ts[1], flg], [h2Ts[1]])

        for m in range(nm):
            x1ts = []
            h2T = h2Ts[m % 2]
            cp("dve", h2T.ap[:, :, 0:2], h2Ts[(m + 1) % 2].ap[:, :, TB:TB + 2], [h2Ts[(m + 1) % 2]], [h2T])
            for s in range(NS):
                row = 128 + (m * NS + s) * 128
                xt = xring.next()
                P.dma("sp", xt.sem, xt.ap, x1d[row:row + 128, :], reads=[x1d_bufs[row // 128]], writes=[xt])
                load_norm_T(None, nw2T, h2T, s, xt=xt, off=2)
                x1ts.append(xt)
            for c in range(NFC):
                res = []
                for (cc, ac_ring) in ((c, acg), (NFC + c, acv)):
                    pb = psum()
                    for kc in range(8):
                        lc = (0 if cc < NFC else 256) + (c % 2) * 128
                        mm(pb.ap[:, 0:TB + 2], w_up.ap[:, c // 2, kc, lc:lc + 128], h2T.ap[:, kc, :], kc == 0, kc == 7,
                           wupb[c // 2] + [h2T], [pb], inc=(kc == 7))
                    ac = ac_ring.next()
                    act(ac.ap, pb.ap[:, 2:TB + 2], AF.Identity, [pb, CW], [ac], scale=cw2[:, cc:cc + 1], bias=cbT[:, cc:cc + 1])
                    stt(ac.ap, pb.ap[:, 1:TB + 1], cw1[:, cc:cc + 1], ac.ap, ALU.mult, ALU.add, [pb, ac, CW], [ac])
                    stt(ac.ap, pb.ap[:, 0:TB], cw0[:, cc:cc + 1], ac.ap, ALU.mult, ALU.add, [pb, ac, CW], [ac])
                    res.append(ac)
                sg_ = sgl.next()
                act(sg_.ap, res[0].ap, AF.Silu, [res[0]], [sg_])
                tt("dve", gT.ap[:, c, :], sg_.ap, res[1].ap, ALU.mult, [sg_, res[1]], [gT])
            for s in range(NS):
                ssl = slice(s * 128, (s + 1) * 128)
                po1 = psum()
                po2 = psum()
                for c in range(NFC):
                    mm(po1.ap, gT.ap[:, c, ssl], w_dn.ap[:, c, 0:512], c == 0, c == NFC - 1, [gT, wdnb[c // 11]], [po1],
                       inc=False)
                for c in range(NFC):
                    mm(po2.ap, gT.ap[:, c, ssl], w_dn.ap[:, c, 512:1024], c == 0, c == NFC - 1, [gT, wdnb[c // 11]], [po2],
                       inc=(c == NFC - 1))
                xo = xo2.next()
                xt = x1ts[s]
                tt("dve", xo.ap[:, 0:512], po1.ap, xt.ap[:, 0:512], ALU.add, [po1, xt], [xo])
                tt("dve", xo.ap[:, 512:1024], po2.ap, xt.ap[:, 512:1024], ALU.add, [po2, xt], [xo])
                ss = sm.next()
                jk = xnr.next()
                act(jk.ap, xo.ap, AF.Square, [xo], [jk, ss], accum_out=ss.ap[:, 0:1])
                rs = sm.next()
                rsqrt_small(rs.ap[:, 0:1], [], ss.ap[:, 0:1], 1, 1.0 / D, 1e-6, [ss], [rs])
                y_ = yo.next()
                stt(y_.ap, xo.ap, rs.ap[:, 0:1], fnwb.ap, ALU.mult, ALU.mult, [xo, rs, fnwb], [y_])
                row = (m * NS + s) * 128
                P.dma("sp", y_.sem, out[row:row + 128, :], y_.ap, reads=[y_])
        P.barrier()
        P.replay(block)
    return nc, P


def _prep_inputs(inputs):
    sq = {}
    for name, shp in PARAM_SHAPES:
        a = np.asarray(inputs[name], dtype=np.float32)
        sq[name] = np.ascontiguousarray(a.reshape(shp))
    return sq


def kernel(**inputs):
    x = np.asarray(inputs["x"], dtype=np.float32)
    B, T, Dm = x.shape
    half = T // 2
    params = _prep_inputs(inputs)
    nl = half // TB - 1
    nm = half // TB
    nc, _ = build(nl, nm)
    in_maps = []
    for c in range(8):
        b, j = c // 2, c % 2
        xs = np.zeros((2 * half, Dm), np.float32)
        if j == 1:
            xs[:half] = x[b, :half]
        xs[half:] = x[b, j * half:(j + 1) * half]
        m = dict(params)
        m["xs"] = xs
        m["flag"] = np.full((128, 1), float(j), np.float32)
        in_maps.append(m)
    res = run_bass_kernel_spmd(nc, in_maps, core_ids=list(range(8)))
    outp = np.empty((B, T, Dm), np.float32)
    for c in range(8):
        b, j = c // 2, c % 2
        outp[b, j * half:(j + 1) * half] = res.results[c]["out"]
    return outp
```
